# Optimizing a Trainium2 kernel written in Bass

```python
import math
import jax, jax.numpy as jnp
from jax import lax
import numpy as np

D_MODEL = 1024
BATCH = 8
SEQ = 4096
DEPTH = 4

D_MIX = D_MODEL
ATT_HEADS = 4
ATT_QK_DIM = 64
ATT_V_DIM = 128
ATT_WIDTH = ATT_HEADS * ATT_V_DIM
ATT_QK_COLS = ATT_HEADS * 2 * ATT_QK_DIM
CONV_WIDTH = D_MIX // 4
CONV_K = 3
RWKV_HEAD = 64
RWKV_WIDTH = D_MIX // 4
RWKV_HEADS = RWKV_WIDTH // RWKV_HEAD
DECAY_LORA = 64
ICLR_LORA = 64
RWKV_SHIFT_COLS = 3 * RWKV_WIDTH + DECAY_LORA + ICLR_LORA
IN_COLS = 2 * ATT_QK_COLS + 2 * ATT_WIDTH + 4 * CONV_WIDTH + RWKV_SHIFT_COLS + RWKV_WIDTH
N_BUCKETS = 32
MAX_DISTANCE = 128
Q_BLOCK = 128
NEG_INF = -1e30
NORM_EPS = 1e-6
SUBLN_EPS = 1e-5
GN_EPS = 64e-5

kernel_name = 'hybrid_diffattn_shortconv_rwkv7'


def rmsnorm(x, g, eps=NORM_EPS):
    xf = x.astype(jnp.float32)
    y = xf * lax.rsqrt(jnp.mean(xf * xf, axis=-1, keepdims=True) + eps)
    return (y * g.astype(jnp.float32)).astype(x.dtype)


def split_cols(p, sizes):
    out, start = [], 0
    for n in sizes:
        out.append(p[..., start:start + n])
        start += n
    return out


def t5_causal_bucket(dist):
    n = jnp.maximum(dist, 0)
    max_exact = N_BUCKETS // 2
    nf = jnp.maximum(n, 1).astype(jnp.float32)
    large = max_exact + (jnp.log(nf / max_exact) / math.log(MAX_DISTANCE / max_exact)
                         * (N_BUCKETS - max_exact)).astype(jnp.int32)
    large = jnp.minimum(large, N_BUCKETS - 1)
    return jnp.where(n < max_exact, n, large)


def diff_attention(q, k, v, lam, lambda_init, subln_g, rel_bias):
    b, s, h = q.shape[0], q.shape[1], q.shape[2]
    nb = s // Q_BLOCK
    scale = ATT_QK_DIM ** -0.5
    qf = (q.astype(jnp.float32) * scale).reshape(b, nb, Q_BLOCK, h, 2, ATT_QK_DIM).swapaxes(0, 1)
    kf = k.astype(jnp.float32)
    vf = v.astype(jnp.float32)
    table = rel_bias.astype(jnp.float32)
    key_pos = jnp.arange(s)

    def block(args):
        qb, i = args
        q_pos = i * Q_BLOCK + jnp.arange(Q_BLOCK)
        dist = q_pos[:, None] - key_pos[None, :]
        bias = jnp.transpose(table[t5_causal_bucket(dist)], (2, 0, 1))
        logits = jnp.einsum('bqhmd,bkhmd->bhmqk', qb, kf) + bias[None, :, None]
        logits = jnp.where(dist >= 0, logits, NEG_INF)
        p = jax.nn.softmax(logits, axis=-1)
        a = p[:, :, 0] - lam * p[:, :, 1]
        return jnp.einsum('bhqk,bkhd->bqhd', a, vf)

    o = lax.map(block, (qf, jnp.arange(nb)))
    o = o.swapaxes(0, 1).reshape(b, s, h, ATT_V_DIM)
    o = rmsnorm(o, subln_g, eps=SUBLN_EPS) * (1.0 - lambda_init)
    return o.reshape(b, s, h * ATT_V_DIM)


def short_conv(bg, cg, hin, conv_w):
    u = cg * hin
    kern = conv_w[:, None, :].astype(u.dtype)
    y = lax.conv_general_dilated(u, kern, window_strides=(1,), padding=[(CONV_K - 1, 0)],
                                 dimension_numbers=('NWC', 'WIO', 'NWC'),
                                 feature_group_count=u.shape[-1])
    return bg * y


def rwkv7_time_mix(p, mu, w0, w_up, a0, a_up, k_k, k_a, r_k, lnx_g, lnx_b):
    b, s, _ = p.shape
    f32 = jnp.float32
    p = p.astype(f32)
    prev = jnp.pad(p, ((0, 0), (1, 0), (0, 0)))[:, :-1]
    p = p + (prev - p) * mu.astype(f32)
    r, k, v, wd, ad = split_cols(p, (RWKV_WIDTH, RWKV_WIDTH, RWKV_WIDTH, DECAY_LORA, ICLR_LORA))
    w = -jax.nn.softplus(-(w0.astype(f32) + jnp.tanh(wd) @ w_up.astype(f32))) - 0.5
    decay = jnp.exp(-jnp.exp(w))
    a = jax.nn.sigmoid(a0.astype(f32) + ad @ a_up.astype(f32))
    heads = lambda t: t.reshape(b, s, RWKV_HEADS, RWKV_HEAD)
    kk = heads(k * k_k.astype(f32))
    kk = kk / jnp.maximum(jnp.sqrt(jnp.sum(kk * kk, axis=-1, keepdims=True)), 1e-12)
    k = k * (1.0 + (a - 1.0) * k_a.astype(f32))
    rh, kh, vh, wh, ah = heads(r), heads(k), heads(v), heads(decay), heads(a)

    def step(state, inp):
        r_t, k_t, v_t, w_t, kk_t, a_t = inp
        sa = jnp.einsum('bhvk,bhk->bhv', state, kk_t)
        state = (state * w_t[:, :, None, :]
                 - sa[..., None] * (kk_t * a_t)[:, :, None, :]
                 + v_t[..., None] * k_t[:, :, None, :])
        y = jnp.einsum('bhvk,bhk->bhv', state, r_t)
        return state, y

    xs = tuple(t.swapaxes(0, 1) for t in (rh, kh, vh, wh, kk, ah))
    s0 = jnp.zeros((b, RWKV_HEADS, RWKV_HEAD, RWKV_HEAD), f32)
    _, y = lax.scan(step, s0, xs)
    y = y.swapaxes(0, 1)
    mean = jnp.mean(y, axis=-1, keepdims=True)
    var = jnp.mean(jnp.square(y - mean), axis=-1, keepdims=True)
    y = (y - mean) * lax.rsqrt(var + GN_EPS)
    y = y.reshape(b, s, RWKV_WIDTH) * lnx_g.astype(f32) + lnx_b.astype(f32)
    bonus = jnp.sum(rh * kh * r_k.astype(f32), axis=-1, keepdims=True) * vh
    return y + bonus.reshape(b, s, RWKV_WIDTH)


def setup_inputs(seed: int = 0) -> dict:
    key = jax.random.key(seed)
    ks = jax.random.split(key, 20)
    nrm = jax.random.normal
    return {
        'x': nrm(ks[0], (BATCH, SEQ, D_MODEL), jnp.float32),
        'norm_g': 1.0 + 0.02 * nrm(ks[1], (DEPTH, D_MODEL), jnp.float32),
        'w_in': nrm(ks[2], (DEPTH, D_MODEL, IN_COLS), jnp.float32) * D_MODEL ** -0.5,
        'w_out': nrm(ks[3], (DEPTH, D_MIX, D_MODEL), jnp.float32) * D_MIX ** -0.5,
        'final_norm_g': 1.0 + 0.02 * nrm(ks[4], (D_MODEL,), jnp.float32),
        'rel_bias': 0.2 * nrm(ks[5], (N_BUCKETS, ATT_HEADS), jnp.float32),
        'lam_qk': 0.1 * nrm(ks[6], (DEPTH, 4, ATT_QK_DIM), jnp.float32),
        'subln_g': 1.0 + 0.02 * nrm(ks[7], (DEPTH, ATT_V_DIM), jnp.float32),
        'conv_w': nrm(ks[8], (DEPTH, CONV_K, CONV_WIDTH), jnp.float32) * CONV_K ** -0.5,
        'rwkv_mu': jax.random.uniform(ks[9], (DEPTH, RWKV_SHIFT_COLS), jnp.float32),
        'w0': jax.random.uniform(ks[10], (DEPTH, RWKV_WIDTH), jnp.float32, minval=-5.0, maxval=1.0),
        'w_up': 0.5 * nrm(ks[11], (DEPTH, DECAY_LORA, RWKV_WIDTH), jnp.float32) * DECAY_LORA ** -0.5,
        'a0': 0.1 * nrm(ks[12], (DEPTH, RWKV_WIDTH), jnp.float32),
        'a_up': 0.5 * nrm(ks[13], (DEPTH, ICLR_LORA, RWKV_WIDTH), jnp.float32) * ICLR_LORA ** -0.5,
        'k_k': 0.85 + 0.02 * nrm(ks[14], (DEPTH, RWKV_WIDTH), jnp.float32),
        'k_a': 1.0 + 0.02 * nrm(ks[15], (DEPTH, RWKV_WIDTH), jnp.float32),
        'r_k': 0.1 * nrm(ks[16], (DEPTH, RWKV_HEADS, RWKV_HEAD), jnp.float32),
        'lnx_g': 1.0 + 0.02 * nrm(ks[17], (DEPTH, RWKV_WIDTH), jnp.float32),
        'lnx_b': 0.01 * nrm(ks[18], (DEPTH, RWKV_WIDTH), jnp.float32),
    }


def reference(x, norm_g, w_in, w_out, final_norm_g, rel_bias, lam_qk, subln_g, conv_w,
              rwkv_mu, w0, w_up, a0, a_up, k_k, k_a, r_k, lnx_g, lnx_b):
    b, s, _ = x.shape
    sizes = (ATT_QK_COLS, ATT_QK_COLS, ATT_WIDTH, ATT_WIDTH,
             CONV_WIDTH, CONV_WIDTH, CONV_WIDTH, CONV_WIDTH,
             RWKV_SHIFT_COLS, RWKV_WIDTH)
    for l in range(DEPTH):
        h = rmsnorm(x, norm_g[l])
        p = h @ w_in[l]
        q, k, v, z_att, cb, cc, ch, z_conv, rw_p, z_rwkv = split_cols(p, sizes)
        lambda_init = 0.8 - 0.6 * math.exp(-0.3 * l)
        lq = lam_qk[l].astype(jnp.float32)
        lam = jnp.exp(jnp.sum(lq[0] * lq[1])) - jnp.exp(jnp.sum(lq[2] * lq[3])) + lambda_init
        att = diff_attention(q.reshape(b, s, ATT_HEADS, 2, ATT_QK_DIM),
                             k.reshape(b, s, ATT_HEADS, 2, ATT_QK_DIM),
                             v.reshape(b, s, ATT_HEADS, ATT_V_DIM),
                             lam, lambda_init, subln_g[l], rel_bias).astype(x.dtype)
        cv = short_conv(cb, cc, ch, conv_w[l])
        rw = rwkv7_time_mix(rw_p, rwkv_mu[l], w0[l], w_up[l], a0[l], a_up[l], k_k[l], k_a[l],
                            r_k[l], lnx_g[l], lnx_b[l]).astype(x.dtype)
        mixed = jnp.concatenate([att * jax.nn.silu(z_att),
                                 cv * jax.nn.silu(z_conv),
                                 rw * jax.nn.silu(z_rwkv)], axis=-1)
        x = x + mixed @ w_out[l]
    return rmsnorm(x, final_norm_g)
```

```python
import math
from contextlib import ExitStack

import numpy as np
import ml_dtypes

import concourse.bass as bass
import concourse.mybir as mybir
from concourse.bass_utils import run_bass_kernel_spmd

F32 = mybir.dt.float32
BF16 = mybir.dt.bfloat16
AF = mybir.ActivationFunctionType
ALU = mybir.AluOpType
AX = mybir.AxisListType

S = 4096
D = 1024
NT = 32
NG = 8
L = 4
INC = 4224
NEG8 = -240000.0
NORM_EPS = 1e-6
SUBLN_EPS = 1e-5
GN_EPS = 64e-5
SCALE = 0.125
DBG = set()
POOL_AS = "dve"


class Res:
    __slots__ = ("name", "writer", "readers")

    def __init__(self, name):
        self.name = name
        self.writer = None
        self.readers = []


class Op:
    __slots__ = ("eng", "fn", "deps", "dma", "idx", "sig", "waits", "has_dep")


class Prog:
    def __init__(self):
        self.ops = []

    def add(self, eng, fn, reads=(), writes=(), dma=None):
        if dma is None and eng == "pool" and POOL_AS:
            eng = POOL_AS
        op = Op()
        op.eng = eng
        op.fn = fn
        op.dma = dma
        op.idx = len(self.ops)
        op.deps = {}
        op.has_dep = False
        op.sig = None

        def dep(d, kind):
            if d is None or d is op:
                return
            if op.deps.get(d) != "raw":
                op.deps[d] = kind

        for r in reads:
            dep(r.writer, "raw")
        for w in writes:
            dep(w.writer, "waw")
            for rd in w.readers:
                dep(rd, "war")
        k = (op.eng, op.dma)
        for r in reads:
            r.readers = [x for x in r.readers if (x.eng, x.dma) != k]
            r.readers.append(op)
        for w in writes:
            w.writer = op
            w.readers = []
        self.ops.append(op)
        return op

    def finalize(self):
        for op in self.ops:
            keep = {}
            for d, kind in op.deps.items():
                if d.dma is None and op.dma is None and d.eng == op.eng:
                    if op.eng == "pe":
                        continue
                keep[d] = kind
            op.deps = keep
            for d in keep:
                d.has_dep = True
        cnt = {}
        waited = {}
        for op in self.ops:
            w = {}
            for d in op.deps:
                key = d.dma if d.dma else d.eng
                val = cnt[key] if d.dma else d.sig
                if w.get(key, 0) < val:
                    w[key] = val
            q = waited.setdefault(op.eng, {})
            op.waits = []
            for kk, v in w.items():
                if q.get(kk, 0) < v:
                    q[kk] = v
                    op.waits.append((kk, v))
            if op.dma:
                cnt[op.dma] = cnt.get(op.dma, 0) + 16
                op.sig = cnt[op.dma]
            elif op.has_dep:
                cnt[op.eng] = cnt.get(op.eng, 0) + 1
                op.sig = cnt[op.eng]
        self.cnt = cnt

    def emit(self, nc, st):
        self.finalize()
        keys = set()
        for op in self.ops:
            for kk, _ in op.waits:
                keys.add(kk)
            if op.dma:
                keys.add(op.dma)
            elif op.sig is not None:
                keys.add(op.eng)
        sems = {kk: st.enter_context(nc.semaphore("s_" + kk)) for kk in sorted(keys)}
        block = st.enter_context(nc.Block())
        ops = self.ops

        def run(name):
            def body(e):
                for op in ops:
                    if op.eng != name:
                        continue
                    for kk, v in op.waits:
                        e.wait_ge(sems[kk], v)
                    ins = op.fn(e)
                    if op.sig is not None:
                        ins.then_inc(sems[op.dma or op.eng], 16 if op.dma else 1)

            return body

        block.tensor(run("pe"))
        block.scalar(run("act"))
        block.vector(run("dve"))
        block.gpsimd(run("pool"))
        block.sync(run("sp"))
        return len(sems)


def _bucket(dist):
    n = np.maximum(dist, 0)
    max_exact = 16
    nf = np.maximum(n, 1).astype(np.float32)
    large = max_exact + (np.log(nf / max_exact) / math.log(128 / max_exact) * (32 - max_exact)).astype(np.int32)
    large = np.minimum(large, 31)
    return np.where(n < max_exact, n, large)


def _bucket_jax_exact():
    import jax
    import jax.numpy as jnp

    with jax.default_device(jax.devices("cpu")[0]):
        dist = jnp.arange(0, 256)
        n = jnp.maximum(dist, 0)
        nf = jnp.maximum(n, 1).astype(jnp.float32)
        large = 16 + (jnp.log(nf / 16) / math.log(128 / 16) * 16).astype(jnp.int32)
        large = jnp.minimum(large, 31)
        return np.asarray(jnp.where(n < 16, n, large))


def make_consts():
    c = {}
    c["c_identf"] = np.eye(128, dtype=np.float32)
    try:
        bk = _bucket_jax_exact()
    except Exception:
        bk = _bucket(np.arange(256))
    oh = np.zeros((33, 384), np.float32)
    for m in range(384):
        dist = m - 128
        if dist < 0:
            oh[32, m] = 8.0
        else:
            oh[bk[dist], m] = 8.0
    c["c_onehot8"] = oh
    j = np.arange(64)[:, None]
    t = np.arange(64)[None, :]
    strict = (j < t).astype(np.float32)
    incl = (j <= t).astype(np.float32)
    c["c_maskT2"] = np.concatenate([strict, incl], axis=1)
    c["c_maskL"] = (t < j).astype(np.float32)
    sm = np.ones((64, 1024), np.float32)
    sm[:, ::64] = 0.0
    c["c_scanmask"] = sm
    return c


def host_params(inp):
    g = np.asarray(inp["norm_g"], np.float32)
    gT = g.reshape(L, 8, 128).transpose(2, 0, 1).reshape(128, L * 8)
    sublnT = np.asarray(inp["subln_g"], np.float32).T
    convT = np.asarray(inp["conv_w"], np.float32).reshape(L, 3, 2, 128).transpose(3, 0, 1, 2).reshape(128, L * 6)
    prm128 = np.ascontiguousarray(np.concatenate([gT, sublnT, convT], axis=1))
    fg = np.ascontiguousarray(np.broadcast_to(np.asarray(inp["final_norm_g"], np.float32)[None, :], (128, D)))
    lamrep = np.ascontiguousarray(np.broadcast_to(np.asarray(inp["lam_qk"], np.float32).reshape(1, L * 256), (128, L * 256)))
    mu = np.asarray(inp["rwkv_mu"], np.float32)
    mu_rkv = mu[:, :768].reshape(L, 3, 4, 64).transpose(3, 0, 1, 2).reshape(64, L * 12)
    mu_wa = mu[:, 768:896].reshape(L, 2, 64).transpose(2, 0, 1).reshape(64, L * 2)

    def ch(a):
        return np.asarray(a, np.float32).reshape(L, 4, 64).transpose(2, 0, 1).reshape(64, L * 4)

    prm64 = np.ascontiguousarray(np.concatenate(
        [mu_rkv, mu_wa, ch(inp["w0"]), ch(inp["a0"]), ch(inp["k_k"]), ch(inp["k_a"]),
         ch(np.asarray(inp["r_k"]).reshape(L, 256)), ch(inp["lnx_g"]), ch(inp["lnx_b"])], axis=1))
    return prm128, fg, lamrep, prm64


def build(depth=L, taps=(), do_rwkv=True, tap_layer=0, stop_after=None):
    nc = bass.Bass("TRN2", target_bir_lowering=False)
    P = Prog()
    dram_in = lambda n, s, d=F32: nc.dram_tensor(n, list(s), d, kind="ExternalInput")
    x_t = dram_in("x", [S, D])
    win_t = dram_in("w_in", [L, D, INC])
    wout_t = dram_in("w_out", [L, D, D])
    relb_t = dram_in("rel_bias", [32, 4])
    wup_t = dram_in("w_up", [L, 64, 256])
    aup_t = dram_in("a_up", [L, 64, 256])
    prm128_t = dram_in("prm128", [128, 60])
    fg_t = dram_in("fg", [128, D])
    lamrep_t = dram_in("lamrep", [128, L * 256])
    prm64_t = dram_in("prm64", [64, 168])
    cidf_t = dram_in("c_identf", [128, 128])
    coh_t = dram_in("c_onehot8", [33, 384])
    cm2_t = dram_in("c_maskT2", [64, 128])
    cml_t = dram_in("c_maskL", [64, 64])
    csm_t = dram_in("c_scanmask", [64, 1024])
    out_t = nc.dram_tensor("out", [S, D], F32, kind="ExternalOutput")
    xs_t = nc.dram_tensor("xs", [S, D], F32, kind="Internal")
    pt_t = nc.dram_tensor("ptf", [INC, S], F32, kind="Internal")
    vtok_t = nc.dram_tensor("vtok", [S, 512], BF16, kind="Internal")
    gsc_t = nc.dram_tensor("gsc", [4, 130 * 384], F32, kind="Internal")
    tap_t = {}
    if "pt" in taps:
        tap_t["pt"] = nc.dram_tensor("tap_pt", [INC, S], F32, kind="ExternalOutput")
    if "mixed" in taps:
        tap_t["mixed"] = nc.dram_tensor("tap_mixed", [D, S], BF16, kind="ExternalOutput")
    if "xs" in taps:
        tap_t["xs"] = nc.dram_tensor("tap_xs", [S, D], F32, kind="ExternalOutput")

    x_d, win_d, wout_d = x_t.ap(), win_t.ap(), wout_t.ap()
    out_d, xs_d, pt_d, vtok_d = out_t.ap(), xs_t.ap(), pt_t.ap(), vtok_t.ap()

    xs_res = [Res(f"xs{i}") for i in range(NT)]
    pt_res = [Res(f"pt{j}") for j in range(33)]
    vtok_res = [Res(f"vt{i}") for i in range(NT)]
    out_res = Res("out")
    gsc_res = Res("gsc")

    with ExitStack() as st:
        sb = lambda n, s, d=F32: st.enter_context(nc.sbuf_tensor(n, list(s), d))
        big = sb("big", [128, 8, S], BF16)
        big_res = [[Res(f"big{k}_{g}") for g in range(NG)] for k in range(8)]
        ar = sb("arena", [128, 32768], F32)
        ARES = [Res(f"ar{i}") for i in range(128)]
        ps_all = st.enter_context(nc.psum_tensor("psall", [128, 8 * 512], F32))
        PSR = [Res(f"ps{i}") for i in range(8)]

        def psb(bank, nb=1, parts=128):
            return ps_all[0:parts, bank * 512:(bank + nb) * 512], PSR[bank:bank + nb]

        class Arena:
            def __init__(self):
                self.p = 0

            def reset(self, p=0):
                self.p = p

            def alloc(self, n, dtype=F32, parts=128):
                nf = n if dtype is F32 else (n + 1) // 2
                nf = (nf + 1) // 2 * 2
                off = self.p
                self.p += nf
                assert self.p <= 32768, self.p
                ap = ar[0:parts, off:off + nf]
                if dtype is BF16:
                    ap = ap.bitcast(BF16)[:, 0:n]
                else:
                    ap = ap[:, 0:n]
                return ap, ARES[off // 256:(off + nf - 1) // 256 + 1]

        A = Arena()

        identf = sb("identf", [128, 128])
        identb = sb("identb", [128, 128], BF16)
        onesb = sb("onesb", [128, 128], BF16)
        ones64 = sb("ones64", [64, 64])
        ones64r = sb("ones64r", [64, 64])
        prm128 = sb("prm128s", [128, 60])
        prm64 = sb("prm64s", [64, 168])
        nprm64 = sb("nprm64", [64, 32])
        lamv = sb("lamv", [128, L * 8])
        wup = sb("wups", [64, 256])
        aup = sb("aups", [64, 256])
        wua_r = Res("wua")
        maskT2p = sb("maskT2p", [64, 128])
        maskT2n = sb("maskT2n", [64, 128])
        maskLn = sb("maskLn", [64, 64])
        bblk = sb("bblk", [128, 4, 3, 128], BF16)
        cfar = sb("cfar", [128, 8])
        ss = sb("ss", [128, 3 * NT])
        cst = Res("consts")
        ss_res = [Res(f"ss{i}") for i in range(NT)]
        sst_res = [[Res(f"sst{a}_{n}") for n in range(4)] for a in range(2)]

        def dma(eng, out, in_, reads, writes, key):
            P.add(eng, lambda e: e.dma_start(out=out, in_=in_), reads, writes, dma=key)

        dma("sp", identf[:], cidf_t.ap(), [], [cst], "c0")
        dma("sp", prm128[:], prm128_t.ap(), [], [cst], "c0")
        dma("sp", prm64[:], prm64_t.ap(), [], [cst], "c0")
        lamrep, lamrep_r = A.alloc(L * 256)
        dma("sp", lamrep, lamrep_t.ap(), [], lamrep_r, "c0")
        dma("sp", maskT2p[:], cm2_t.ap(), [], [cst], "c0")
        dma("sp", maskLn[:], cml_t.ap(), [], [cst], "c0")
        onehot, onehot_r = A.alloc(384, parts=64)
        relb, relb_rr = A.alloc(128, parts=64)
        gsb, gsb_rr = A.alloc(384, parts=4)
        relb_r = Res("relb")
        P.add("pool", lambda e: e.memset(onehot[:], 0.0), [], onehot_r)
        dma("sp", onehot[0:33, :], coh_t.ap(), [], onehot_r, "c0")
        P.add("pool", lambda e: e.memset(relb[:], 0.0), [], [relb_r] + relb_rr)
        dma("sp", relb[0:32, 0:4], relb_t.ap(), [], [relb_r], "c1")
        cst2 = Res("consts2")
        P.add("dve", lambda e: e.tensor_copy(out=identb[:], in_=identf[:]), [cst], [cst2])
        P.add("pool", lambda e: e.memset(onesb[:], 1.0), [], [cst2])
        P.add("pool", lambda e: e.memset(ones64[:], 1.0 / 64), [], [cst2])
        P.add("pool", lambda e: e.memset(ones64r[:], 1.0), [], [cst2])
        P.add("dve", lambda e: e.tensor_scalar(out=maskT2n[:], in0=maskT2p[:], scalar1=-1.0, scalar2=None, op0=ALU.mult), [cst], [cst2])
        P.add("dve", lambda e: e.tensor_scalar(out=maskLn[:], in0=maskLn[:], scalar1=-1.0, scalar2=None, op0=ALU.mult), [cst], [cst2])
        P.add("dve", lambda e: e.tensor_scalar(out=nprm64[:], in0=prm64[:, 56:88], scalar1=-1.0, scalar2=None, op0=ALU.mult), [cst], [cst2])
        P.add("pool", lambda e: e.memset(relb[32:33, 0:4], NEG8 / 8.0), [relb_r], [relb_r])

        lam_r = Res("lam")
        for l in range(depth):
            lq = lamrep[:, l * 256:(l + 1) * 256]
            tmpa, tmpr = A.alloc(128)
            tmpr = tmpr + lamrep_r
            for pr in range(2):
                P.add("dve", (lambda pr=pr, lq=lq, tmpa=tmpa: lambda e: e.tensor_tensor(out=tmpa[:, pr * 64:(pr + 1) * 64], in0=lq[:, (2 * pr) * 64:(2 * pr + 1) * 64], in1=lq[:, (2 * pr + 1) * 64:(2 * pr + 2) * 64], op=ALU.mult))(), [cst], tmpr)
                P.add("dve", (lambda pr=pr, l=l, tmpa=tmpa: lambda e: e.reduce_sum(out=lamv[:, l * 8 + pr:l * 8 + pr + 1], in_=tmpa[:, pr * 64:(pr + 1) * 64], axis=AX.X))(), tmpr, [lam_r])
            P.add("act", (lambda l=l: lambda e: e.activation(out=lamv[:, l * 8 + 2:l * 8 + 4], in_=lamv[:, l * 8:l * 8 + 2], func=AF.Exp))(), [lam_r], [lam_r])
            linit = 0.8 - 0.6 * math.exp(-0.3 * l)
            P.add("dve", (lambda l=l: lambda e: e.tensor_tensor(out=lamv[:, l * 8 + 4:l * 8 + 5], in0=lamv[:, l * 8 + 2:l * 8 + 3], in1=lamv[:, l * 8 + 3:l * 8 + 4], op=ALU.subtract))(), [lam_r], [lam_r])
            P.add("dve", (lambda l=l, linit=linit: lambda e: e.tensor_scalar(out=lamv[:, l * 8 + 5:l * 8 + 6], in0=lamv[:, l * 8 + 4:l * 8 + 5], scalar1=linit, scalar2=-1.0, op0=ALU.add, op1=ALU.mult))(), [lam_r], [lam_r])

        (pg, pgr) = psb(0)
        P.add("pe", lambda e: e.matmul(pg[:, 0:384], relb[:, :], onehot[:, :], start=True, stop=True), [relb_r] + onehot_r, pgr)
        gsb_r = Res("gsb")
        P.add("dve", lambda e: e.tensor_copy(out=gsb[:], in_=pg[0:4, 0:384]), pgr, [gsb_r] + gsb_rr)
        dma("sp", gsc_t.ap().rearrange("h (r n) -> h r n", n=384), gsb.unsqueeze(1).broadcast_to([4, 130, 384]), [gsb_r], [gsc_res], "c2")
        tdo, tdo_r = A.alloc(4 * 2 * 128)
        tdo4 = tdo.rearrange("p (h a q) -> p h a q", h=4, a=2)
        for h in range(4):
            for a_, base in ((0, 128), (1, 256)):
                dma("sp", tdo4[:, h, a_, :], bass.AP(gsc_t, h * 130 * 384 + base, [[383, 128], [1, 128]]), [gsc_res], tdo_r, "c3")
            dma("sp", cfar[:, h:h + 1], bass.AP(gsc_t, h * 130 * 384 + 383, [[0, 128], [1, 1]]), [gsc_res], [cst2], "c3")
        P.add("dve", lambda e: e.tensor_scalar(out=cfar[:, 4:8], in0=cfar[:, 0:4], scalar1=SCALE, scalar2=None, op0=ALU.mult), [cst2], [cst2])
        zt, zt_r = A.alloc(128)
        P.add("pool", lambda e: e.memset(zt[:], 0.0), [], zt_r)
        bias_r = Res("biasT")
        for h in range(4):
            P.add("dve", (lambda h=h: lambda e: e.tensor_copy(out=bblk[:, h, 0, :], in_=tdo4[:, h, 0, :]))(), tdo_r, [bias_r])
            P.add("pool", (lambda h=h: lambda e: e.tensor_copy(out=bblk[:, h, 1, :], in_=tdo4[:, h, 1, :]))(), tdo_r, [bias_r])
            P.add("dve", (lambda h=h: lambda e: e.tensor_scalar(out=bblk[:, h, 2, :], in0=zt[:], scalar1=cfar[:, h:h + 1], scalar2=None, op0=ALU.add))(), zt_r + [cst2], [bias_r])

        def TT(eng, out, a, b, op, rd, wr):
            P.add(eng, lambda e: e.tensor_tensor(out=out, in0=a, in1=b, op=op), rd, wr)

        def TS(eng, out, a, s1, s2, op0, op1, rd, wr):
            if s2 is None:
                P.add(eng, lambda e: e.tensor_scalar(out=out, in0=a, scalar1=s1, scalar2=None, op0=op0), rd, wr)
            else:
                P.add(eng, lambda e: e.tensor_scalar(out=out, in0=a, scalar1=s1, scalar2=s2, op0=op0, op1=op1), rd, wr)

        def STT(eng, out, a, sc, b, op0, op1, rd, wr):
            P.add(eng, lambda e: e.scalar_tensor_tensor(out=out, in0=a, scalar=sc, in1=b, op0=op0, op1=op1), rd, wr)

        def ACTF(out, in_, func, rd, wr, scale=1.0, bias=None):
            if bias is None:
                P.add("act", lambda e: e.activation(out=out, in_=in_, func=func, scale=scale), rd, wr)
            else:
                P.add("act", lambda e: e.activation(out=out, in_=in_, func=func, scale=scale, bias=bias), rd, wr)

        def MM(out, lhsT, rhs, st_, sp_, rd, wr):
            P.add("pe", lambda e: e.matmul(out, lhsT, rhs, start=st_, stop=sp_), rd, wr)

        def CP(eng, out, in_, rd, wr):
            if eng == "act":
                P.add("act", lambda e: e.activation(out=out, in_=in_, func=AF.Copy), rd, wr)
            else:
                P.add(eng, lambda e: e.tensor_copy(out=out, in_=in_), rd, wr)

        def RCP(out, in_, rd, wr):
            P.add("dve", lambda e: e.reciprocal(out=out, in_=in_), rd, wr)

        V = lambda buf, w: buf[0].rearrange("c (p t) -> c p t", p=16)
        V16 = lambda ap: ap.rearrange("c (p t) -> c p t", p=16)
        V4 = lambda ap: ap.rearrange("c (h t) -> c h t", h=4)
        pbk_state = [0]

        def pbk(n):
            if pbk_state[0] + n > 8:
                pbk_state[0] = 0
            b = pbk_state[0]
            pbk_state[0] += n
            return psb(b, n, parts=64)

        def rwkv_phase(l):
            A.reset()
            al = lambda n, dt=F32: A.alloc(n, dt, parts=64)
            scanmask = al(1024)
            Sst, Sst_r0 = al(2048)
            Sst5 = Sst.rearrange("c (a n h v) -> c a n h v", a=2, n=4, h=4)
            gC = al(16)
            KR = al(2048)
            kc, vc = al(1024), al(1024)
            pv = [al(1024), al(1024)]
            X1, X2, X4, C1, T, kt, bt, RK = [al(1024) for _ in range(8)]
            wdc, adc, wdp, adp, th = [al(256) for _ in range(5)]
            UWrhs, NAbT, AkT, UW = [al(2048) for _ in range(4)]
            Vtok, Khtok, Bntok, NAkb = [al(1024) for _ in range(4)]
            KR0 = (KR[0][:, 0:1024], KR[1][0:4])
            KR1 = (KR[0][:, 1024:2048], KR[1][4:8])
            KR4 = KR[0].rearrange("c (q p t) -> c q p t", q=2, p=16)
            id64 = identf[0:64, 0:64]
            id_bc = id64.unsqueeze(1).broadcast_to([64, 16, 64])

            def bc(col):
                return prm64[:, col:col + 4].unsqueeze(2).broadcast_to([64, 4, 256])

            dma("sp", scanmask[0], csm_t.ap(), [], scanmask[1], "rw_c")
            dma("sp", wup[:], wup_t.ap()[l], [], [wua_r], "rw_c")
            dma("sp", aup[:], aup_t.ap()[l], [], [wua_r], "rw_c")
            P.add("pool", lambda e: e.memset(Sst5[:, 0, 0, :, :], 0.0), [], [sst_res[0][0]] + Sst_r0)

            def src(row0, c0, c1):
                return pt_d[row0:row0 + 256, c0:c1].rearrange("(h c) t -> c h t", c=64)

            for gi in range(16):
                par = gi % 2
                t0 = gi * 256
                curs = [KR1, kc, vc]
                for q in range(3):
                    row0 = 3072 + 256 * q
                    prs = [pt_res[row0 // 128], pt_res[row0 // 128 + 1]]
                    cur = curs[q]
                    pvb = pv[q % 2]
                    dma("sp", V4(cur[0]), src(row0, t0, t0 + 256), prs, cur[1], f"rw_c{q}")
                    if gi == 0:
                        P.add("pool", (lambda pvb=pvb: lambda e: e.memset(V4(pvb[0])[:, :, 0:1], 0.0))(), [], pvb[1])
                        dma("sp", V4(pvb[0])[:, :, 1:256], src(row0, 0, 255), prs, pvb[1], f"rw_p{q % 2}")
                    else:
                        dma("sp", V4(pvb[0]), src(row0, t0 - 1, t0 + 255), prs, pvb[1], f"rw_p{q % 2}")
                    eng = "pool" if q == 1 else "dve"
                    TT(eng, pvb[0], pvb[0], cur[0], ALU.subtract, pvb[1] + cur[1], pvb[1])
                    TT(eng, V4(pvb[0]), V4(pvb[0]), bc(l * 12 + q * 4), ALU.mult, pvb[1] + [cst], pvb[1])
                    TT(eng, cur[0], cur[0], pvb[0], ALU.add, pvb[1] + cur[1], cur[1])
                for (cur, prv, row0, mcol, wk) in ((wdc, wdp, 3840, 48 + l * 2, 0), (adc, adp, 3904, 48 + l * 2 + 1, 1)):
                    dma("sp", cur[0], pt_d[row0:row0 + 64, t0:t0 + 256], [pt_res[30]], cur[1], f"rw_wc{wk}")
                    if gi == 0:
                        P.add("pool", (lambda prv=prv: lambda e: e.memset(prv[0][:, 0:1], 0.0))(), [], prv[1])
                        dma("sp", prv[0][:, 1:256], pt_d[row0:row0 + 64, 0:255], [pt_res[30]], prv[1], f"rw_wp{wk}")
                    else:
                        dma("sp", prv[0], pt_d[row0:row0 + 64, t0 - 1:t0 + 255], [pt_res[30]], prv[1], f"rw_wp{wk}")
                    TT("pool", prv[0], prv[0], cur[0], ALU.subtract, prv[1] + cur[1], prv[1])
                    STT("pool", cur[0], prv[0], prm64[:, mcol:mcol + 1], cur[0], ALU.mult, ALU.add, prv[1] + cur[1] + [cst], cur[1])
                ACTF(th[0], wdc[0], AF.Exp, wdc[1], th[1], scale=2.0)
                TS("pool", th[0], th[0], 1.0, None, ALU.add, None, th[1], th[1])
                RCP(th[0], th[0], th[1], th[1])
                TS("dve", th[0], th[0], -2.0, 1.0, ALU.mult, ALU.add, th[1], th[1])
                ups, upr = pbk(2)
                for h in range(4):
                    MM(ups[:, h * 256:(h + 1) * 256], wup[:, 64 * h:64 * h + 64], th[0], True, True, th[1] + [wua_r], upr)
                for h in range(4):
                    ACTF(V4(X1[0])[:, h, :], ups[:, h * 256:(h + 1) * 256], AF.Exp, upr + [cst2], X1[1], scale=-1.0, bias=nprm64[:, l * 4 + h:l * 4 + h + 1])
                TS("pool", X1[0], X1[0], 1.0, None, ALU.add, None, X1[1], X1[1])
                RCP(X1[0], X1[0], X1[1], X1[1])
                TS("pool", X1[0], X1[0], -0.6065306597126334, None, ALU.mult, None, X1[1], X1[1])
                aps, apr = pbk(2)
                for h in range(4):
                    MM(aps[:, h * 256:(h + 1) * 256], aup[:, 64 * h:64 * h + 64], adc[0], True, True, adc[1] + [wua_r], apr)
                for h in range(4):
                    ACTF(V4(X2[0])[:, h, :], aps[:, h * 256:(h + 1) * 256], AF.Exp, apr + [cst2], X2[1], scale=-1.0, bias=nprm64[:, 16 + l * 4 + h:16 + l * 4 + h + 1])
                TS("pool", X2[0], X2[0], 1.0, None, ALU.add, None, X2[1], X2[1])
                RCP(X2[0], X2[0], X2[1], X2[1])
                TT("dve", V4(KR0[0]), V4(kc[0]), bc(88 + l * 4), ALU.mult, kc[1] + [cst], KR0[1])
                ACTF(X4[0], KR0[0], AF.Square, KR0[1], X4[1])
                sps, spr = pbk(2)
                for hf in range(2):
                    MM(sps[:, hf * 512:(hf + 1) * 512], ones64r[:], X4[0][:, hf * 512:(hf + 1) * 512], True, True, X4[1] + [cst2], [spr[hf]])
                TS("dve", X4[0], sps, 1e-24, None, ALU.max, None, spr, X4[1])
                ACTF(X4[0], X4[0], AF.Ln, X4[1], X4[1])
                ACTF(X4[0], X4[0], AF.Exp, X4[1], X4[1], scale=-0.5)
                TT("dve", KR0[0], KR0[0], X4[0], ALU.mult, KR0[1] + X4[1], KR0[1])
                STT("dve", V4(X4[0]), V4(X2[0]), -1.0, bc(104 + l * 4), ALU.add, ALU.mult, X2[1] + [cst], X4[1])
                STT("dve", kc[0], X4[0], 1.0, kc[0], ALU.add, ALU.mult, X4[1] + kc[1], kc[1])
                TT("pool", RK[0], KR1[0], kc[0], ALU.mult, KR1[1] + kc[1], RK[1])
                TT("pool", V4(RK[0]), V4(RK[0]), bc(120 + l * 4), ALU.mult, RK[1] + [cst], RK[1])
                TT("pool", X2[0], KR0[0], X2[0], ALU.mult, KR0[1] + X2[1], X2[1])
                P.add("dve", lambda e: e.tensor_tensor_scan(out=C1[0], data0=scanmask[0], data1=X1[0], initial=0.0, op0=ALU.mult, op1=ALU.add), scanmask[1] + X1[1], C1[1])
                ACTF(T[0], C1[0], AF.Exp, C1[1], T[1])
                CP("pool", gC[0], V16(T[0])[:, :, 63], T[1], gC[1])
                TT("dve", KR1[0], KR1[0], T[0], ALU.mult, KR1[1] + T[1], KR1[1])
                ACTF(T[0], C1[0], AF.Exp, C1[1], T[1], scale=-1.0)
                TT("dve", kt[0], kc[0], T[0], ALU.mult, kc[1] + T[1], kt[1])
                TT("pool", bt[0], X2[0], T[0], ALU.mult, X2[1] + T[1], bt[1])
                TT("pool", T[0], C1[0], X1[0], ALU.subtract, C1[1] + X1[1], T[1])
                ACTF(T[0], T[0], AF.Exp, T[1], T[1])
                TT("dve", KR0[0], KR0[0], T[0], ALU.mult, KR0[1] + T[1], KR0[1])
                TT("pool", V16(T[0]), V16(C1[0])[:, :, 63:64].broadcast_to([64, 16, 64]), V16(C1[0]), ALU.subtract, C1[1], T[1])
                ACTF(T[0], T[0], AF.Exp, T[1], T[1])
                TT("dve", kc[0], kc[0], T[0], ALU.mult, kc[1] + T[1], kc[1])
                STT("pool", X2[0], X2[0], -1.0, T[0], ALU.mult, ALU.mult, X2[1] + T[1], X2[1])

                for (srcb, dstv, dstr) in ((KR0, V(UWrhs, 128)[:, :, 64:128], UWrhs[1]), (vc, V16(Vtok[0]), Vtok[1]), (kc, V16(Khtok[0]), Khtok[1]), (X2, V16(Bntok[0]), Bntok[1])):
                    tp, tpr = pbk(2)
                    for p in range(16):
                        P.add("pe", (lambda tp=tp, p=p, srcb=srcb: lambda e: e.transpose(tp[:, p * 64:(p + 1) * 64], V16(srcb[0])[:, p, :], id64))(), srcb[1] + [cst], [tpr[p // 8]])
                    CP("act", dstv, V16(tp), tpr, dstr)
                for (lh, dst, msk) in ((bt, NAbT, maskT2n), (kt, AkT, maskT2p)):
                    ap_, apr_ = pbk(4)
                    for p in range(16):
                        MM(ap_[:, p * 128:(p + 1) * 128], V16(lh[0])[:, p, :], KR4[:, :, p, :], True, True, lh[1] + KR[1], [apr_[p // 4]])
                    TT("dve", V(dst, 128), ap_.rearrange("c (p t) -> c p t", p=16), msk[:].unsqueeze(1).broadcast_to([64, 16, 128]), ALU.mult, apr_ + [cst, cst2], dst[1])
                ap_, apr_ = pbk(2)
                for p in range(16):
                    MM(ap_[:, p * 64:(p + 1) * 64], KR4[:, 0, p, :], V16(bt[0])[:, p, :], True, True, bt[1] + KR0[1], [apr_[p // 8]])
                TT("dve", V16(NAkb[0]), V16(ap_), maskLn[:].unsqueeze(1).broadcast_to([64, 16, 64]), ALU.mult, apr_ + [cst2], NAkb[1])
                NAb3 = V(NAbT, 128)
                R = T
                TT("dve", V16(R[0]), NAb3[:, :, 0:64], id_bc, ALU.add, NAbT[1] + [cst], R[1])
                Pprev = (NAb3[:, :, 0:64], NAbT[1])
                PTprev = (V16(NAkb[0]), NAkb[1])
                Pbufs = [X1, X4]
                PTbufs = [C1, NAkb]
                for k in range(1, 6):
                    Pn = Pbufs[(k - 1) % 2]
                    PTn = PTbufs[(k - 1) % 2]
                    if k < 5:
                        pp, ppr = pbk(2)
                        for p in range(16):
                            MM(pp[:, p * 64:(p + 1) * 64], PTprev[0][:, p, :], Pprev[0][:, p, :], True, True, PTprev[1] + Pprev[1], [ppr[p // 8]])
                    pt2, pt2r = pbk(2)
                    for p in range(16):
                        MM(pt2[:, p * 64:(p + 1) * 64], Pprev[0][:, p, :], PTprev[0][:, p, :], True, True, PTprev[1] + Pprev[1], [pt2r[p // 8]])
                    if k < 5:
                        CP("act", Pn[0], pp, ppr, Pn[1])
                    CP("dve", PTn[0], pt2, pt2r, PTn[1])
                    rr, rrr = pbk(2)
                    for p in range(16):
                        MM(rr[:, p * 64:(p + 1) * 64], V16(PTn[0])[:, p, :], V16(R[0])[:, p, :], True, True, PTn[1] + R[1], [rrr[p // 8]])
                    TT("dve", R[0], rr, R[0], ALU.add, rrr + R[1], R[1])
                    Pprev = (V16(Pn[0]), Pn[1])
                    PTprev = (V16(PTn[0]), PTn[1])
                xp, xpr = pbk(2)
                for p in range(16):
                    MM(xp[:, p * 64:(p + 1) * 64], V(AkT, 128)[:, p, 0:64], V16(Vtok[0])[:, p, :], True, True, AkT[1] + Vtok[1], [xpr[p // 8]])
                CP("act", V(UWrhs, 128)[:, :, 0:64], V16(xp), xpr, UWrhs[1])
                up4, up4r = pbk(4)
                for p in range(16):
                    MM(up4[:, p * 128:(p + 1) * 128], V16(R[0])[:, p, :], V(UWrhs, 128)[:, p, :], True, True, R[1] + UWrhs[1], [up4r[p // 4]])
                CP("act", UW[0][:, 0:1024], up4[:, 0:1024], up4r[0:2], UW[1][0:4])
                CP("dve", UW[0][:, 1024:2048], up4[:, 1024:2048], up4r[2:4], UW[1][4:8])
                UW3 = V(UW, 128)
                Qs, PTs, Dg, GT, Ysb, zr, Gg = X1, X4, pv[0], pv[1], kt, bt, C1
                qp, qpr = pbk(2)
                for p in range(16):
                    MM(qp[:, p * 64:(p + 1) * 64], V16(Khtok[0])[:, p, :], V16(Vtok[0])[:, p, :], True, False, Khtok[1] + Vtok[1], [qpr[p // 8]])
                    MM(qp[:, p * 64:(p + 1) * 64], V16(Bntok[0])[:, p, :], UW3[:, p, 0:64], False, True, Bntok[1] + UW[1], [qpr[p // 8]])
                CP("act", Qs[0], qp, qpr, Qs[1])
                pp2, pp2r = pbk(2)
                for p in range(16):
                    MM(pp2[:, p * 64:(p + 1) * 64], UW3[:, p, 64:128], V16(Bntok[0])[:, p, :], True, True, Bntok[1] + UW[1], [pp2r[p // 8]])
                TT("pool", V16(Dg[0]), id_bc, gC[0].unsqueeze(2).broadcast_to([64, 16, 64]), ALU.mult, gC[1] + [cst], Dg[1])
                TT("dve", PTs[0], pp2, Dg[0], ALU.add, pp2r + Dg[1], PTs[1])
                gp, gpr = pbk(2)
                for p in range(16):
                    MM(gp[:, p * 64:(p + 1) * 64], UW3[:, p, 64:128], NAb3[:, p, 64:128], True, True, UW[1] + NAbT[1], [gpr[p // 8]])
                TT("dve", GT[0], gp, KR1[0], ALU.add, gpr + KR1[1], GT[1])
                Q4 = Qs[0].rearrange("c (h n v) -> c h n v", h=4, n=4)
                for n in range(4):
                    sp_, spr_ = pbk(1)
                    for h in range(4):
                        MM(sp_[:, h * 64:(h + 1) * 64], V16(PTs[0])[:, h * 4 + n, :], Sst5[:, par, n, h, :], True, True, PTs[1] + [sst_res[par][n]], spr_)
                    if n < 3:
                        dsts, dres = Sst5[:, par, n + 1, :, :], sst_res[par][n + 1]
                    else:
                        dsts, dres = Sst5[:, 1 - par, 0, :, :], sst_res[1 - par][0]
                    TT("dve", dsts, sp_[:, 0:256].rearrange("c (h v) -> c h v", h=4), Q4[:, :, n, :], ALU.add, spr_ + Qs[1], [dres])
                yp, ypr = pbk(2)
                for p in range(16):
                    h, n = p // 4, p % 4
                    MM(yp[:, p * 64:(p + 1) * 64], Sst5[:, par, n, h, :], V16(GT[0])[:, p, :], True, False, [sst_res[par][n]] + GT[1], [ypr[p // 8]])
                    MM(yp[:, p * 64:(p + 1) * 64], V16(Vtok[0])[:, p, :], V(AkT, 128)[:, p, 64:128], False, False, Vtok[1] + AkT[1], [ypr[p // 8]])
                    MM(yp[:, p * 64:(p + 1) * 64], UW3[:, p, 0:64], NAb3[:, p, 64:128], False, True, UW[1] + NAbT[1], [ypr[p // 8]])
                CP("act", Ysb[0], yp, ypr, Ysb[1])
                mp, mpr = pbk(2)
                for hf in range(2):
                    MM(mp[:, hf * 512:(hf + 1) * 512], ones64[:], Ysb[0][:, hf * 512:(hf + 1) * 512], True, True, Ysb[1] + [cst2], [mpr[hf]])
                TT("dve", Ysb[0], Ysb[0], mp, ALU.subtract, Ysb[1] + mpr, Ysb[1])
                ACTF(T[0], Ysb[0], AF.Square, Ysb[1], T[1])
                vp_, vpr_ = pbk(2)
                for hf in range(2):
                    MM(vp_[:, hf * 512:(hf + 1) * 512], ones64[:], T[0][:, hf * 512:(hf + 1) * 512], True, True, T[1] + [cst2], [vpr_[hf]])
                ACTF(X4[0], vp_, AF.Ln, vpr_, X4[1], bias=GN_EPS)
                ACTF(X4[0], X4[0], AF.Exp, X4[1], X4[1], scale=-0.5)
                TT("dve", Ysb[0], Ysb[0], X4[0], ALU.mult, Ysb[1] + X4[1], Ysb[1])
                for h in range(4):
                    TS("pool" if h % 2 else "dve", V4(Ysb[0])[:, h, :], V4(Ysb[0])[:, h, :], prm64[:, 136 + l * 4 + h:137 + l * 4 + h], prm64[:, 152 + l * 4 + h:153 + l * 4 + h], ALU.mult, ALU.add, Ysb[1] + [cst], Ysb[1])
                bp, bpr = pbk(2)
                for hf in range(2):
                    MM(bp[:, hf * 512:(hf + 1) * 512], ones64r[:], RK[0][:, hf * 512:(hf + 1) * 512], True, True, RK[1] + [cst2], [bpr[hf]])
                TT("dve", T[0], bp, vc[0], ALU.mult, bpr + vc[1], T[1])
                TT("pool", Ysb[0], Ysb[0], T[0], ALU.add, Ysb[1] + T[1], Ysb[1])
                dma("sp", V4(zr[0]), src(3968, t0, t0 + 256), [pt_res[31], pt_res[32]], zr[1], "rw_z")
                ACTF(Gg[0], zr[0], AF.Exp, zr[1], Gg[1], scale=-1.0)
                TS("pool", Gg[0], Gg[0], 1.0, None, ALU.add, None, Gg[1], Gg[1])
                RCP(Gg[0], Gg[0], Gg[1], Gg[1])
                TT("pool", Gg[0], Gg[0], zr[0], ALU.mult, Gg[1] + zr[1], Gg[1])
                outb = (Dg[0].bitcast(BF16)[:, 0:1024], Dg[1])
                TT("dve", outb[0], Ysb[0], Gg[0], ALU.mult, Ysb[1] + Gg[1], outb[1])
                for h in range(4):
                    dma("sp", big[64 * (h % 2):64 * (h % 2) + 64, 6 + h // 2, t0:t0 + 256], V4(outb[0])[:, h, :], outb[1], [big_res[6 + h // 2][gi // 2]], "rw_o")


        def emit_layer(l):
            src_d = x_d if l == 0 else xs_d
            if stop_after == 'setup':
                return
            A.reset()
            xt = [A.alloc(D) for _ in range(3)]
            hb = [A.alloc(D, BF16) for _ in range(2)]
            junk = A.alloc(D, BF16)
            for i in range(NT):
                xa, xr = xt[i % 3]
                ha, hr = hb[i % 2]
                dma("sp", xa, src_d[i * 128:(i + 1) * 128, :], [xs_res[i]] if l > 0 else [], xr, f"xt{i % 3}")
                P.add("act", (lambda xa=xa, i=i: lambda e: e.activation(out=junk[0], in_=xa, func=AF.Square, accum_out=ss[:, 3 * i:3 * i + 1]))(), xr, junk[1] + [ss_res[i]])
                P.add("act", (lambda i=i: lambda e: e.activation(out=ss[:, 3 * i + 1:3 * i + 2], in_=ss[:, 3 * i:3 * i + 1], func=AF.Ln, scale=1.0 / D, bias=NORM_EPS))(), [ss_res[i]], [ss_res[i]])
                P.add("act", (lambda i=i: lambda e: e.activation(out=ss[:, 3 * i + 2:3 * i + 3], in_=ss[:, 3 * i + 1:3 * i + 2], func=AF.Exp, scale=-0.5))(), [ss_res[i]], [ss_res[i]])
                P.add("dve", (lambda xa=xa, ha=ha, i=i: lambda e: e.tensor_scalar(out=ha, in0=xa, scalar1=ss[:, 3 * i + 2:3 * i + 3], scalar2=None, op0=ALU.mult))(), xr + [ss_res[i]], hr)
                bank = i % 2
                pa, pr_ = psb(bank)
                pab = pa.bitcast(BF16)
                for k in range(8):
                    P.add("pe", (lambda pab=pab, ha=ha, k=k: lambda e: e.transpose(pab[:, k * 128:(k + 1) * 128], ha[:, k * 128:(k + 1) * 128], identb[:]))(), hr + [cst2], pr_)
                eng = "act" if i % 2 == 0 else "dve"
                dst = big[:, :, i * 128:(i + 1) * 128]
                srcv = pab.rearrange("p (k t) -> p k t", k=8)
                wr = [big_res[k][i // 4] for k in range(8)]
                if eng == "act":
                    P.add("act", (lambda dst=dst, srcv=srcv: lambda e: e.activation(out=dst, in_=srcv, func=AF.Copy))(), pr_, wr)
                else:
                    P.add("dve", (lambda dst=dst, srcv=srcv: lambda e: e.tensor_copy(out=dst, in_=srcv))(), pr_, wr)

            if stop_after == 'A':
                return
            A.reset()
            wst = [A.alloc(8 * 512) for _ in range(2)]
            wbf = [A.alloc(8 * 512, BF16) for _ in range(2)]
            stage = [A.alloc(S) for _ in range(2)]
            vst = [A.alloc(512, BF16) for _ in range(2)]
            allbig = [big_res[k][g] for k in range(8) for g in range(NG)]

            def load_w(r):
                width = 512 if r < 8 else 128
                wa, wr_ = wst[r % 2]
                wb, wbr = wbf[r % 2]
                wa3 = wa.rearrange("p (k c) -> p k c", k=8)
                wb3 = wb.rearrange("p (k c) -> p k c", k=8)
                if "nowload" not in DBG:
                    dma("sp", wa3[:, :, 0:width], win_d[l, :, r * 512:r * 512 + width].rearrange("(k p) c -> p k c", p=128), [], wr_, f"wst{r % 2}")
                else:
                    P.add("dve", lambda e: e.memset(wa3[:, :, 0:width], 0.5), [], wr_)
                for k in range(8):
                    P.add("pool", (lambda wa3=wa3, wb3=wb3, k=k, width=width: lambda e: e.tensor_scalar(out=wb3[:, k, 0:width], in0=wa3[:, k, 0:width], scalar1=prm128[:, l * 8 + k:l * 8 + k + 1], scalar2=None, op0=ALU.mult))(), wr_ + [cst], wbr)

            load_w(0)
            pbank = 0
            ev = 0
            for r in range(9):
                if r + 1 < 9:
                    load_w(r + 1)
                width = 512 if r < 8 else 128
                wb, wbr = wbf[r % 2]
                wb3 = wb.rearrange("p (k c) -> p k c", k=8)
                if r == 2:
                    for i in range(NT):
                        pa, pr_ = psb(4 + pbank % 4)
                        pbank += 1
                        for k in range(8):
                            P.add("pe", (lambda pa=pa, k=k, i=i, wb3=wb3: lambda e: e.matmul(pa, big[:, k, i * 128:(i + 1) * 128], wb3[:, k, :], start=(k == 0), stop=(k == 7)))(), [big_res[k][i // 4], ] + wbr, pr_)
                        va, vr = vst[i % 2]
                        eng = "act" if ev % 2 == 0 else "dve"
                        ev += 1
                        if eng == "act":
                            P.add("act", (lambda va=va, pa=pa: lambda e: e.activation(out=va, in_=pa, func=AF.Copy))(), pr_, vr)
                        else:
                            P.add("dve", (lambda va=va, pa=pa: lambda e: e.tensor_copy(out=va, in_=pa))(), pr_, vr)
                        if "nostore" not in DBG:
                            dma("sp", vtok_d[i * 128:(i + 1) * 128, :], va, vr, [vtok_res[i]], f"vst{i % 2}")
                    continue
                for jj in range(width // 128):
                    j = 4 * r + jj
                    sa, sr = stage[j % 2]
                    for tg in range(NG):
                        pa, pr_ = psb(4 + pbank % 4)
                        pbank += 1
                        for k in range(8):
                            P.add("pe", (lambda pa=pa, k=k, tg=tg, jj=jj, wb3=wb3: lambda e: e.matmul(pa, wb3[:, k, jj * 128:(jj + 1) * 128], big[:, k, tg * 512:(tg + 1) * 512], start=(k == 0), stop=(k == 7)))(), [big_res[k][tg]] + wbr, pr_)
                        eng = "act" if ev % 2 == 0 else "dve"
                        ev += 1
                        sres = sr[tg * 2:(tg + 1) * 2]
                        if eng == "act":
                            P.add("act", (lambda sa=sa, pa=pa, tg=tg: lambda e: e.activation(out=sa[:, tg * 512:(tg + 1) * 512], in_=pa, func=AF.Copy))(), pr_, sres)
                        else:
                            P.add("dve", (lambda sa=sa, pa=pa, tg=tg: lambda e: e.tensor_copy(out=sa[:, tg * 512:(tg + 1) * 512], in_=pa))(), pr_, sres)
                        if "nostore" not in DBG and "nopt" not in DBG:
                            dma("sp", pt_d[j * 128:(j + 1) * 128, tg * 512:(tg + 1) * 512], sa[:, tg * 512:(tg + 1) * 512], sres, [pt_res[j]], f"stg{j % 2}_{tg}")

            if l == tap_layer and "pt" in taps:
                dma("sp", tap_t["pt"].ap()[0:1024, :], pt_d[0:1024, :], pt_res, [out_res], "tap")
                dma("sp", tap_t["pt"].ap()[1536:INC, :], pt_d[1536:INC, :], pt_res, [out_res], "tap")

            if stop_after == 'B':
                return
            A.reset()
            qkf = A.alloc(S)
            qb = [A.alloc(S, BF16) for _ in range(2)]
            kb = [A.alloc(S, BF16) for _ in range(2)]
            vh = [A.alloc(NT * 128, BF16) for _ in range(2)]
            zf = [A.alloc(512) for _ in range(2)]
            ob = [A.alloc(4 * 512) for _ in range(2)]
            ptb = [A.alloc(2 * 512, BF16) for _ in range(3)]
            r01 = A.alloc(2 * 512)
            o01 = A.alloc(2 * 512)
            osb = A.alloc(512)
            sqb = A.alloc(512, BF16)
            rsd = A.alloc(512)
            gat = A.alloc(512)
            linit = 0.8 - 0.6 * math.exp(-0.3 * l)
            pti = 0
            sti = 0
            zi = 0
            for h in range(4):
                hs = h % 2
                for c4 in range(4):
                    dma("sp", qkf[0][:, c4 * 1024:(c4 + 1) * 1024], pt_d[h * 128:(h + 1) * 128, c4 * 1024:(c4 + 1) * 1024], [pt_res[h]], qkf[1], "qkf")
                P.add("dve", (lambda hs=hs: lambda e: e.tensor_copy(out=qb[hs][0], in_=qkf[0]))(), qkf[1], qb[hs][1])
                for c4 in range(4):
                    dma("sp", qkf[0][:, c4 * 1024:(c4 + 1) * 1024], pt_d[512 + h * 128:512 + (h + 1) * 128, c4 * 1024:(c4 + 1) * 1024], [pt_res[4 + h]], qkf[1], "qkf")
                P.add("pool", (lambda hs=hs: lambda e: e.tensor_copy(out=kb[hs][0], in_=qkf[0]))(), qkf[1], kb[hs][1])
                vh3 = vh[hs][0].rearrange("p (n d) -> p n d", n=NT)
                for c4 in range(4):
                    dma("sp", vh3[:, c4 * 8:(c4 + 1) * 8, :], vtok_d[c4 * 1024:(c4 + 1) * 1024, h * 128:(h + 1) * 128].rearrange("(n p) d -> p n d", p=128), vtok_res, vh[hs][1], f"vh{hs}")
                qbh, kbh = qb[hs][0], kb[hs][0]
                for g in range(NG):
                    nk = 4 * g + 4
                    zsl, zsr = zf[zi % 2]
                    dma("sp", zsl, pt_d[1536 + h * 128:1536 + (h + 1) * 128, g * 512:(g + 1) * 512], [pt_res[12 + h]], zsr, f"zf{zi % 2}")
                    zi += 1
                    accs = [psb(4 + a_) for a_ in range(4)]
                    for kt in range(nk):
                        sbank = (sti % 2) * 2
                        sti += 1
                        s2, s2r = psb(sbank, 2)
                        r = kt - 4 * g
                        diag = r >= -1
                        c0 = 128 * max(r, 0)
                        for m in range(2):
                            P.add("pe", (lambda s2=s2, m=m, kt=kt, g=g, diag=diag, kbh=kbh, qbh=qbh, c0=c0: lambda e: e.matmul(s2[:, m * 512 + c0:(m + 1) * 512], kbh[m * 64:(m + 1) * 64, kt * 128:(kt + 1) * 128], qbh[m * 64:(m + 1) * 64, g * 512 + c0:(g + 1) * 512], start=True, stop=not diag))(), qb[hs][1] + kb[hs][1], [s2r[m]])
                            if diag:
                                for s_ in range(max(r, 0), 4):
                                    dlt = s_ - r
                                    bi = 0 if dlt == 0 else (1 if dlt == 1 else 2)
                                    P.add("pe", (lambda s2=s2, m=m, h=h, s_=s_, bi=bi: lambda e: e.matmul(s2[:, m * 512 + s_ * 128:m * 512 + (s_ + 1) * 128], identb[:], bblk[:, h, bi, :], start=False, stop=(s_ == 3)))(), [cst2, bias_r], [s2r[m]])
                        pa_, par = ptb[pti % 3]
                        pti += 1
                        s23 = s2.rearrange("p (m q) -> p m q", m=2)[:, :, c0:512]
                        pa3 = pa_.rearrange("p (m q) -> p m q", m=2)[:, :, c0:512]
                        if diag:
                            P.add("act", (lambda s23=s23, pa3=pa3: lambda e: e.activation(out=pa3, in_=s23, func=AF.Exp, scale=SCALE))(), s2r, par)
                        else:
                            P.add("act", (lambda s23=s23, pa3=pa3, h=h: lambda e: e.activation(out=pa3, in_=s23, func=AF.Exp, scale=SCALE, bias=cfar[:, 4 + h:5 + h]))(), s2r + [cst2], par)
                        for m in range(2):
                            P.add("pe", (lambda m=m, kt=kt, pa_=pa_, nk=nk, vh3=vh3, acc=accs[m][0], c0=c0: lambda e: e.matmul(acc[:, c0:512], vh3[:, kt, :], pa_[:, m * 512 + c0:(m + 1) * 512], start=(kt == 0), stop=(kt == nk - 1)))(), par + vh[hs][1], accs[m][1])
                            P.add("pe", (lambda m=m, kt=kt, pa_=pa_, nk=nk, acc=accs[2 + m][0], c0=c0: lambda e: e.matmul(acc[:, c0:512], onesb[:], pa_[:, m * 512 + c0:(m + 1) * 512], start=(kt == 0), stop=(kt == nk - 1)))(), par + [cst2], accs[2 + m][1])
                    oba, obr = ob[g % 2]
                    acc4, acc4r = psb(4, 4)
                    P.add("act", (lambda oba=oba, acc4=acc4: lambda e: e.activation(out=oba[:, 0:1024], in_=acc4[:, 0:1024], func=AF.Copy))(), acc4r[0:2], obr)
                    P.add("dve", (lambda oba=oba, acc4=acc4: lambda e: e.tensor_copy(out=oba[:, 1024:2048], in_=acc4[:, 1024:2048]))(), acc4r[2:4], obr)
                    P.add("dve", (lambda oba=oba: lambda e: e.reciprocal(out=r01[0], in_=oba[:, 1024:2048]))(), obr, r01[1])
                    P.add("dve", (lambda oba=oba: lambda e: e.tensor_tensor(out=o01[0], in0=oba[:, 0:1024], in1=r01[0], op=ALU.mult))(), obr + r01[1], o01[1])
                    P.add("dve", (lambda: lambda e: e.scalar_tensor_tensor(out=osb[0], in0=o01[0][:, 512:1024], scalar=lamv[:, l * 8 + 5:l * 8 + 6], in1=o01[0][:, 0:512], op0=ALU.mult, op1=ALU.add))(), o01[1] + [lam_r], osb[1])
                    P.add("act", (lambda: lambda e: e.activation(out=sqb[0], in_=osb[0], func=AF.Square))(), osb[1], sqb[1])
                    sq_ps, sq_r = psb(0)
                    P.add("pe", (lambda sq_ps=sq_ps: lambda e: e.matmul(sq_ps, onesb[:], sqb[0], start=True, stop=True))(), sqb[1] + [cst2], sq_r)
                    P.add("act", (lambda sq_ps=sq_ps: lambda e: e.activation(out=rsd[0], in_=sq_ps, func=AF.Ln, scale=1.0 / 128, bias=SUBLN_EPS))(), sq_r, rsd[1])
                    P.add("act", (lambda: lambda e: e.activation(out=rsd[0], in_=rsd[0], func=AF.Exp, scale=-0.5))(), rsd[1], rsd[1])
                    P.add("act", (lambda zsl=zsl: lambda e: e.activation(out=gat[0], in_=zsl, func=AF.Exp, scale=-1.0))(), zsr, gat[1])
                    P.add("pool", (lambda: lambda e: e.tensor_scalar(out=gat[0], in0=gat[0], scalar1=1.0, scalar2=None, op0=ALU.add))(), gat[1], gat[1])
                    P.add("dve", (lambda: lambda e: e.reciprocal(out=gat[0], in_=gat[0]))(), gat[1], gat[1])
                    P.add("pool", (lambda zsl=zsl: lambda e: e.tensor_tensor(out=gat[0], in0=gat[0], in1=zsl, op=ALU.mult))(), gat[1] + zsr, gat[1])
                    P.add("dve", (lambda: lambda e: e.tensor_tensor(out=osb[0], in0=osb[0], in1=rsd[0], op=ALU.mult))(), osb[1] + rsd[1], osb[1])
                    P.add("dve", (lambda: lambda e: e.tensor_scalar(out=osb[0], in0=osb[0], scalar1=prm128[:, 32 + l:33 + l], scalar2=(1.0 - linit), op0=ALU.mult, op1=ALU.mult))(), osb[1] + [cst], osb[1])
                    P.add("dve", (lambda h=h, g=g: lambda e: e.tensor_tensor(out=big[:, h, g * 512:(g + 1) * 512], in0=osb[0], in1=gat[0], op=ALU.mult))(), osb[1] + gat[1], [big_res[h][g]])

            if stop_after == 'C':
                return
            A.reset()
            cbuf = [[A.alloc(516) for _ in range(4)] for _ in range(2)]
            cw = lambda k, j: prm128[:, 36 + l * 6 + k * 2 + j:36 + l * 6 + k * 2 + j + 1]
            it = 0
            for j in range(2):
                for g in range(NG):
                    bs = cbuf[it % 2]
                    it += 1
                    (cba, cbr), (cca, ccr), (cha, chr_), (cza, czr) = bs
                    t0 = g * 512
                    lo = 2 if g > 0 else 0
                    dma("sp", cca[:, 2 - lo:514], pt_d[(18 + j) * 128:(19 + j) * 128, t0 - lo:t0 + 512], [pt_res[18 + j]], ccr, f"cv{it % 2}")
                    dma("sp", cha[:, 2 - lo:514], pt_d[(20 + j) * 128:(21 + j) * 128, t0 - lo:t0 + 512], [pt_res[20 + j]], chr_, f"cv{it % 2}")
                    dma("sp", cba[:, 0:512], pt_d[(16 + j) * 128:(17 + j) * 128, t0:t0 + 512], [pt_res[16 + j]], cbr, f"cv{it % 2}")
                    dma("sp", cza[:, 0:512], pt_d[(22 + j) * 128:(23 + j) * 128, t0:t0 + 512], [pt_res[22 + j]], czr, f"cv{it % 2}")
                    if g == 0:
                        P.add("pool", (lambda cca=cca: lambda e: e.memset(cca[:, 0:2], 0.0))(), [], ccr)
                        P.add("pool", (lambda cha=cha: lambda e: e.memset(cha[:, 0:2], 0.0))(), [], chr_)
                    P.add("pool", (lambda cca=cca, cha=cha: lambda e: e.tensor_tensor(out=cca[:, 0:514], in0=cca[:, 0:514], in1=cha[:, 0:514], op=ALU.mult))(), ccr + chr_, ccr)
                    P.add("dve", (lambda cca=cca, cha=cha, j=j: lambda e: e.tensor_scalar(out=cha[:, 0:512], in0=cca[:, 0:512], scalar1=cw(0, j), scalar2=None, op0=ALU.mult))(), ccr + [cst], chr_)
                    P.add("dve", (lambda cca=cca, cha=cha, j=j: lambda e: e.scalar_tensor_tensor(out=cha[:, 0:512], in0=cca[:, 1:513], scalar=cw(1, j), in1=cha[:, 0:512], op0=ALU.mult, op1=ALU.add))(), ccr + chr_ + [cst], chr_)
                    P.add("dve", (lambda cca=cca, cha=cha, j=j: lambda e: e.scalar_tensor_tensor(out=cha[:, 0:512], in0=cca[:, 2:514], scalar=cw(2, j), in1=cha[:, 0:512], op0=ALU.mult, op1=ALU.add))(), ccr + chr_ + [cst], chr_)
                    P.add("act", (lambda cza=cza, cca=cca: lambda e: e.activation(out=cca[:, 0:512], in_=cza[:, 0:512], func=AF.Exp, scale=-1.0))(), czr + ccr, ccr)
                    P.add("pool", (lambda cca=cca: lambda e: e.tensor_scalar(out=cca[:, 0:512], in0=cca[:, 0:512], scalar1=1.0, scalar2=None, op0=ALU.add))(), ccr, ccr)
                    P.add("dve", (lambda cca=cca: lambda e: e.reciprocal(out=cca[:, 0:512], in_=cca[:, 0:512]))(), ccr, ccr)
                    P.add("pool", (lambda cca=cca, cza=cza: lambda e: e.tensor_tensor(out=cca[:, 0:512], in0=cca[:, 0:512], in1=cza[:, 0:512], op=ALU.mult))(), ccr + czr, ccr)
                    P.add("pool", (lambda cha=cha, cba=cba: lambda e: e.tensor_tensor(out=cha[:, 0:512], in0=cha[:, 0:512], in1=cba[:, 0:512], op=ALU.mult))(), chr_ + cbr, chr_)
                    P.add("dve", (lambda cha=cha, cca=cca, j=j, g=g: lambda e: e.tensor_tensor(out=big[:, 4 + j, g * 512:(g + 1) * 512], in0=cha[:, 0:512], in1=cca[:, 0:512], op=ALU.mult))(), chr_ + ccr, [big_res[4 + j][g]])

            if stop_after == 'D':
                return
            if do_rwkv:
                rwkv_phase(l)
            else:
                for k in (6, 7):
                    for g in range(NG):
                        P.add("pool", (lambda k=k, g=g: lambda e: e.memset(big[:, k, g * 512:(g + 1) * 512], 0.0))(), [], [big_res[k][g]])

            if l == tap_layer and "mixed" in taps:
                dma("sp", tap_t["mixed"].ap().rearrange("(k p) t -> p k t", p=128), big[:], allbig, [out_res], "tap")

            if stop_after == 'E':
                return
            A.reset()
            wo_st = [A.alloc(D) for _ in range(2)]
            wo = A.alloc(8 * D, BF16)
            wo3 = wo[0].rearrange("p (k d) -> p k d", k=8)
            xin = [A.alloc(D) for _ in range(2)]
            xo = [A.alloc(D) for _ in range(2)]
            for k in range(8):
                wa, wr_ = wo_st[k % 2]
                dma("sp", wa, wout_d[l, k * 128:(k + 1) * 128, :], [], wr_, f"wo{k % 2}")
                P.add("pool" if k % 2 else "dve", (lambda wa=wa, k=k: lambda e: e.tensor_copy(out=wo3[:, k, :], in_=wa))(), wr_, wo[1])
            last = (l == depth - 1)
            if last:
                fga, fgr = A.alloc(D)
                dma("sp", fga, fg_t.ap(), [], fgr, "fgl")
            for i in range(NT):
                xa, xr = xin[i % 2]
                ya, yr = xo[i % 2]
                dma("sp", xa, src_d[i * 128:(i + 1) * 128, :], [xs_res[i]] if l > 0 else [], xr, f"xin{i % 2}")
                p2, p2r = psb((i % 2) * 2, 2)
                for half in range(2):
                    for k in range(8):
                        P.add("pe", (lambda p2=p2, half=half, k=k, i=i: lambda e: e.matmul(p2[:, half * 512:(half + 1) * 512], big[:, k, i * 128:(i + 1) * 128], wo3[:, k, half * 512:(half + 1) * 512], start=(k == 0), stop=(k == 7)))(), [big_res[k][i // 4]] + wo[1], [p2r[half]])
                P.add("dve", (lambda ya=ya, p2=p2, xa=xa: lambda e: e.tensor_tensor(out=ya, in0=p2, in1=xa, op=ALU.add))(), p2r + xr, yr)
                if not last:
                    dma("sp", xs_d[i * 128:(i + 1) * 128, :], ya, yr, [xs_res[i]], f"xo{i % 2}")
                else:
                    if "xs" in taps:
                        dma("sp", tap_t["xs"].ap()[i * 128:(i + 1) * 128, :], ya, yr, [out_res], f"xo{i % 2}")
                    fr = Res(f"fin{i}")
                    P.add("act", (lambda ya=ya, xa=xa, i=i: lambda e: e.activation(out=xa, in_=ya, func=AF.Square, accum_out=ss[:, 3 * i:3 * i + 1]))(), yr, xr + [ss_res[i]])
                    P.add("act", (lambda i=i: lambda e: e.activation(out=ss[:, 3 * i + 1:3 * i + 2], in_=ss[:, 3 * i:3 * i + 1], func=AF.Ln, scale=1.0 / D, bias=NORM_EPS))(), [ss_res[i]], [ss_res[i]])
                    P.add("act", (lambda i=i: lambda e: e.activation(out=ss[:, 3 * i + 2:3 * i + 3], in_=ss[:, 3 * i + 1:3 * i + 2], func=AF.Exp, scale=-0.5))(), [ss_res[i]], [ss_res[i]])
                    P.add("dve", (lambda ya=ya, xa=xa, i=i: lambda e: e.scalar_tensor_tensor(out=xa, in0=ya, scalar=ss[:, 3 * i + 2:3 * i + 3], in1=fga, op0=ALU.mult, op1=ALU.mult))(), yr + [ss_res[i]] + fgr, xr)
                    dma("sp", out_d[i * 128:(i + 1) * 128, :], xa, xr, [out_res], f"xo{i % 2}")

        for l_ in range(depth):
            emit_layer(l_)
        P.add("sp", lambda e: e.nop(), [out_res], [])
        nsem = P.emit(nc, st)
    return nc, len(P.ops), nsem


_CACHE = {}


def kernel(**inputs):
    x = np.asarray(inputs["x"], np.float32)
    prm128, fg, lamrep, prm64 = host_params(inputs)
    consts = make_consts()
    if "nc" not in _CACHE:
        _CACHE["nc"] = build(L)[0]
    nc = _CACHE["nc"]
    shared = {
        "w_in": np.ascontiguousarray(np.asarray(inputs["w_in"], np.float32)),
        "w_out": np.ascontiguousarray(np.asarray(inputs["w_out"], np.float32)),
        "rel_bias": np.ascontiguousarray(np.asarray(inputs["rel_bias"], np.float32)),
        "w_up": np.ascontiguousarray(np.asarray(inputs["w_up"], np.float32)),
        "a_up": np.ascontiguousarray(np.asarray(inputs["a_up"], np.float32)),
        "prm128": prm128, "fg": fg, "lamrep": lamrep, "prm64": prm64,
    }
    shared.update(consts)
    in_maps = []
    for b in range(8):
        m = dict(shared)
        m["x"] = np.ascontiguousarray(x[b])
        in_maps.append(m)
    res = run_bass_kernel_spmd(nc, in_maps, core_ids=list(range(8)))
    return np.stack([np.asarray(r["out"], np.float32) for r in res.results], axis=0)
```

```python
import math
from contextlib import ExitStack

import numpy as np
import ml_dtypes

import concourse.bass as bass
import concourse.mybir as mybir
from concourse.bass_utils import run_bass_kernel_spmd

F32 = mybir.dt.float32
BF16 = mybir.dt.bfloat16
AF = mybir.ActivationFunctionType
ALU = mybir.AluOpType
AX = mybir.AxisListType

S = 4096
D = 1024
NT = 32
NG = 8
L = 4
INC = 4224
NEG8 = -240000.0
NORM_EPS = 1e-6
SUBLN_EPS = 1e-5
GN_EPS = 64e-5
SCALE = 0.125
DBG = set()
POOL_AS = "dve"


class Res:
    __slots__ = ("name", "writer", "readers")

    def __init__(self, name):
        self.name = name
        self.writer = None
        self.readers = []


class Op:
    __slots__ = ("eng", "fn", "deps", "dma", "idx", "sig", "waits", "has_dep")


class Prog:
    def __init__(self):
        self.ops = []

    def add(self, eng, fn, reads=(), writes=(), dma=None):
        if dma is None and eng == "pool" and POOL_AS:
            eng = POOL_AS
        op = Op()
        op.eng = eng
        op.fn = fn
        op.dma = dma
        op.idx = len(self.ops)
        op.deps = {}
        op.has_dep = False
        op.sig = None

        def dep(d, kind):
            if d is None or d is op:
                return
            if op.deps.get(d) != "raw":
                op.deps[d] = kind

        for r in reads:
            dep(r.writer, "raw")
        for w in writes:
            dep(w.writer, "waw")
            for rd in w.readers:
                dep(rd, "war")
        k = (op.eng, op.dma)
        for r in reads:
            r.readers = [x for x in r.readers if (x.eng, x.dma) != k]
            r.readers.append(op)
        for w in writes:
            w.writer = op
            w.readers = []
        self.ops.append(op)
        return op

    def finalize(self):
        for op in self.ops:
            keep = {}
            for d, kind in op.deps.items():
                if d.dma is None and op.dma is None and d.eng == op.eng:
                    if op.eng == "pe":
                        continue
                keep[d] = kind
            op.deps = keep
            for d in keep:
                d.has_dep = True
        cnt = {}
        waited = {}
        for op in self.ops:
            w = {}
            for d in op.deps:
                key = d.dma if d.dma else d.eng
                val = cnt[key] if d.dma else d.sig
                if w.get(key, 0) < val:
                    w[key] = val
            q = waited.setdefault(op.eng, {})
            op.waits = []
            for kk, v in w.items():
                if q.get(kk, 0) < v:
                    q[kk] = v
                    op.waits.append((kk, v))
            if op.dma:
                cnt[op.dma] = cnt.get(op.dma, 0) + 16
                op.sig = cnt[op.dma]
            elif op.has_dep:
                cnt[op.eng] = cnt.get(op.eng, 0) + 1
                op.sig = cnt[op.eng]
        self.cnt = cnt

    def emit(self, nc, st):
        self.finalize()
        keys = set()
        for op in self.ops:
            for kk, _ in op.waits:
                keys.add(kk)
            if op.dma:
                keys.add(op.dma)
            elif op.sig is not None:
                keys.add(op.eng)
        sems = {kk: st.enter_context(nc.semaphore("s_" + kk)) for kk in sorted(keys)}
        block = st.enter_context(nc.Block())
        ops = self.ops

        def run(name):
            def body(e):
                for op in ops:
                    if op.eng != name:
                        continue
                    for kk, v in op.waits:
                        e.wait_ge(sems[kk], v)
                    ins = op.fn(e)
                    if op.sig is not None:
                        ins.then_inc(sems[op.dma or op.eng], 16 if op.dma else 1)

            return body

        block.tensor(run("pe"))
        block.scalar(run("act"))
        block.vector(run("dve"))
        block.gpsimd(run("pool"))
        block.sync(run("sp"))
        return len(sems)


def _bucket(dist):
    n = np.maximum(dist, 0)
    max_exact = 16
    nf = np.maximum(n, 1).astype(np.float32)
    large = max_exact + (np.log(nf / max_exact) / math.log(128 / max_exact) * (32 - max_exact)).astype(np.int32)
    large = np.minimum(large, 31)
    return np.where(n < max_exact, n, large)


def _bucket_jax_exact():
    import jax
    import jax.numpy as jnp

    with jax.default_device(jax.devices("cpu")[0]):
        dist = jnp.arange(0, 256)
        n = jnp.maximum(dist, 0)
        nf = jnp.maximum(n, 1).astype(jnp.float32)
        large = 16 + (jnp.log(nf / 16) / math.log(128 / 16) * 16).astype(jnp.int32)
        large = jnp.minimum(large, 31)
        return np.asarray(jnp.where(n < 16, n, large))


def make_consts():
    c = {}
    c["c_identf"] = np.eye(128, dtype=np.float32)
    try:
        bk = _bucket_jax_exact()
    except Exception:
        bk = _bucket(np.arange(256))
    oh = np.zeros((33, 384), np.float32)
    for m in range(384):
        dist = m - 128
        if dist < 0:
            oh[32, m] = 8.0
        else:
            oh[bk[dist], m] = 8.0
    c["c_onehot8"] = oh
    j = np.arange(64)[:, None]
    t = np.arange(64)[None, :]
    strict = (j < t).astype(np.float32)
    incl = (j <= t).astype(np.float32)
    c["c_maskT2"] = np.concatenate([strict, incl], axis=1)
    c["c_maskL"] = (t < j).astype(np.float32)
    sm = np.ones((64, 1024), np.float32)
    sm[:, ::64] = 0.0
    c["c_scanmask"] = sm
    return c


def host_params(inp):
    g = np.asarray(inp["norm_g"], np.float32)
    gT = g.reshape(L, 8, 128).transpose(2, 0, 1).reshape(128, L * 8)
    sublnT = np.asarray(inp["subln_g"], np.float32).T
    convT = np.asarray(inp["conv_w"], np.float32).reshape(L, 3, 2, 128).transpose(3, 0, 1, 2).reshape(128, L * 6)
    prm128 = np.ascontiguousarray(np.concatenate([gT, sublnT, convT], axis=1))
    fg = np.ascontiguousarray(np.broadcast_to(np.asarray(inp["final_norm_g"], np.float32)[None, :], (128, D)))
    lamrep = np.ascontiguousarray(np.broadcast_to(np.asarray(inp["lam_qk"], np.float32).reshape(1, L * 256), (128, L * 256)))
    mu = np.asarray(inp["rwkv_mu"], np.float32)
    mu_rkv = mu[:, :768].reshape(L, 3, 4, 64).transpose(3, 0, 1, 2).reshape(64, L * 12)
    mu_wa = mu[:, 768:896].reshape(L, 2, 64).transpose(2, 0, 1).reshape(64, L * 2)

    def ch(a):
        return np.asarray(a, np.float32).reshape(L, 4, 64).transpose(2, 0, 1).reshape(64, L * 4)

    prm64 = np.ascontiguousarray(np.concatenate(
        [mu_rkv, mu_wa, ch(inp["w0"]), ch(inp["a0"]), ch(inp["k_k"]), ch(inp["k_a"]),
         ch(np.asarray(inp["r_k"]).reshape(L, 256)), ch(inp["lnx_g"]), ch(inp["lnx_b"])], axis=1))
    return prm128, fg, lamrep, prm64


def build(depth=L, taps=(), do_rwkv=True, tap_layer=0, stop_after=None):
    nc = bass.Bass("TRN2", target_bir_lowering=False)
    P = Prog()
    dram_in = lambda n, s, d=F32: nc.dram_tensor(n, list(s), d, kind="ExternalInput")
    x_t = dram_in("x", [S, D])
    win_t = dram_in("w_in", [L, D, INC])
    wout_t = dram_in("w_out", [L, D, D])
    relb_t = dram_in("rel_bias", [32, 4])
    wup_t = dram_in("w_up", [L, 64, 256])
    aup_t = dram_in("a_up", [L, 64, 256])
    prm128_t = dram_in("prm128", [128, 60])
    fg_t = dram_in("fg", [128, D])
    lamrep_t = dram_in("lamrep", [128, L * 256])
    prm64_t = dram_in("prm64", [64, 168])
    cidf_t = dram_in("c_identf", [128, 128])
    coh_t = dram_in("c_onehot8", [33, 384])
    cm2_t = dram_in("c_maskT2", [64, 128])
    cml_t = dram_in("c_maskL", [64, 64])
    csm_t = dram_in("c_scanmask", [64, 1024])
    out_t = nc.dram_tensor("out", [S, D], F32, kind="ExternalOutput")
    xs_t = nc.dram_tensor("xs", [S, D], F32, kind="Internal")
    pt_t = nc.dram_tensor("ptf", [INC, S], F32, kind="Internal")
    vtok_t = nc.dram_tensor("vtok", [S, 512], BF16, kind="Internal")
    gsc_t = nc.dram_tensor("gsc", [4, 130 * 384], F32, kind="Internal")
    tap_t = {}
    if "pt" in taps:
        tap_t["pt"] = nc.dram_tensor("tap_pt", [INC, S], F32, kind="ExternalOutput")
    if "mixed" in taps:
        tap_t["mixed"] = nc.dram_tensor("tap_mixed", [D, S], BF16, kind="ExternalOutput")
    if "xs" in taps:
        tap_t["xs"] = nc.dram_tensor("tap_xs", [S, D], F32, kind="ExternalOutput")

    x_d, win_d, wout_d = x_t.ap(), win_t.ap(), wout_t.ap()
    out_d, xs_d, pt_d, vtok_d = out_t.ap(), xs_t.ap(), pt_t.ap(), vtok_t.ap()

    xs_res = [Res(f"xs{i}") for i in range(NT)]
    pt_res = [Res(f"pt{j}") for j in range(33)]
    vtok_res = [Res(f"vt{i}") for i in range(NT)]
    out_res = Res("out")
    gsc_res = Res("gsc")

    with ExitStack() as st:
        sb = lambda n, s, d=F32: st.enter_context(nc.sbuf_tensor(n, list(s), d))
        big = sb("big", [128, 8, S], BF16)
        big_res = [[Res(f"big{k}_{g}") for g in range(NG)] for k in range(8)]
        ar = sb("arena", [128, 32768], F32)
        ARES = [Res(f"ar{i}") for i in range(128)]
        ps_all = st.enter_context(nc.psum_tensor("psall", [128, 8 * 512], F32))
        PSR = [Res(f"ps{i}") for i in range(8)]

        def psb(bank, nb=1, parts=128):
            return ps_all[0:parts, bank * 512:(bank + nb) * 512], PSR[bank:bank + nb]

        class Arena:
            def __init__(self):
                self.p = 0

            def reset(self, p=0):
                self.p = p

            def alloc(self, n, dtype=F32, parts=128):
                nf = n if dtype is F32 else (n + 1) // 2
                nf = (nf + 1) // 2 * 2
                off = self.p
                self.p += nf
                assert self.p <= 32768, self.p
                ap = ar[0:parts, off:off + nf]
                if dtype is BF16:
                    ap = ap.bitcast(BF16)[:, 0:n]
                else:
                    ap = ap[:, 0:n]
                return ap, ARES[off // 256:(off + nf - 1) // 256 + 1]

        A = Arena()

        identf = sb("identf", [128, 128])
        identb = sb("identb", [128, 128], BF16)
        onesb = sb("onesb", [128, 128], BF16)
        ones64 = sb("ones64", [64, 64])
        ones64r = sb("ones64r", [64, 64])
        prm128 = sb("prm128s", [128, 60])
        prm64 = sb("prm64s", [64, 168])
        nprm64 = sb("nprm64", [64, 32])
        lamv = sb("lamv", [128, L * 8])
        wup = sb("wups", [64, 256])
        aup = sb("aups", [64, 256])
        wua_r = Res("wua")
        maskT2p = sb("maskT2p", [64, 128])
        maskT2n = sb("maskT2n", [64, 128])
        maskLn = sb("maskLn", [64, 64])
        bblk = sb("bblk", [128, 4, 3, 128], BF16)
        cfar = sb("cfar", [128, 8])
        ss = sb("ss", [128, 3 * NT])
        cst = Res("consts")
        ss_res = [Res(f"ss{i}") for i in range(NT)]
        sst_res = [[Res(f"sst{a}_{n}") for n in range(4)] for a in range(2)]

        def dma(eng, out, in_, reads, writes, key):
            P.add(eng, lambda e: e.dma_start(out=out, in_=in_), reads, writes, dma=key)

        dma("sp", identf[:], cidf_t.ap(), [], [cst], "c0")
        dma("sp", prm128[:], prm128_t.ap(), [], [cst], "c0")
        dma("sp", prm64[:], prm64_t.ap(), [], [cst], "c0")
        lamrep, lamrep_r = A.alloc(L * 256)
        dma("sp", lamrep, lamrep_t.ap(), [], lamrep_r, "c0")
        dma("sp", maskT2p[:], cm2_t.ap(), [], [cst], "c0")
        dma("sp", maskLn[:], cml_t.ap(), [], [cst], "c0")
        onehot, onehot_r = A.alloc(384, parts=64)
        relb, relb_rr = A.alloc(128, parts=64)
        gsb, gsb_rr = A.alloc(384, parts=4)
        relb_r = Res("relb")
        P.add("pool", lambda e: e.memset(onehot[:], 0.0), [], onehot_r)
        dma("sp", onehot[0:33, :], coh_t.ap(), [], onehot_r, "c0")
        P.add("pool", lambda e: e.memset(relb[:], 0.0), [], [relb_r] + relb_rr)
        dma("sp", relb[0:32, 0:4], relb_t.ap(), [], [relb_r], "c1")
        cst2 = Res("consts2")
        P.add("dve", lambda e: e.tensor_copy(out=identb[:], in_=identf[:]), [cst], [cst2])
        P.add("pool", lambda e: e.memset(onesb[:], 1.0), [], [cst2])
        P.add("pool", lambda e: e.memset(ones64[:], 1.0 / 64), [], [cst2])
        P.add("pool", lambda e: e.memset(ones64r[:], 1.0), [], [cst2])
        P.add("dve", lambda e: e.tensor_scalar(out=maskT2n[:], in0=maskT2p[:], scalar1=-1.0, scalar2=None, op0=ALU.mult), [cst], [cst2])
        P.add("dve", lambda e: e.tensor_scalar(out=maskLn[:], in0=maskLn[:], scalar1=-1.0, scalar2=None, op0=ALU.mult), [cst], [cst2])
        P.add("dve", lambda e: e.tensor_scalar(out=nprm64[:], in0=prm64[:, 56:88], scalar1=-1.0, scalar2=None, op0=ALU.mult), [cst], [cst2])
        P.add("pool", lambda e: e.memset(relb[32:33, 0:4], NEG8 / 8.0), [relb_r], [relb_r])

        lam_r = Res("lam")
        for l in range(depth):
            lq = lamrep[:, l * 256:(l + 1) * 256]
            tmpa, tmpr = A.alloc(128)
            tmpr = tmpr + lamrep_r
            for pr in range(2):
                P.add("dve", (lambda pr=pr, lq=lq, tmpa=tmpa: lambda e: e.tensor_tensor(out=tmpa[:, pr * 64:(pr + 1) * 64], in0=lq[:, (2 * pr) * 64:(2 * pr + 1) * 64], in1=lq[:, (2 * pr + 1) * 64:(2 * pr + 2) * 64], op=ALU.mult))(), [cst], tmpr)
                P.add("dve", (lambda pr=pr, l=l, tmpa=tmpa: lambda e: e.reduce_sum(out=lamv[:, l * 8 + pr:l * 8 + pr + 1], in_=tmpa[:, pr * 64:(pr + 1) * 64], axis=AX.X))(), tmpr, [lam_r])
            P.add("act", (lambda l=l: lambda e: e.activation(out=lamv[:, l * 8 + 2:l * 8 + 4], in_=lamv[:, l * 8:l * 8 + 2], func=AF.Exp))(), [lam_r], [lam_r])
            linit = 0.8 - 0.6 * math.exp(-0.3 * l)
            P.add("dve", (lambda l=l: lambda e: e.tensor_tensor(out=lamv[:, l * 8 + 4:l * 8 + 5], in0=lamv[:, l * 8 + 2:l * 8 + 3], in1=lamv[:, l * 8 + 3:l * 8 + 4], op=ALU.subtract))(), [lam_r], [lam_r])
            P.add("dve", (lambda l=l, linit=linit: lambda e: e.tensor_scalar(out=lamv[:, l * 8 + 5:l * 8 + 6], in0=lamv[:, l * 8 + 4:l * 8 + 5], scalar1=linit, scalar2=-1.0, op0=ALU.add, op1=ALU.mult))(), [lam_r], [lam_r])

        (pg, pgr) = psb(0)
        P.add("pe", lambda e: e.matmul(pg[:, 0:384], relb[:, :], onehot[:, :], start=True, stop=True), [relb_r] + onehot_r, pgr)
        gsb_r = Res("gsb")
        P.add("dve", lambda e: e.tensor_copy(out=gsb[:], in_=pg[0:4, 0:384]), pgr, [gsb_r] + gsb_rr)
        dma("sp", gsc_t.ap().rearrange("h (r n) -> h r n", n=384), gsb.unsqueeze(1).broadcast_to([4, 130, 384]), [gsb_r], [gsc_res], "c2")
        tdo, tdo_r = A.alloc(4 * 2 * 128)
        tdo4 = tdo.rearrange("p (h a q) -> p h a q", h=4, a=2)
        for h in range(4):
            for a_, base in ((0, 128), (1, 256)):
                dma("sp", tdo4[:, h, a_, :], bass.AP(gsc_t, h * 130 * 384 + base, [[383, 128], [1, 128]]), [gsc_res], tdo_r, "c3")
            dma("sp", cfar[:, h:h + 1], bass.AP(gsc_t, h * 130 * 384 + 383, [[0, 128], [1, 1]]), [gsc_res], [cst2], "c3")
        P.add("dve", lambda e: e.tensor_scalar(out=cfar[:, 4:8], in0=cfar[:, 0:4], scalar1=SCALE, scalar2=None, op0=ALU.mult), [cst2], [cst2])
        zt, zt_r = A.alloc(128)
        P.add("pool", lambda e: e.memset(zt[:], 0.0), [], zt_r)
        bias_r = Res("biasT")
        for h in range(4):
            P.add("dve", (lambda h=h: lambda e: e.tensor_copy(out=bblk[:, h, 0, :], in_=tdo4[:, h, 0, :]))(), tdo_r, [bias_r])
            P.add("pool", (lambda h=h: lambda e: e.tensor_copy(out=bblk[:, h, 1, :], in_=tdo4[:, h, 1, :]))(), tdo_r, [bias_r])
            P.add("dve", (lambda h=h: lambda e: e.tensor_scalar(out=bblk[:, h, 2, :], in0=zt[:], scalar1=cfar[:, h:h + 1], scalar2=None, op0=ALU.add))(), zt_r + [cst2], [bias_r])

        def TT(eng, out, a, b, op, rd, wr):
            P.add(eng, lambda e: e.tensor_tensor(out=out, in0=a, in1=b, op=op), rd, wr)

        def TS(eng, out, a, s1, s2, op0, op1, rd, wr):
            if s2 is None:
                P.add(eng, lambda e: e.tensor_scalar(out=out, in0=a, scalar1=s1, scalar2=None, op0=op0), rd, wr)
            else:
                P.add(eng, lambda e: e.tensor_scalar(out=out, in0=a, scalar1=s1, scalar2=s2, op0=op0, op1=op1), rd, wr)

        def STT(eng, out, a, sc, b, op0, op1, rd, wr):
            P.add(eng, lambda e: e.scalar_tensor_tensor(out=out, in0=a, scalar=sc, in1=b, op0=op0, op1=op1), rd, wr)

        def ACTF(out, in_, func, rd, wr, scale=1.0, bias=None):
            if bias is None:
                P.add("act", lambda e: e.activation(out=out, in_=in_, func=func, scale=scale), rd, wr)
            else:
                P.add("act", lambda e: e.activation(out=out, in_=in_, func=func, scale=scale, bias=bias), rd, wr)

        def MM(out, lhsT, rhs, st_, sp_, rd, wr):
            P.add("pe", lambda e: e.matmul(out, lhsT, rhs, start=st_, stop=sp_), rd, wr)

        def CP(eng, out, in_, rd, wr):
            if eng == "act":
                P.add("act", lambda e: e.activation(out=out, in_=in_, func=AF.Copy), rd, wr)
            else:
                P.add(eng, lambda e: e.tensor_copy(out=out, in_=in_), rd, wr)

        def RCP(out, in_, rd, wr):
            P.add("dve", lambda e: e.reciprocal(out=out, in_=in_), rd, wr)

        V = lambda buf, w: buf[0].rearrange("c (p t) -> c p t", p=16)
        V16 = lambda ap: ap.rearrange("c (p t) -> c p t", p=16)
        V4 = lambda ap: ap.rearrange("c (h t) -> c h t", h=4)
        pbk_state = [0]

        def pbk(n):
            if pbk_state[0] + n > 8:
                pbk_state[0] = 0
            b = pbk_state[0]
            pbk_state[0] += n
            return psb(b, n, parts=64)

        def rwkv_phase(l):
            A.reset()
            al = lambda n, dt=F32: A.alloc(n, dt, parts=64)
            scanmask = al(1024)
            Sst, Sst_r0 = al(2048)
            Sst5 = Sst.rearrange("c (a n h v) -> c a n h v", a=2, n=4, h=4)
            gC = al(16)
            KR = al(2048)
            kc, vc = al(1024), al(1024)
            pv = [al(1024), al(1024)]
            X1, X2, X4, C1, T, kt, bt, RK = [al(1024) for _ in range(8)]
            wdc, adc, wdp, adp, th = [al(256) for _ in range(5)]
            UWrhs, NAbT, AkT, UW = [al(2048) for _ in range(4)]
            Vtok, Khtok, Bntok, NAkb = [al(1024) for _ in range(4)]
            KR0 = (KR[0][:, 0:1024], KR[1][0:4])
            KR1 = (KR[0][:, 1024:2048], KR[1][4:8])
            KR4 = KR[0].rearrange("c (q p t) -> c q p t", q=2, p=16)
            id64 = identf[0:64, 0:64]
            id_bc = id64.unsqueeze(1).broadcast_to([64, 16, 64])

            def bc(col):
                return prm64[:, col:col + 4].unsqueeze(2).broadcast_to([64, 4, 256])

            dma("sp", scanmask[0], csm_t.ap(), [], scanmask[1], "rw_c")
            dma("sp", wup[:], wup_t.ap()[l], [], [wua_r], "rw_c")
            dma("sp", aup[:], aup_t.ap()[l], [], [wua_r], "rw_c")
            P.add("pool", lambda e: e.memset(Sst5[:, 0, 0, :, :], 0.0), [], [sst_res[0][0]] + Sst_r0)

            def src(row0, c0, c1):
                return pt_d[row0:row0 + 256, c0:c1].rearrange("(h c) t -> c h t", c=64)

            for gi in range(16):
                par = gi % 2
                t0 = gi * 256
                curs = [KR1, kc, vc]
                for q in range(3):
                    row0 = 3072 + 256 * q
                    prs = [pt_res[row0 // 128], pt_res[row0 // 128 + 1]]
                    cur = curs[q]
                    pvb = pv[q % 2]
                    dma("sp", V4(cur[0]), src(row0, t0, t0 + 256), prs, cur[1], f"rw_c{q}")
                    if gi == 0:
                        P.add("pool", (lambda pvb=pvb: lambda e: e.memset(V4(pvb[0])[:, :, 0:1], 0.0))(), [], pvb[1])
                        dma("sp", V4(pvb[0])[:, :, 1:256], src(row0, 0, 255), prs, pvb[1], f"rw_p{q % 2}")
                    else:
                        dma("sp", V4(pvb[0]), src(row0, t0 - 1, t0 + 255), prs, pvb[1], f"rw_p{q % 2}")
                    eng = "pool" if q == 1 else "dve"
                    TT(eng, pvb[0], pvb[0], cur[0], ALU.subtract, pvb[1] + cur[1], pvb[1])
                    TT(eng, V4(pvb[0]), V4(pvb[0]), bc(l * 12 + q * 4), ALU.mult, pvb[1] + [cst], pvb[1])
                    TT(eng, cur[0], cur[0], pvb[0], ALU.add, pvb[1] + cur[1], cur[1])
                for (cur, prv, row0, mcol, wk) in ((wdc, wdp, 3840, 48 + l * 2, 0), (adc, adp, 3904, 48 + l * 2 + 1, 1)):
                    dma("sp", cur[0], pt_d[row0:row0 + 64, t0:t0 + 256], [pt_res[30]], cur[1], f"rw_wc{wk}")
                    if gi == 0:
                        P.add("pool", (lambda prv=prv: lambda e: e.memset(prv[0][:, 0:1], 0.0))(), [], prv[1])
                        dma("sp", prv[0][:, 1:256], pt_d[row0:row0 + 64, 0:255], [pt_res[30]], prv[1], f"rw_wp{wk}")
                    else:
                        dma("sp", prv[0], pt_d[row0:row0 + 64, t0 - 1:t0 + 255], [pt_res[30]], prv[1], f"rw_wp{wk}")
                    TT("pool", prv[0], prv[0], cur[0], ALU.subtract, prv[1] + cur[1], prv[1])
                    STT("pool", cur[0], prv[0], prm64[:, mcol:mcol + 1], cur[0], ALU.mult, ALU.add, prv[1] + cur[1] + [cst], cur[1])
                ACTF(th[0], wdc[0], AF.Exp, wdc[1], th[1], scale=2.0)
                TS("pool", th[0], th[0], 1.0, None, ALU.add, None, th[1], th[1])
                RCP(th[0], th[0], th[1], th[1])
                TS("dve", th[0], th[0], -2.0, 1.0, ALU.mult, ALU.add, th[1], th[1])
                ups, upr = pbk(2)
                for h in range(4):
                    MM(ups[:, h * 256:(h + 1) * 256], wup[:, 64 * h:64 * h + 64], th[0], True, True, th[1] + [wua_r], upr)
                for h in range(4):
                    ACTF(V4(X1[0])[:, h, :], ups[:, h * 256:(h + 1) * 256], AF.Exp, upr + [cst2], X1[1], scale=-1.0, bias=nprm64[:, l * 4 + h:l * 4 + h + 1])
                TS("pool", X1[0], X1[0], 1.0, None, ALU.add, None, X1[1], X1[1])
                RCP(X1[0], X1[0], X1[1], X1[1])
                TS("pool", X1[0], X1[0], -0.6065306597126334, None, ALU.mult, None, X1[1], X1[1])
                aps, apr = pbk(2)
                for h in range(4):
                    MM(aps[:, h * 256:(h + 1) * 256], aup[:, 64 * h:64 * h + 64], adc[0], True, True, adc[1] + [wua_r], apr)
                for h in range(4):
                    ACTF(V4(X2[0])[:, h, :], aps[:, h * 256:(h + 1) * 256], AF.Exp, apr + [cst2], X2[1], scale=-1.0, bias=nprm64[:, 16 + l * 4 + h:16 + l * 4 + h + 1])
                TS("pool", X2[0], X2[0], 1.0, None, ALU.add, None, X2[1], X2[1])
                RCP(X2[0], X2[0], X2[1], X2[1])
                TT("dve", V4(KR0[0]), V4(kc[0]), bc(88 + l * 4), ALU.mult, kc[1] + [cst], KR0[1])
                ACTF(X4[0], KR0[0], AF.Square, KR0[1], X4[1])
                sps, spr = pbk(2)
                for hf in range(2):
                    MM(sps[:, hf * 512:(hf + 1) * 512], ones64r[:], X4[0][:, hf * 512:(hf + 1) * 512], True, True, X4[1] + [cst2], [spr[hf]])
                TS("dve", X4[0], sps, 1e-24, None, ALU.max, None, spr, X4[1])
                ACTF(X4[0], X4[0], AF.Ln, X4[1], X4[1])
                ACTF(X4[0], X4[0], AF.Exp, X4[1], X4[1], scale=-0.5)
                TT("dve", KR0[0], KR0[0], X4[0], ALU.mult, KR0[1] + X4[1], KR0[1])
                STT("dve", V4(X4[0]), V4(X2[0]), -1.0, bc(104 + l * 4), ALU.add, ALU.mult, X2[1] + [cst], X4[1])
                STT("dve", kc[0], X4[0], 1.0, kc[0], ALU.add, ALU.mult, X4[1] + kc[1], kc[1])
                TT("pool", RK[0], KR1[0], kc[0], ALU.mult, KR1[1] + kc[1], RK[1])
                TT("pool", V4(RK[0]), V4(RK[0]), bc(120 + l * 4), ALU.mult, RK[1] + [cst], RK[1])
                TT("pool", X2[0], KR0[0], X2[0], ALU.mult, KR0[1] + X2[1], X2[1])
                P.add("dve", lambda e: e.tensor_tensor_scan(out=C1[0], data0=scanmask[0], data1=X1[0], initial=0.0, op0=ALU.mult, op1=ALU.add), scanmask[1] + X1[1], C1[1])
                ACTF(T[0], C1[0], AF.Exp, C1[1], T[1])
                CP("pool", gC[0], V16(T[0])[:, :, 63], T[1], gC[1])
                TT("dve", KR1[0], KR1[0], T[0], ALU.mult, KR1[1] + T[1], KR1[1])
                ACTF(T[0], C1[0], AF.Exp, C1[1], T[1], scale=-1.0)
                TT("dve", kt[0], kc[0], T[0], ALU.mult, kc[1] + T[1], kt[1])
                TT("pool", bt[0], X2[0], T[0], ALU.mult, X2[1] + T[1], bt[1])
                TT("pool", T[0], C1[0], X1[0], ALU.subtract, C1[1] + X1[1], T[1])
                ACTF(T[0], T[0], AF.Exp, T[1], T[1])
                TT("dve", KR0[0], KR0[0], T[0], ALU.mult, KR0[1] + T[1], KR0[1])
                TT("pool", V16(T[0]), V16(C1[0])[:, :, 63:64].broadcast_to([64, 16, 64]), V16(C1[0]), ALU.subtract, C1[1], T[1])
                ACTF(T[0], T[0], AF.Exp, T[1], T[1])
                TT("dve", kc[0], kc[0], T[0], ALU.mult, kc[1] + T[1], kc[1])
                STT("pool", X2[0], X2[0], -1.0, T[0], ALU.mult, ALU.mult, X2[1] + T[1], X2[1])

                for (srcb, dstv, dstr) in ((KR0, V(UWrhs, 128)[:, :, 64:128], UWrhs[1]), (vc, V16(Vtok[0]), Vtok[1]), (kc, V16(Khtok[0]), Khtok[1]), (X2, V16(Bntok[0]), Bntok[1])):
                    tp, tpr = pbk(2)
                    for p in range(16):
                        P.add("pe", (lambda tp=tp, p=p, srcb=srcb: lambda e: e.transpose(tp[:, p * 64:(p + 1) * 64], V16(srcb[0])[:, p, :], id64))(), srcb[1] + [cst], [tpr[p // 8]])
                    CP("act", dstv, V16(tp), tpr, dstr)
                for (lh, dst, msk) in ((bt, NAbT, maskT2n), (kt, AkT, maskT2p)):
                    ap_, apr_ = pbk(4)
                    for p in range(16):
                        MM(ap_[:, p * 128:(p + 1) * 128], V16(lh[0])[:, p, :], KR4[:, :, p, :], True, True, lh[1] + KR[1], [apr_[p // 4]])
                    TT("dve", V(dst, 128), ap_.rearrange("c (p t) -> c p t", p=16), msk[:].unsqueeze(1).broadcast_to([64, 16, 128]), ALU.mult, apr_ + [cst, cst2], dst[1])
                ap_, apr_ = pbk(2)
                for p in range(16):
                    MM(ap_[:, p * 64:(p + 1) * 64], KR4[:, 0, p, :], V16(bt[0])[:, p, :], True, True, bt[1] + KR0[1], [apr_[p // 8]])
                TT("dve", V16(NAkb[0]), V16(ap_), maskLn[:].unsqueeze(1).broadcast_to([64, 16, 64]), ALU.mult, apr_ + [cst2], NAkb[1])
                NAb3 = V(NAbT, 128)
                R = T
                TT("dve", V16(R[0]), NAb3[:, :, 0:64], id_bc, ALU.add, NAbT[1] + [cst], R[1])
                Pprev = (NAb3[:, :, 0:64], NAbT[1])
                PTprev = (V16(NAkb[0]), NAkb[1])
                Pbufs = [X1, X4]
                PTbufs = [C1, NAkb]
                for k in range(1, 6):
                    Pn = Pbufs[(k - 1) % 2]
                    PTn = PTbufs[(k - 1) % 2]
                    if k < 5:
                        pp, ppr = pbk(2)
                        for p in range(16):
                            MM(pp[:, p * 64:(p + 1) * 64], PTprev[0][:, p, :], Pprev[0][:, p, :], True, True, PTprev[1] + Pprev[1], [ppr[p // 8]])
                    pt2, pt2r = pbk(2)
                    for p in range(16):
                        MM(pt2[:, p * 64:(p + 1) * 64], Pprev[0][:, p, :], PTprev[0][:, p, :], True, True, PTprev[1] + Pprev[1], [pt2r[p // 8]])
                    if k < 5:
                        CP("act", Pn[0], pp, ppr, Pn[1])
                    CP("dve", PTn[0], pt2, pt2r, PTn[1])
                    rr, rrr = pbk(2)
                    for p in range(16):
                        MM(rr[:, p * 64:(p + 1) * 64], V16(PTn[0])[:, p, :], V16(R[0])[:, p, :], True, True, PTn[1] + R[1], [rrr[p // 8]])
                    TT("dve", R[0], rr, R[0], ALU.add, rrr + R[1], R[1])
                    Pprev = (V16(Pn[0]), Pn[1])
                    PTprev = (V16(PTn[0]), PTn[1])
                xp, xpr = pbk(2)
                for p in range(16):
                    MM(xp[:, p * 64:(p + 1) * 64], V(AkT, 128)[:, p, 0:64], V16(Vtok[0])[:, p, :], True, True, AkT[1] + Vtok[1], [xpr[p // 8]])
                CP("act", V(UWrhs, 128)[:, :, 0:64], V16(xp), xpr, UWrhs[1])
                up4, up4r = pbk(4)
                for p in range(16):
                    MM(up4[:, p * 128:(p + 1) * 128], V16(R[0])[:, p, :], V(UWrhs, 128)[:, p, :], True, True, R[1] + UWrhs[1], [up4r[p // 4]])
                CP("act", UW[0][:, 0:1024], up4[:, 0:1024], up4r[0:2], UW[1][0:4])
                CP("dve", UW[0][:, 1024:2048], up4[:, 1024:2048], up4r[2:4], UW[1][4:8])
                UW3 = V(UW, 128)
                Qs, PTs, Dg, GT, Ysb, zr, Gg = X1, X4, pv[0], pv[1], kt, bt, C1
                qp, qpr = pbk(2)
                for p in range(16):
                    MM(qp[:, p * 64:(p + 1) * 64], V16(Khtok[0])[:, p, :], V16(Vtok[0])[:, p, :], True, False, Khtok[1] + Vtok[1], [qpr[p // 8]])
                    MM(qp[:, p * 64:(p + 1) * 64], V16(Bntok[0])[:, p, :], UW3[:, p, 0:64], False, True, Bntok[1] + UW[1], [qpr[p // 8]])
                CP("act", Qs[0], qp, qpr, Qs[1])
                pp2, pp2r = pbk(2)
                for p in range(16):
                    MM(pp2[:, p * 64:(p + 1) * 64], UW3[:, p, 64:128], V16(Bntok[0])[:, p, :], True, True, Bntok[1] + UW[1], [pp2r[p // 8]])
                TT("pool", V16(Dg[0]), id_bc, gC[0].unsqueeze(2).broadcast_to([64, 16, 64]), ALU.mult, gC[1] + [cst], Dg[1])
                TT("dve", PTs[0], pp2, Dg[0], ALU.add, pp2r + Dg[1], PTs[1])
                gp, gpr = pbk(2)
                for p in range(16):
                    MM(gp[:, p * 64:(p + 1) * 64], UW3[:, p, 64:128], NAb3[:, p, 64:128], True, True, UW[1] + NAbT[1], [gpr[p // 8]])
                TT("dve", GT[0], gp, KR1[0], ALU.add, gpr + KR1[1], GT[1])
                Q4 = Qs[0].rearrange("c (h n v) -> c h n v", h=4, n=4)
                for n in range(4):
                    sp_, spr_ = pbk(1)
                    for h in range(4):
                        MM(sp_[:, h * 64:(h + 1) * 64], V16(PTs[0])[:, h * 4 + n, :], Sst5[:, par, n, h, :], True, True, PTs[1] + [sst_res[par][n]], spr_)
                    if n < 3:
                        dsts, dres = Sst5[:, par, n + 1, :, :], sst_res[par][n + 1]
                    else:
                        dsts, dres = Sst5[:, 1 - par, 0, :, :], sst_res[1 - par][0]
                    TT("dve", dsts, sp_[:, 0:256].rearrange("c (h v) -> c h v", h=4), Q4[:, :, n, :], ALU.add, spr_ + Qs[1], [dres])
                yp, ypr = pbk(2)
                for p in range(16):
                    h, n = p // 4, p % 4
                    MM(yp[:, p * 64:(p + 1) * 64], Sst5[:, par, n, h, :], V16(GT[0])[:, p, :], True, False, [sst_res[par][n]] + GT[1], [ypr[p // 8]])
                    MM(yp[:, p * 64:(p + 1) * 64], V16(Vtok[0])[:, p, :], V(AkT, 128)[:, p, 64:128], False, False, Vtok[1] + AkT[1], [ypr[p // 8]])
                    MM(yp[:, p * 64:(p + 1) * 64], UW3[:, p, 0:64], NAb3[:, p, 64:128], False, True, UW[1] + NAbT[1], [ypr[p // 8]])
                CP("act", Ysb[0], yp, ypr, Ysb[1])
                mp, mpr = pbk(2)
                for hf in range(2):
                    MM(mp[:, hf * 512:(hf + 1) * 512], ones64[:], Ysb[0][:, hf * 512:(hf + 1) * 512], True, True, Ysb[1] + [cst2], [mpr[hf]])
                TT("dve", Ysb[0], Ysb[0], mp, ALU.subtract, Ysb[1] + mpr, Ysb[1])
                ACTF(T[0], Ysb[0], AF.Square, Ysb[1], T[1])
                vp_, vpr_ = pbk(2)
                for hf in range(2):
                    MM(vp_[:, hf * 512:(hf + 1) * 512], ones64[:], T[0][:, hf * 512:(hf + 1) * 512], True, True, T[1] + [cst2], [vpr_[hf]])
                ACTF(X4[0], vp_, AF.Ln, vpr_, X4[1], bias=GN_EPS)
                ACTF(X4[0], X4[0], AF.Exp, X4[1], X4[1], scale=-0.5)
                TT("dve", Ysb[0], Ysb[0], X4[0], ALU.mult, Ysb[1] + X4[1], Ysb[1])
                for h in range(4):
                    TS("pool" if h % 2 else "dve", V4(Ysb[0])[:, h, :], V4(Ysb[0])[:, h, :], prm64[:, 136 + l * 4 + h:137 + l * 4 + h], prm64[:, 152 + l * 4 + h:153 + l * 4 + h], ALU.mult, ALU.add, Ysb[1] + [cst], Ysb[1])
                bp, bpr = pbk(2)
                for hf in range(2):
                    MM(bp[:, hf * 512:(hf + 1) * 512], ones64r[:], RK[0][:, hf * 512:(hf + 1) * 512], True, True, RK[1] + [cst2], [bpr[hf]])
                TT("dve", T[0], bp, vc[0], ALU.mult, bpr + vc[1], T[1])
                TT("pool", Ysb[0], Ysb[0], T[0], ALU.add, Ysb[1] + T[1], Ysb[1])
                dma("sp", V4(zr[0]), src(3968, t0, t0 + 256), [pt_res[31], pt_res[32]], zr[1], "rw_z")
                ACTF(Gg[0], zr[0], AF.Exp, zr[1], Gg[1], scale=-1.0)
                TS("pool", Gg[0], Gg[0], 1.0, None, ALU.add, None, Gg[1], Gg[1])
                RCP(Gg[0], Gg[0], Gg[1], Gg[1])
                TT("pool", Gg[0], Gg[0], zr[0], ALU.mult, Gg[1] + zr[1], Gg[1])
                outb = (Dg[0].bitcast(BF16)[:, 0:1024], Dg[1])
                TT("dve", outb[0], Ysb[0], Gg[0], ALU.mult, Ysb[1] + Gg[1], outb[1])
                for h in range(4):
                    dma("sp", big[64 * (h % 2):64 * (h % 2) + 64, 6 + h // 2, t0:t0 + 256], V4(outb[0])[:, h, :], outb[1], [big_res[6 + h // 2][gi // 2]], "rw_o")


        def emit_layer(l):
            src_d = x_d if l == 0 else xs_d
            if stop_after == 'setup':
                return
            A.reset()
            xt = [A.alloc(D) for _ in range(3)]
            hb = [A.alloc(D, BF16) for _ in range(2)]
            junk = A.alloc(D, BF16)
            for i in range(NT):
                xa, xr = xt[i % 3]
                ha, hr = hb[i % 2]
                dma("sp", xa, src_d[i * 128:(i + 1) * 128, :], [xs_res[i]] if l > 0 else [], xr, f"xt{i % 3}")
                P.add("act", (lambda xa=xa, i=i: lambda e: e.activation(out=junk[0], in_=xa, func=AF.Square, accum_out=ss[:, 3 * i:3 * i + 1]))(), xr, junk[1] + [ss_res[i]])
                P.add("act", (lambda i=i: lambda e: e.activation(out=ss[:, 3 * i + 1:3 * i + 2], in_=ss[:, 3 * i:3 * i + 1], func=AF.Ln, scale=1.0 / D, bias=NORM_EPS))(), [ss_res[i]], [ss_res[i]])
                P.add("act", (lambda i=i: lambda e: e.activation(out=ss[:, 3 * i + 2:3 * i + 3], in_=ss[:, 3 * i + 1:3 * i + 2], func=AF.Exp, scale=-0.5))(), [ss_res[i]], [ss_res[i]])
                P.add("dve", (lambda xa=xa, ha=ha, i=i: lambda e: e.tensor_scalar(out=ha, in0=xa, scalar1=ss[:, 3 * i + 2:3 * i + 3], scalar2=None, op0=ALU.mult))(), xr + [ss_res[i]], hr)
                bank = i % 2
                pa, pr_ = psb(bank)
                pab = pa.bitcast(BF16)
                for k in range(8):
                    P.add("pe", (lambda pab=pab, ha=ha, k=k: lambda e: e.transpose(pab[:, k * 128:(k + 1) * 128], ha[:, k * 128:(k + 1) * 128], identb[:]))(), hr + [cst2], pr_)
                eng = "act" if i % 2 == 0 else "dve"
                dst = big[:, :, i * 128:(i + 1) * 128]
                srcv = pab.rearrange("p (k t) -> p k t", k=8)
                wr = [big_res[k][i // 4] for k in range(8)]
                if eng == "act":
                    P.add("act", (lambda dst=dst, srcv=srcv: lambda e: e.activation(out=dst, in_=srcv, func=AF.Copy))(), pr_, wr)
                else:
                    P.add("dve", (lambda dst=dst, srcv=srcv: lambda e: e.tensor_copy(out=dst, in_=srcv))(), pr_, wr)

            if stop_after == 'A':
                return
            A.reset()
            wst = [A.alloc(8 * 512) for _ in range(2)]
            wbf = [A.alloc(8 * 512, BF16) for _ in range(2)]
            stage = [A.alloc(S) for _ in range(2)]
            vst = [A.alloc(512, BF16) for _ in range(2)]
            allbig = [big_res[k][g] for k in range(8) for g in range(NG)]

            def load_w(r):
                width = 512 if r < 8 else 128
                wa, wr_ = wst[r % 2]
                wb, wbr = wbf[r % 2]
                wa3 = wa.rearrange("p (k c) -> p k c", k=8)
                wb3 = wb.rearrange("p (k c) -> p k c", k=8)
                if "nowload" not in DBG:
                    dma("sp", wa3[:, :, 0:width], win_d[l, :, r * 512:r * 512 + width].rearrange("(k p) c -> p k c", p=128), [], wr_, f"wst{r % 2}")
                else:
                    P.add("dve", lambda e: e.memset(wa3[:, :, 0:width], 0.5), [], wr_)
                for k in range(8):
                    P.add("pool", (lambda wa3=wa3, wb3=wb3, k=k, width=width: lambda e: e.tensor_scalar(out=wb3[:, k, 0:width], in0=wa3[:, k, 0:width], scalar1=prm128[:, l * 8 + k:l * 8 + k + 1], scalar2=None, op0=ALU.mult))(), wr_ + [cst], wbr)

            load_w(0)
            pbank = 0
            ev = 0
            for r in range(9):
                if r + 1 < 9:
                    load_w(r + 1)
                width = 512 if r < 8 else 128
                wb, wbr = wbf[r % 2]
                wb3 = wb.rearrange("p (k c) -> p k c", k=8)
                if r == 2:
                    for i in range(NT):
                        pa, pr_ = psb(4 + pbank % 4)
                        pbank += 1
                        for k in range(8):
                            P.add("pe", (lambda pa=pa, k=k, i=i, wb3=wb3: lambda e: e.matmul(pa, big[:, k, i * 128:(i + 1) * 128], wb3[:, k, :], start=(k == 0), stop=(k == 7)))(), [big_res[k][i // 4], ] + wbr, pr_)
                        va, vr = vst[i % 2]
                        eng = "act" if ev % 2 == 0 else "dve"
                        ev += 1
                        if eng == "act":
                            P.add("act", (lambda va=va, pa=pa: lambda e: e.activation(out=va, in_=pa, func=AF.Copy))(), pr_, vr)
                        else:
                            P.add("dve", (lambda va=va, pa=pa: lambda e: e.tensor_copy(out=va, in_=pa))(), pr_, vr)
                        if "nostore" not in DBG:
                            dma("sp", vtok_d[i * 128:(i + 1) * 128, :], va, vr, [vtok_res[i]], f"vst{i % 2}")
                    continue
                for jj in range(width // 128):
                    j = 4 * r + jj
                    sa, sr = stage[j % 2]
                    for tg in range(NG):
                        pa, pr_ = psb(4 + pbank % 4)
                        pbank += 1
                        for k in range(8):
                            P.add("pe", (lambda pa=pa, k=k, tg=tg, jj=jj, wb3=wb3: lambda e: e.matmul(pa, wb3[:, k, jj * 128:(jj + 1) * 128], big[:, k, tg * 512:(tg + 1) * 512], start=(k == 0), stop=(k == 7)))(), [big_res[k][tg]] + wbr, pr_)
                        eng = "act" if ev % 2 == 0 else "dve"
                        ev += 1
                        sres = sr[tg * 2:(tg + 1) * 2]
                        if eng == "act":
                            P.add("act", (lambda sa=sa, pa=pa, tg=tg: lambda e: e.activation(out=sa[:, tg * 512:(tg + 1) * 512], in_=pa, func=AF.Copy))(), pr_, sres)
                        else:
                            P.add("dve", (lambda sa=sa, pa=pa, tg=tg: lambda e: e.tensor_copy(out=sa[:, tg * 512:(tg + 1) * 512], in_=pa))(), pr_, sres)
                        if "nostore" not in DBG and "nopt" not in DBG:
                            dma("sp", pt_d[j * 128:(j + 1) * 128, tg * 512:(tg + 1) * 512], sa[:, tg * 512:(tg + 1) * 512], sres, [pt_res[j]], f"stg{j % 2}_{tg}")

            if l == tap_layer and "pt" in taps:
                dma("sp", tap_t["pt"].ap()[0:1024, :], pt_d[0:1024, :], pt_res, [out_res], "tap")
                dma("sp", tap_t["pt"].ap()[1536:INC, :], pt_d[1536:INC, :], pt_res, [out_res], "tap")

            if stop_after == 'B':
                return
            A.reset()
            qkf = A.alloc(S)
            qb = [A.alloc(S, BF16) for _ in range(2)]
            kb = [A.alloc(S, BF16) for _ in range(2)]
            vh = [A.alloc(NT * 128, BF16) for _ in range(2)]
            zf = [A.alloc(512) for _ in range(2)]
            ob = [A.alloc(4 * 512) for _ in range(2)]
            ptb = [A.alloc(2 * 512, BF16) for _ in range(3)]
            r01 = A.alloc(2 * 512)
            o01 = A.alloc(2 * 512)
            osb = A.alloc(512)
            sqb = A.alloc(512, BF16)
            rsd = A.alloc(512)
            gat = A.alloc(512)
            linit = 0.8 - 0.6 * math.exp(-0.3 * l)
            pt_cnt = [0]
            st_cnt = [0]
            pend = [None]
            zi = 0
            for h in range(4):
                hs = h % 2
                for c4 in range(4):
                    dma("sp", qkf[0][:, c4 * 1024:(c4 + 1) * 1024], pt_d[h * 128:(h + 1) * 128, c4 * 1024:(c4 + 1) * 1024], [pt_res[h]], qkf[1], "qkf")
                P.add("dve", (lambda hs=hs: lambda e: e.tensor_copy(out=qb[hs][0], in_=qkf[0]))(), qkf[1], qb[hs][1])
                for c4 in range(4):
                    dma("sp", qkf[0][:, c4 * 1024:(c4 + 1) * 1024], pt_d[512 + h * 128:512 + (h + 1) * 128, c4 * 1024:(c4 + 1) * 1024], [pt_res[4 + h]], qkf[1], "qkf")
                P.add("pool", (lambda hs=hs: lambda e: e.tensor_copy(out=kb[hs][0], in_=qkf[0]))(), qkf[1], kb[hs][1])
                vh3 = vh[hs][0].rearrange("p (n d) -> p n d", n=NT)
                for c4 in range(4):
                    dma("sp", vh3[:, c4 * 8:(c4 + 1) * 8, :], vtok_d[c4 * 1024:(c4 + 1) * 1024, h * 128:(h + 1) * 128].rearrange("(n p) d -> p n d", p=128), vtok_res, vh[hs][1], f"vh{hs}")
                qbh, kbh = qb[hs][0], kb[hs][0]
                for g in range(NG):
                    nk = 4 * g + 4
                    zsl, zsr = zf[zi % 2]
                    dma("sp", zsl, pt_d[1536 + h * 128:1536 + (h + 1) * 128, g * 512:(g + 1) * 512], [pt_res[12 + h]], zsr, f"zf{zi % 2}")
                    zi += 1
                    accs = [psb(4 + a_) for a_ in range(4)]

                    def emit_qk(kt, g=g, h=h, kbh=kbh, qbh=qbh, hs=hs):
                        sbank = (st_cnt[0] % 2) * 2
                        st_cnt[0] += 1
                        s2, s2r = psb(sbank, 2)
                        r = kt - 4 * g
                        diag = r >= -1
                        c0 = 128 * max(r, 0)
                        for m in range(2):
                            P.add("pe", (lambda s2=s2, m=m, kt=kt, diag=diag, c0=c0: lambda e: e.matmul(s2[:, m * 512 + c0:(m + 1) * 512], kbh[m * 64:(m + 1) * 64, kt * 128:(kt + 1) * 128], qbh[m * 64:(m + 1) * 64, g * 512 + c0:(g + 1) * 512], start=True, stop=not diag))(), qb[hs][1] + kb[hs][1], [s2r[m]])
                            if diag:
                                for s_ in range(max(r, 0), 4):
                                    dlt = s_ - r
                                    bi = 0 if dlt == 0 else (1 if dlt == 1 else 2)
                                    P.add("pe", (lambda s2=s2, m=m, s_=s_, bi=bi: lambda e: e.matmul(s2[:, m * 512 + s_ * 128:m * 512 + (s_ + 1) * 128], identb[:], bblk[:, h, bi, :], start=False, stop=(s_ == 3)))(), [cst2, bias_r], [s2r[m]])
                        return (s2, s2r, diag, c0)

                    def emit_rest(kt, qk, nk=nk, h=h, accs=accs, vh3=vh3, hs=hs):
                        s2, s2r, diag, c0 = qk
                        pa_, par = ptb[pt_cnt[0] % 3]
                        pt_cnt[0] += 1
                        s23 = s2.rearrange("p (m q) -> p m q", m=2)[:, :, c0:512]
                        pa3 = pa_.rearrange("p (m q) -> p m q", m=2)[:, :, c0:512]
                        if diag:
                            P.add("act", (lambda: lambda e: e.activation(out=pa3, in_=s23, func=AF.Exp, scale=SCALE))(), s2r, par)
                        else:
                            P.add("act", (lambda: lambda e: e.activation(out=pa3, in_=s23, func=AF.Exp, scale=SCALE, bias=cfar[:, 4 + h:5 + h]))(), s2r + [cst2], par)
                        for m in range(2):
                            P.add("pe", (lambda m=m, acc=accs[m][0]: lambda e: e.matmul(acc[:, c0:512], vh3[:, kt, :], pa_[:, m * 512 + c0:(m + 1) * 512], start=(kt == 0), stop=(kt == nk - 1)))(), par + vh[hs][1], accs[m][1])
                            P.add("pe", (lambda m=m, acc=accs[2 + m][0]: lambda e: e.matmul(acc[:, c0:512], onesb[:], pa_[:, m * 512 + c0:(m + 1) * 512], start=(kt == 0), stop=(kt == nk - 1)))(), par + [cst2], accs[2 + m][1])

                    nxt = emit_qk(0)
                    for kt in range(nk):
                        cur = nxt
                        if kt + 1 < nk:
                            nxt = emit_qk(kt + 1)
                        emit_rest(kt, cur)
                        if kt == 2 and pend[0] is not None:
                            pend[0]()
                            pend[0] = None
                    oba, obr = ob[g % 2]
                    acc4, acc4r = psb(4, 4)
                    P.add("act", (lambda oba=oba, acc4=acc4: lambda e: e.activation(out=oba[:, 0:1024], in_=acc4[:, 0:1024], func=AF.Copy))(), acc4r[0:2], obr)
                    P.add("dve", (lambda oba=oba, acc4=acc4: lambda e: e.tensor_copy(out=oba[:, 1024:2048], in_=acc4[:, 1024:2048]))(), acc4r[2:4], obr)
                    P.add("dve", (lambda oba=oba: lambda e: e.reciprocal(out=r01[0], in_=oba[:, 1024:2048]))(), obr, r01[1])
                    P.add("dve", (lambda oba=oba: lambda e: e.tensor_tensor(out=o01[0], in0=oba[:, 0:1024], in1=r01[0], op=ALU.mult))(), obr + r01[1], o01[1])
                    P.add("dve", (lambda: lambda e: e.scalar_tensor_tensor(out=osb[0], in0=o01[0][:, 512:1024], scalar=lamv[:, l * 8 + 5:l * 8 + 6], in1=o01[0][:, 0:512], op0=ALU.mult, op1=ALU.add))(), o01[1] + [lam_r], osb[1])

                    def tail(h=h, g=g, zsl=zsl, zsr=zsr):
                        P.add("act", (lambda: lambda e: e.activation(out=sqb[0], in_=osb[0], func=AF.Square))(), osb[1], sqb[1])
                        P.add("act", (lambda: lambda e: e.activation(out=gat[0], in_=zsl, func=AF.Exp, scale=-1.0))(), zsr, gat[1])
                        sq_ps, sq_r = psb(0)
                        P.add("pe", (lambda: lambda e: e.matmul(sq_ps, onesb[:], sqb[0], start=True, stop=True))(), sqb[1] + [cst2], sq_r)
                        P.add("act", (lambda: lambda e: e.activation(out=rsd[0], in_=sq_ps, func=AF.Ln, scale=1.0 / 128, bias=SUBLN_EPS))(), sq_r, rsd[1])
                        P.add("act", (lambda: lambda e: e.activation(out=rsd[0], in_=rsd[0], func=AF.Exp, scale=-0.5))(), rsd[1], rsd[1])
                        P.add("pool", (lambda: lambda e: e.tensor_scalar(out=gat[0], in0=gat[0], scalar1=1.0, scalar2=None, op0=ALU.add))(), gat[1], gat[1])
                        P.add("dve", (lambda: lambda e: e.reciprocal(out=gat[0], in_=gat[0]))(), gat[1], gat[1])
                        P.add("pool", (lambda: lambda e: e.tensor_tensor(out=gat[0], in0=gat[0], in1=zsl, op=ALU.mult))(), gat[1] + zsr, gat[1])
                        P.add("dve", (lambda: lambda e: e.tensor_tensor(out=osb[0], in0=osb[0], in1=rsd[0], op=ALU.mult))(), osb[1] + rsd[1], osb[1])
                        P.add("dve", (lambda: lambda e: e.tensor_scalar(out=osb[0], in0=osb[0], scalar1=prm128[:, 32 + l:33 + l], scalar2=(1.0 - linit), op0=ALU.mult, op1=ALU.mult))(), osb[1] + [cst], osb[1])
                        P.add("dve", (lambda: lambda e: e.tensor_tensor(out=big[:, h, g * 512:(g + 1) * 512], in0=osb[0], in1=gat[0], op=ALU.mult))(), osb[1] + gat[1], [big_res[h][g]])

                    pend[0] = tail
            if pend[0] is not None:
                pend[0]()
                pend[0] = None

            if stop_after == 'C':
                return
            A.reset()
            cbuf = [[A.alloc(516) for _ in range(4)] for _ in range(2)]
            cw = lambda k, j: prm128[:, 36 + l * 6 + k * 2 + j:36 + l * 6 + k * 2 + j + 1]
            it = 0
            for j in range(2):
                for g in range(NG):
                    bs = cbuf[it % 2]
                    it += 1
                    (cba, cbr), (cca, ccr), (cha, chr_), (cza, czr) = bs
                    t0 = g * 512
                    lo = 2 if g > 0 else 0
                    dma("sp", cca[:, 2 - lo:514], pt_d[(18 + j) * 128:(19 + j) * 128, t0 - lo:t0 + 512], [pt_res[18 + j]], ccr, f"cv{it % 2}")
                    dma("sp", cha[:, 2 - lo:514], pt_d[(20 + j) * 128:(21 + j) * 128, t0 - lo:t0 + 512], [pt_res[20 + j]], chr_, f"cv{it % 2}")
                    dma("sp", cba[:, 0:512], pt_d[(16 + j) * 128:(17 + j) * 128, t0:t0 + 512], [pt_res[16 + j]], cbr, f"cv{it % 2}")
                    dma("sp", cza[:, 0:512], pt_d[(22 + j) * 128:(23 + j) * 128, t0:t0 + 512], [pt_res[22 + j]], czr, f"cv{it % 2}")
                    if g == 0:
                        P.add("pool", (lambda cca=cca: lambda e: e.memset(cca[:, 0:2], 0.0))(), [], ccr)
                        P.add("pool", (lambda cha=cha: lambda e: e.memset(cha[:, 0:2], 0.0))(), [], chr_)
                    P.add("pool", (lambda cca=cca, cha=cha: lambda e: e.tensor_tensor(out=cca[:, 0:514], in0=cca[:, 0:514], in1=cha[:, 0:514], op=ALU.mult))(), ccr + chr_, ccr)
                    P.add("dve", (lambda cca=cca, cha=cha, j=j: lambda e: e.tensor_scalar(out=cha[:, 0:512], in0=cca[:, 0:512], scalar1=cw(0, j), scalar2=None, op0=ALU.mult))(), ccr + [cst], chr_)
                    P.add("dve", (lambda cca=cca, cha=cha, j=j: lambda e: e.scalar_tensor_tensor(out=cha[:, 0:512], in0=cca[:, 1:513], scalar=cw(1, j), in1=cha[:, 0:512], op0=ALU.mult, op1=ALU.add))(), ccr + chr_ + [cst], chr_)
                    P.add("dve", (lambda cca=cca, cha=cha, j=j: lambda e: e.scalar_tensor_tensor(out=cha[:, 0:512], in0=cca[:, 2:514], scalar=cw(2, j), in1=cha[:, 0:512], op0=ALU.mult, op1=ALU.add))(), ccr + chr_ + [cst], chr_)
                    P.add("act", (lambda cza=cza, cca=cca: lambda e: e.activation(out=cca[:, 0:512], in_=cza[:, 0:512], func=AF.Exp, scale=-1.0))(), czr + ccr, ccr)
                    P.add("pool", (lambda cca=cca: lambda e: e.tensor_scalar(out=cca[:, 0:512], in0=cca[:, 0:512], scalar1=1.0, scalar2=None, op0=ALU.add))(), ccr, ccr)
                    P.add("dve", (lambda cca=cca: lambda e: e.reciprocal(out=cca[:, 0:512], in_=cca[:, 0:512]))(), ccr, ccr)
                    P.add("pool", (lambda cca=cca, cza=cza: lambda e: e.tensor_tensor(out=cca[:, 0:512], in0=cca[:, 0:512], in1=cza[:, 0:512], op=ALU.mult))(), ccr + czr, ccr)
                    P.add("pool", (lambda cha=cha, cba=cba: lambda e: e.tensor_tensor(out=cha[:, 0:512], in0=cha[:, 0:512], in1=cba[:, 0:512], op=ALU.mult))(), chr_ + cbr, chr_)
                    P.add("dve", (lambda cha=cha, cca=cca, j=j, g=g: lambda e: e.tensor_tensor(out=big[:, 4 + j, g * 512:(g + 1) * 512], in0=cha[:, 0:512], in1=cca[:, 0:512], op=ALU.mult))(), chr_ + ccr, [big_res[4 + j][g]])

            if stop_after == 'D':
                return
            if do_rwkv:
                rwkv_phase(l)
            else:
                for k in (6, 7):
                    for g in range(NG):
                        P.add("pool", (lambda k=k, g=g: lambda e: e.memset(big[:, k, g * 512:(g + 1) * 512], 0.0))(), [], [big_res[k][g]])

            if l == tap_layer and "mixed" in taps:
                dma("sp", tap_t["mixed"].ap().rearrange("(k p) t -> p k t", p=128), big[:], allbig, [out_res], "tap")

            if stop_after == 'E':
                return
            A.reset()
            wo_st = [A.alloc(D) for _ in range(2)]
            wo = A.alloc(8 * D, BF16)
            wo3 = wo[0].rearrange("p (k d) -> p k d", k=8)
            xin = [A.alloc(D) for _ in range(2)]
            xo = [A.alloc(D) for _ in range(2)]
            for k in range(8):
                wa, wr_ = wo_st[k % 2]
                dma("sp", wa, wout_d[l, k * 128:(k + 1) * 128, :], [], wr_, f"wo{k % 2}")
                P.add("pool" if k % 2 else "dve", (lambda wa=wa, k=k: lambda e: e.tensor_copy(out=wo3[:, k, :], in_=wa))(), wr_, wo[1])
            last = (l == depth - 1)
            if last:
                fga, fgr = A.alloc(D)
                dma("sp", fga, fg_t.ap(), [], fgr, "fgl")
            for i in range(NT):
                xa, xr = xin[i % 2]
                ya, yr = xo[i % 2]
                dma("sp", xa, src_d[i * 128:(i + 1) * 128, :], [xs_res[i]] if l > 0 else [], xr, f"xin{i % 2}")
                p2, p2r = psb((i % 2) * 2, 2)
                for half in range(2):
                    for k in range(8):
                        P.add("pe", (lambda p2=p2, half=half, k=k, i=i: lambda e: e.matmul(p2[:, half * 512:(half + 1) * 512], big[:, k, i * 128:(i + 1) * 128], wo3[:, k, half * 512:(half + 1) * 512], start=(k == 0), stop=(k == 7)))(), [big_res[k][i // 4]] + wo[1], [p2r[half]])
                P.add("dve", (lambda ya=ya, p2=p2, xa=xa: lambda e: e.tensor_tensor(out=ya, in0=p2, in1=xa, op=ALU.add))(), p2r + xr, yr)
                if not last:
                    dma("sp", xs_d[i * 128:(i + 1) * 128, :], ya, yr, [xs_res[i]], f"xo{i % 2}")
                else:
                    if "xs" in taps:
                        dma("sp", tap_t["xs"].ap()[i * 128:(i + 1) * 128, :], ya, yr, [out_res], f"xo{i % 2}")
                    fr = Res(f"fin{i}")
                    P.add("act", (lambda ya=ya, xa=xa, i=i: lambda e: e.activation(out=xa, in_=ya, func=AF.Square, accum_out=ss[:, 3 * i:3 * i + 1]))(), yr, xr + [ss_res[i]])
                    P.add("act", (lambda i=i: lambda e: e.activation(out=ss[:, 3 * i + 1:3 * i + 2], in_=ss[:, 3 * i:3 * i + 1], func=AF.Ln, scale=1.0 / D, bias=NORM_EPS))(), [ss_res[i]], [ss_res[i]])
                    P.add("act", (lambda i=i: lambda e: e.activation(out=ss[:, 3 * i + 2:3 * i + 3], in_=ss[:, 3 * i + 1:3 * i + 2], func=AF.Exp, scale=-0.5))(), [ss_res[i]], [ss_res[i]])
                    P.add("dve", (lambda ya=ya, xa=xa, i=i: lambda e: e.scalar_tensor_tensor(out=xa, in0=ya, scalar=ss[:, 3 * i + 2:3 * i + 3], in1=fga, op0=ALU.mult, op1=ALU.mult))(), yr + [ss_res[i]] + fgr, xr)
                    dma("sp", out_d[i * 128:(i + 1) * 128, :], xa, xr, [out_res], f"xo{i % 2}")

        for l_ in range(depth):
            emit_layer(l_)
        P.add("sp", lambda e: e.nop(), [out_res], [])
        nsem = P.emit(nc, st)
    return nc, len(P.ops), nsem


_CACHE = {}


def kernel(**inputs):
    x = np.asarray(inputs["x"], np.float32)
    prm128, fg, lamrep, prm64 = host_params(inputs)
    consts = make_consts()
    if "nc" not in _CACHE:
        _CACHE["nc"] = build(L)[0]
    nc = _CACHE["nc"]
    shared = {
        "w_in": np.ascontiguousarray(np.asarray(inputs["w_in"], np.float32)),
        "w_out": np.ascontiguousarray(np.asarray(inputs["w_out"], np.float32)),
        "rel_bias": np.ascontiguousarray(np.asarray(inputs["rel_bias"], np.float32)),
        "w_up": np.ascontiguousarray(np.asarray(inputs["w_up"], np.float32)),
        "a_up": np.ascontiguousarray(np.asarray(inputs["a_up"], np.float32)),
        "prm128": prm128, "fg": fg, "lamrep": lamrep, "prm64": prm64,
    }
    shared.update(consts)
    in_maps = []
    for b in range(8):
        m = dict(shared)
        m["x"] = np.ascontiguousarray(x[b])
        in_maps.append(m)
    res = run_bass_kernel_spmd(nc, in_maps, core_ids=list(range(8)))
    return np.stack([np.asarray(r["out"], np.float32) for r in res.results], axis=0)
```

```python
import math
from contextlib import ExitStack

import numpy as np
import ml_dtypes

import concourse.bass as bass
import concourse.mybir as mybir
from concourse.bass_utils import run_bass_kernel_spmd

F32 = mybir.dt.float32
BF16 = mybir.dt.bfloat16
AF = mybir.ActivationFunctionType
ALU = mybir.AluOpType
AX = mybir.AxisListType

S = 4096
D = 1024
NT = 32
NG = 8
L = 4
INC = 4224
NEG8 = -240000.0
NORM_EPS = 1e-6
SUBLN_EPS = 1e-5
GN_EPS = 64e-5
SCALE = 0.125
DBG = set()
POOL_AS = "dve"
RWKV_POOL_AS = "dve"


class Res:
    __slots__ = ("name", "writer", "readers")

    def __init__(self, name):
        self.name = name
        self.writer = None
        self.readers = []


class Op:
    __slots__ = ("eng", "fn", "deps", "dma", "idx", "sig", "waits", "has_dep")


class Prog:
    def __init__(self):
        self.ops = []
        self.pool_as = POOL_AS

    def add(self, eng, fn, reads=(), writes=(), dma=None):
        if dma is None and eng == "pool" and self.pool_as:
            eng = self.pool_as
        op = Op()
        op.eng = eng
        op.fn = fn
        op.dma = dma
        op.idx = len(self.ops)
        op.deps = {}
        op.has_dep = False
        op.sig = None

        def dep(d, kind):
            if d is None or d is op:
                return
            if op.deps.get(d) != "raw":
                op.deps[d] = kind

        for r in reads:
            dep(r.writer, "raw")
        for w in writes:
            dep(w.writer, "waw")
            for rd in w.readers:
                dep(rd, "war")
        k = (op.eng, op.dma)
        for r in reads:
            r.readers = [x for x in r.readers if (x.eng, x.dma) != k]
            r.readers.append(op)
        for w in writes:
            w.writer = op
            w.readers = []
        self.ops.append(op)
        return op

    def finalize(self):
        for op in self.ops:
            keep = {}
            for d, kind in op.deps.items():
                if d.dma is None and op.dma is None and d.eng == op.eng:
                    if op.eng == "pe":
                        continue
                keep[d] = kind
            op.deps = keep
            for d in keep:
                d.has_dep = True
        cnt = {}
        waited = {}
        for op in self.ops:
            w = {}
            for d in op.deps:
                key = d.dma if d.dma else d.eng
                val = cnt[key] if d.dma else d.sig
                if w.get(key, 0) < val:
                    w[key] = val
            q = waited.setdefault(op.eng, {})
            op.waits = []
            for kk, v in w.items():
                if q.get(kk, 0) < v:
                    q[kk] = v
                    op.waits.append((kk, v))
            if op.dma:
                cnt[op.dma] = cnt.get(op.dma, 0) + 16
                op.sig = cnt[op.dma]
            elif op.has_dep:
                cnt[op.eng] = cnt.get(op.eng, 0) + 1
                op.sig = cnt[op.eng]
        self.cnt = cnt

    def emit(self, nc, st):
        self.finalize()
        keys = set()
        for op in self.ops:
            for kk, _ in op.waits:
                keys.add(kk)
            if op.dma:
                keys.add(op.dma)
            elif op.sig is not None:
                keys.add(op.eng)
        sems = {kk: st.enter_context(nc.semaphore("s_" + kk)) for kk in sorted(keys)}
        block = st.enter_context(nc.Block())
        ops = self.ops

        def run(name):
            def body(e):
                for op in ops:
                    if op.eng != name:
                        continue
                    for kk, v in op.waits:
                        e.wait_ge(sems[kk], v)
                    ins = op.fn(e)
                    if op.sig is not None:
                        ins.then_inc(sems[op.dma or op.eng], 16 if op.dma else 1)

            return body

        block.tensor(run("pe"))
        block.scalar(run("act"))
        block.vector(run("dve"))
        block.gpsimd(run("pool"))
        block.sync(run("sp"))
        return len(sems)


def _bucket(dist):
    n = np.maximum(dist, 0)
    max_exact = 16
    nf = np.maximum(n, 1).astype(np.float32)
    large = max_exact + (np.log(nf / max_exact) / math.log(128 / max_exact) * (32 - max_exact)).astype(np.int32)
    large = np.minimum(large, 31)
    return np.where(n < max_exact, n, large)


def _bucket_jax_exact():
    import jax
    import jax.numpy as jnp

    with jax.default_device(jax.devices("cpu")[0]):
        dist = jnp.arange(0, 256)
        n = jnp.maximum(dist, 0)
        nf = jnp.maximum(n, 1).astype(jnp.float32)
        large = 16 + (jnp.log(nf / 16) / math.log(128 / 16) * 16).astype(jnp.int32)
        large = jnp.minimum(large, 31)
        return np.asarray(jnp.where(n < 16, n, large))


def make_consts():
    c = {}
    c["c_identf"] = np.eye(128, dtype=np.float32)
    try:
        bk = _bucket_jax_exact()
    except Exception:
        bk = _bucket(np.arange(256))
    oh = np.zeros((33, 384), np.float32)
    for m in range(384):
        dist = m - 128
        if dist < 0:
            oh[32, m] = 8.0
        else:
            oh[bk[dist], m] = 8.0
    c["c_onehot8"] = oh
    j = np.arange(64)[:, None]
    t = np.arange(64)[None, :]
    strict = (j < t).astype(np.float32)
    incl = (j <= t).astype(np.float32)
    c["c_maskT2"] = np.tile(np.concatenate([strict, incl], axis=1), (2, 1))
    c["c_maskL"] = np.tile((t < j).astype(np.float32), (2, 1))
    sm = np.ones((128, 1024), np.float32)
    sm[:, ::64] = 0.0
    c["c_scanmask"] = sm
    return c


def host_params(inp):
    g = np.asarray(inp["norm_g"], np.float32)
    gT = g.reshape(L, 8, 128).transpose(2, 0, 1).reshape(128, L * 8)
    sublnT = np.asarray(inp["subln_g"], np.float32).T
    convT = np.asarray(inp["conv_w"], np.float32).reshape(L, 3, 2, 128).transpose(3, 0, 1, 2).reshape(128, L * 6)
    prm128 = np.ascontiguousarray(np.concatenate([gT, sublnT, convT], axis=1))
    fg = np.ascontiguousarray(np.broadcast_to(np.asarray(inp["final_norm_g"], np.float32)[None, :], (128, D)))
    lamrep = np.ascontiguousarray(np.broadcast_to(np.asarray(inp["lam_qk"], np.float32).reshape(1, L * 256), (128, L * 256)))
    mu = np.asarray(inp["rwkv_mu"], np.float32)
    mu_rkv = mu[:, :768].reshape(L, 3, 4, 64).transpose(3, 0, 1, 2).reshape(64, L * 12)
    mu_wa = mu[:, 768:896].reshape(L, 2, 64).transpose(2, 0, 1).reshape(64, L * 2)

    def ch(a):
        return np.asarray(a, np.float32).reshape(L, 4, 64).transpose(2, 0, 1).reshape(64, L * 4)

    prm64 = np.ascontiguousarray(np.concatenate(
        [mu_rkv, mu_wa, ch(inp["w0"]), ch(inp["a0"]), ch(inp["k_k"]), ch(inp["k_a"]),
         ch(np.asarray(inp["r_k"]).reshape(L, 256)), ch(inp["lnx_g"]), ch(inp["lnx_b"])], axis=1))
    prm64 = np.ascontiguousarray(np.tile(prm64, (2, 1)))
    return prm128, fg, lamrep, prm64


def build(depth=L, taps=(), do_rwkv=True, tap_layer=0, stop_after=None):
    nc = bass.Bass("TRN2", target_bir_lowering=False)
    P = Prog()
    dram_in = lambda n, s, d=F32: nc.dram_tensor(n, list(s), d, kind="ExternalInput")
    x_t = dram_in("x", [S, D])
    win_t = dram_in("w_in", [L, D, INC])
    wout_t = dram_in("w_out", [L, D, D])
    relb_t = dram_in("rel_bias", [32, 4])
    wup_t = dram_in("w_up", [L, 64, 256])
    aup_t = dram_in("a_up", [L, 64, 256])
    prm128_t = dram_in("prm128", [128, 60])
    fg_t = dram_in("fg", [128, D])
    lamrep_t = dram_in("lamrep", [128, L * 256])
    prm64_t = dram_in("prm64", [128, 168])
    cidf_t = dram_in("c_identf", [128, 128])
    coh_t = dram_in("c_onehot8", [33, 384])
    cm2_t = dram_in("c_maskT2", [128, 128])
    cml_t = dram_in("c_maskL", [128, 64])
    csm_t = dram_in("c_scanmask", [128, 1024])
    out_t = nc.dram_tensor("out", [S, D], F32, kind="ExternalOutput")
    xs_t = nc.dram_tensor("xs", [S, D], F32, kind="Internal")
    pt_t = nc.dram_tensor("ptf", [INC, S], F32, kind="Internal")
    vtok_t = nc.dram_tensor("vtok", [S, 512], BF16, kind="Internal")
    gsc_t = nc.dram_tensor("gsc", [4, 130 * 384], F32, kind="Internal")
    tap_t = {}
    if "pt" in taps:
        tap_t["pt"] = nc.dram_tensor("tap_pt", [INC, S], F32, kind="ExternalOutput")
    if "mixed" in taps:
        tap_t["mixed"] = nc.dram_tensor("tap_mixed", [D, S], BF16, kind="ExternalOutput")
    if "xs" in taps:
        tap_t["xs"] = nc.dram_tensor("tap_xs", [S, D], F32, kind="ExternalOutput")

    x_d, win_d, wout_d = x_t.ap(), win_t.ap(), wout_t.ap()
    out_d, xs_d, pt_d, vtok_d = out_t.ap(), xs_t.ap(), pt_t.ap(), vtok_t.ap()

    xs_res = [Res(f"xs{i}") for i in range(NT)]
    pt_res = [Res(f"pt{j}") for j in range(33)]
    vtok_res = [Res(f"vt{i}") for i in range(NT)]
    out_res = Res("out")
    gsc_res = Res("gsc")

    with ExitStack() as st:
        sb = lambda n, s, d=F32: st.enter_context(nc.sbuf_tensor(n, list(s), d))
        big = sb("big", [128, 8, S], BF16)
        big_res = [[Res(f"big{k}_{g}") for g in range(NG)] for k in range(8)]
        ar = sb("arena", [128, 32768], F32)
        ARES = [Res(f"ar{i}") for i in range(128)]
        ps_all = st.enter_context(nc.psum_tensor("psall", [128, 8 * 512], F32))
        PSR = [Res(f"ps{i}") for i in range(8)]

        def psb(bank, nb=1, parts=128):
            return ps_all[0:parts, bank * 512:(bank + nb) * 512], PSR[bank:bank + nb]

        class Arena:
            def __init__(self):
                self.p = 0

            def reset(self, p=0):
                self.p = p

            def alloc(self, n, dtype=F32, parts=128):
                nf = n if dtype is F32 else (n + 1) // 2
                nf = (nf + 1) // 2 * 2
                off = self.p
                self.p += nf
                assert self.p <= 32768, self.p
                ap = ar[0:parts, off:off + nf]
                if dtype is BF16:
                    ap = ap.bitcast(BF16)[:, 0:n]
                else:
                    ap = ap[:, 0:n]
                return ap, ARES[off // 256:(off + nf - 1) // 256 + 1]

        A = Arena()

        identf = sb("identf", [128, 128])
        identb = sb("identb", [128, 128], BF16)
        onesb = sb("onesb", [128, 128], BF16)
        ones64 = sb("ones64", [128, 64])
        ones64r = sb("ones64r", [128, 64])
        prm128 = sb("prm128s", [128, 60])
        prm64 = sb("prm64s", [128, 168])
        nprm64 = sb("nprm64", [128, 32])
        lamv = sb("lamv", [128, L * 8])
        wup = sb("wups", [128, 256])
        aup = sb("aups", [128, 256])
        wua_r = Res("wua")
        maskT2p = sb("maskT2p", [128, 128])
        maskT2n = sb("maskT2n", [128, 128])
        maskLn = sb("maskLn", [128, 64])
        bblk = sb("bblk", [128, 4, 3, 128], BF16)
        cfar = sb("cfar", [128, 8])
        ss = sb("ss", [128, 3 * NT])
        cst = Res("consts")
        ss_res = [Res(f"ss{i}") for i in range(NT)]
        sst_res = [[Res(f"sst{a}_{n}") for n in range(4)] for a in range(2)]

        def dma(eng, out, in_, reads, writes, key):
            P.add(eng, lambda e: e.dma_start(out=out, in_=in_), reads, writes, dma=key)

        dma("sp", identf[:], cidf_t.ap(), [], [cst], "c0")
        dma("sp", prm128[:], prm128_t.ap(), [], [cst], "c0")
        dma("sp", prm64[:], prm64_t.ap(), [], [cst], "c0")
        lamrep, lamrep_r = A.alloc(L * 256)
        dma("sp", lamrep, lamrep_t.ap(), [], lamrep_r, "c0")
        dma("sp", maskT2p[:], cm2_t.ap(), [], [cst], "c0")
        dma("sp", maskLn[:], cml_t.ap(), [], [cst], "c0")
        onehot, onehot_r = A.alloc(384, parts=64)
        relb, relb_rr = A.alloc(128, parts=64)
        gsb, gsb_rr = A.alloc(384, parts=4)
        relb_r = Res("relb")
        P.add("pool", lambda e: e.memset(onehot[:], 0.0), [], onehot_r)
        dma("sp", onehot[0:33, :], coh_t.ap(), [], onehot_r, "c0")
        P.add("pool", lambda e: e.memset(relb[:], 0.0), [], [relb_r] + relb_rr)
        dma("sp", relb[0:32, 0:4], relb_t.ap(), [], [relb_r], "c1")
        cst2 = Res("consts2")
        P.add("dve", lambda e: e.tensor_copy(out=identb[:], in_=identf[:]), [cst], [cst2])
        P.add("pool", lambda e: e.memset(onesb[:], 1.0), [], [cst2])
        P.add("pool", lambda e: e.memset(ones64[:], 1.0 / 64), [], [cst2])
        P.add("pool", lambda e: e.memset(ones64r[:], 1.0), [], [cst2])
        P.add("dve", lambda e: e.tensor_scalar(out=maskT2n[:], in0=maskT2p[:], scalar1=-1.0, scalar2=None, op0=ALU.mult), [cst], [cst2])
        P.add("dve", lambda e: e.tensor_scalar(out=maskLn[:], in0=maskLn[:], scalar1=-1.0, scalar2=None, op0=ALU.mult), [cst], [cst2])
        P.add("dve", lambda e: e.tensor_scalar(out=nprm64[:], in0=prm64[:, 56:88], scalar1=-1.0, scalar2=None, op0=ALU.mult), [cst], [cst2])
        P.add("pool", lambda e: e.memset(relb[32:33, 0:4], NEG8 / 8.0), [relb_r], [relb_r])

        lam_r = Res("lam")
        for l in range(depth):
            lq = lamrep[:, l * 256:(l + 1) * 256]
            tmpa, tmpr = A.alloc(128)
            tmpr = tmpr + lamrep_r
            for pr in range(2):
                P.add("dve", (lambda pr=pr, lq=lq, tmpa=tmpa: lambda e: e.tensor_tensor(out=tmpa[:, pr * 64:(pr + 1) * 64], in0=lq[:, (2 * pr) * 64:(2 * pr + 1) * 64], in1=lq[:, (2 * pr + 1) * 64:(2 * pr + 2) * 64], op=ALU.mult))(), [cst], tmpr)
                P.add("dve", (lambda pr=pr, l=l, tmpa=tmpa: lambda e: e.reduce_sum(out=lamv[:, l * 8 + pr:l * 8 + pr + 1], in_=tmpa[:, pr * 64:(pr + 1) * 64], axis=AX.X))(), tmpr, [lam_r])
            P.add("act", (lambda l=l: lambda e: e.activation(out=lamv[:, l * 8 + 2:l * 8 + 4], in_=lamv[:, l * 8:l * 8 + 2], func=AF.Exp))(), [lam_r], [lam_r])
            linit = 0.8 - 0.6 * math.exp(-0.3 * l)
            P.add("dve", (lambda l=l: lambda e: e.tensor_tensor(out=lamv[:, l * 8 + 4:l * 8 + 5], in0=lamv[:, l * 8 + 2:l * 8 + 3], in1=lamv[:, l * 8 + 3:l * 8 + 4], op=ALU.subtract))(), [lam_r], [lam_r])
            P.add("dve", (lambda l=l, linit=linit: lambda e: e.tensor_scalar(out=lamv[:, l * 8 + 5:l * 8 + 6], in0=lamv[:, l * 8 + 4:l * 8 + 5], scalar1=linit, scalar2=-1.0, op0=ALU.add, op1=ALU.mult))(), [lam_r], [lam_r])

        (pg, pgr) = psb(0)
        P.add("pe", lambda e: e.matmul(pg[:, 0:384], relb[:, :], onehot[:, :], start=True, stop=True), [relb_r] + onehot_r, pgr)
        gsb_r = Res("gsb")
        P.add("dve", lambda e: e.tensor_copy(out=gsb[:], in_=pg[0:4, 0:384]), pgr, [gsb_r] + gsb_rr)
        dma("sp", gsc_t.ap().rearrange("h (r n) -> h r n", n=384), gsb.unsqueeze(1).broadcast_to([4, 130, 384]), [gsb_r], [gsc_res], "c2")
        tdo, tdo_r = A.alloc(4 * 2 * 128)
        tdo4 = tdo.rearrange("p (h a q) -> p h a q", h=4, a=2)
        for h in range(4):
            for a_, base in ((0, 128), (1, 256)):
                dma("sp", tdo4[:, h, a_, :], bass.AP(gsc_t, h * 130 * 384 + base, [[383, 128], [1, 128]]), [gsc_res], tdo_r, "c3")
            dma("sp", cfar[:, h:h + 1], bass.AP(gsc_t, h * 130 * 384 + 383, [[0, 128], [1, 1]]), [gsc_res], [cst2], "c3")
        P.add("dve", lambda e: e.tensor_scalar(out=cfar[:, 4:8], in0=cfar[:, 0:4], scalar1=SCALE, scalar2=None, op0=ALU.mult), [cst2], [cst2])
        zt, zt_r = A.alloc(128)
        P.add("pool", lambda e: e.memset(zt[:], 0.0), [], zt_r)
        bias_r = Res("biasT")
        for h in range(4):
            P.add("dve", (lambda h=h: lambda e: e.tensor_copy(out=bblk[:, h, 0, :], in_=tdo4[:, h, 0, :]))(), tdo_r, [bias_r])
            P.add("pool", (lambda h=h: lambda e: e.tensor_copy(out=bblk[:, h, 1, :], in_=tdo4[:, h, 1, :]))(), tdo_r, [bias_r])
            P.add("dve", (lambda h=h: lambda e: e.tensor_scalar(out=bblk[:, h, 2, :], in0=zt[:], scalar1=cfar[:, h:h + 1], scalar2=None, op0=ALU.add))(), zt_r + [cst2], [bias_r])

        def TT(eng, out, a, b, op, rd, wr):
            P.add(eng, lambda e: e.tensor_tensor(out=out, in0=a, in1=b, op=op), rd, wr)

        def TS(eng, out, a, s1, s2, op0, op1, rd, wr):
            if s2 is None:
                P.add(eng, lambda e: e.tensor_scalar(out=out, in0=a, scalar1=s1, scalar2=None, op0=op0), rd, wr)
            else:
                P.add(eng, lambda e: e.tensor_scalar(out=out, in0=a, scalar1=s1, scalar2=s2, op0=op0, op1=op1), rd, wr)

        def STT(eng, out, a, sc, b, op0, op1, rd, wr):
            P.add(eng, lambda e: e.scalar_tensor_tensor(out=out, in0=a, scalar=sc, in1=b, op0=op0, op1=op1), rd, wr)

        def ACTF(out, in_, func, rd, wr, scale=1.0, bias=None):
            if bias is None:
                P.add("act", lambda e: e.activation(out=out, in_=in_, func=func, scale=scale), rd, wr)
            else:
                P.add("act", lambda e: e.activation(out=out, in_=in_, func=func, scale=scale, bias=bias), rd, wr)

        def MM(out, lhsT, rhs, st_, sp_, rd, wr):
            P.add("pe", lambda e: e.matmul(out, lhsT, rhs, start=st_, stop=sp_), rd, wr)

        def CP(eng, out, in_, rd, wr):
            if eng == "act":
                P.add("act", lambda e: e.activation(out=out, in_=in_, func=AF.Copy), rd, wr)
            else:
                P.add(eng, lambda e: e.tensor_copy(out=out, in_=in_), rd, wr)

        def RCP(out, in_, rd, wr):
            P.add("dve", lambda e: e.reciprocal(out=out, in_=in_), rd, wr)

        V = lambda buf, w: buf[0].rearrange("c (p t) -> c p t", p=16)
        V16 = lambda ap: ap.rearrange("c (p t) -> c p t", p=16)
        V4 = lambda ap: ap.rearrange("c (h t) -> c h t", h=4)
        pbk_state = [0]

        def pbk(n):
            if pbk_state[0] + n > 8:
                pbk_state[0] = 0
            b = pbk_state[0]
            pbk_state[0] += n
            return psb(b, n, parts=128)

        def rwkv_phase(l):
            P.pool_as = RWKV_POOL_AS
            A.reset()
            al = lambda n, dt=F32: A.alloc(n, dt, parts=128)
            HF = lambda ap, hf: ap[64 * hf:64 * hf + 64]

            def sel(x, hf):
                return x[hf] if isinstance(x, tuple) else HF(x, hf)

            def MM2(out, lhsT, rhs, st_, sp_, rd, wr):
                for hf in range(2):
                    MM(sel(out, hf), sel(lhsT, hf), sel(rhs, hf), st_, sp_, rd, wr)

            scanmask = al(1024)
            Sst, Sst_r0 = al(2048)
            Sst5 = Sst.rearrange("c (a n h v) -> c a n h v", a=2, n=4, h=4)
            Sfin = al(256)
            id2 = al(64)
            gC = al(16)
            KR = al(2048)
            kc, vc = al(1024), al(1024)
            pv = [al(1024), al(1024)]
            X1, X2, X4, C1, T, kt, bt, RK = [al(1024) for _ in range(8)]
            wdc, adc, wdp, adp, th = [al(256) for _ in range(5)]
            UWrhs, NAbT, AkT, UW = [al(2048) for _ in range(4)]
            Vtok, Khtok, Bntok, NAkb = [al(1024) for _ in range(4)]
            KR0 = (KR[0][:, 0:1024], KR[1][0:4])
            KR1 = (KR[0][:, 1024:2048], KR[1][4:8])
            KR4 = KR[0].rearrange("c (q p t) -> c q p t", q=2, p=16)
            idb = (identf[0:64, 0:64], identf[64:128, 64:128])
            id_bc = id2[0].unsqueeze(1).broadcast_to([128, 16, 64])
            on_r = (ones64r[0:64, :], ones64r[64:128, :])
            on_m = (ones64[0:64, :], ones64[64:128, :])

            def bc(col):
                return prm64[:, col:col + 4].unsqueeze(2).broadcast_to([128, 4, 256])

            dma("sp", scanmask[0], csm_t.ap(), [], scanmask[1], "rw_c")
            for hf in range(2):
                dma("sp", HF(wup[:], hf), wup_t.ap()[l], [], [wua_r], "rw_c")
                dma("sp", HF(aup[:], hf), aup_t.ap()[l], [], [wua_r], "rw_c")
            P.add("dve", lambda e: e.tensor_copy(out=id2[0][0:64, :], in_=identf[0:64, 0:64]), [cst], id2[1])
            P.add("dve", lambda e: e.tensor_copy(out=id2[0][64:128, :], in_=identf[64:128, 64:128]), [cst], id2[1])
            P.add("dve", lambda e: e.memset(Sst5[0:64, 0, 0, :, :], 0.0), [], [sst_res[0][0]] + Sst_r0)

            def src(row0, c0, c1):
                return pt_d[row0:row0 + 256, c0:c1].rearrange("(h c) t -> c h t", c=64)

            for gp in range(8):
                par = gp % 2
                tb = gp * 512
                curs = [KR1, kc, vc]
                for q in range(3):
                    row0 = 3072 + 256 * q
                    prs = [pt_res[row0 // 128], pt_res[row0 // 128 + 1]]
                    cur = curs[q]
                    pvb = pv[q % 2]
                    for hf in range(2):
                        t0 = tb + 256 * hf
                        dma("sp", V4(HF(cur[0], hf)), src(row0, t0, t0 + 256), prs, cur[1], f"rw_c{q}")
                        if t0 == 0:
                            P.add("dve", (lambda pvb=pvb: lambda e: e.memset(V4(HF(pvb[0], 0))[:, :, 0:1], 0.0))(), [], pvb[1])
                            dma("sp", V4(HF(pvb[0], hf))[:, :, 1:256], src(row0, 0, 255), prs, pvb[1], f"rw_p{q % 2}")
                        else:
                            dma("sp", V4(HF(pvb[0], hf)), src(row0, t0 - 1, t0 + 255), prs, pvb[1], f"rw_p{q % 2}")
                    TT("dve", pvb[0], pvb[0], cur[0], ALU.subtract, pvb[1] + cur[1], pvb[1])
                    TT("dve", V4(pvb[0]), V4(pvb[0]), bc(l * 12 + q * 4), ALU.mult, pvb[1] + [cst], pvb[1])
                    TT("dve", cur[0], cur[0], pvb[0], ALU.add, pvb[1] + cur[1], cur[1])
                for (cur, prv, row0, mcol, wk) in ((wdc, wdp, 3840, 48 + l * 2, 0), (adc, adp, 3904, 48 + l * 2 + 1, 1)):
                    for hf in range(2):
                        t0 = tb + 256 * hf
                        dma("sp", HF(cur[0], hf), pt_d[row0:row0 + 64, t0:t0 + 256], [pt_res[30]], cur[1], f"rw_wc{wk}")
                        if t0 == 0:
                            P.add("dve", (lambda prv=prv: lambda e: e.memset(HF(prv[0], 0)[:, 0:1], 0.0))(), [], prv[1])
                            dma("sp", HF(prv[0], hf)[:, 1:256], pt_d[row0:row0 + 64, 0:255], [pt_res[30]], prv[1], f"rw_wp{wk}")
                        else:
                            dma("sp", HF(prv[0], hf), pt_d[row0:row0 + 64, t0 - 1:t0 + 255], [pt_res[30]], prv[1], f"rw_wp{wk}")
                    TT("dve", prv[0], prv[0], cur[0], ALU.subtract, prv[1] + cur[1], prv[1])
                    STT("dve", cur[0], prv[0], prm64[:, mcol:mcol + 1], cur[0], ALU.mult, ALU.add, prv[1] + cur[1] + [cst], cur[1])
                ACTF(th[0], wdc[0], AF.Exp, wdc[1], th[1], scale=2.0)
                TS("dve", th[0], th[0], 1.0, None, ALU.add, None, th[1], th[1])
                RCP(th[0], th[0], th[1], th[1])
                TS("dve", th[0], th[0], -2.0, 1.0, ALU.mult, ALU.add, th[1], th[1])
                ups, upr = pbk(2)
                for h in range(4):
                    MM2(ups[:, h * 256:(h + 1) * 256], wup[:, 64 * h:64 * h + 64], th[0], True, True, th[1] + [wua_r], upr)
                for h in range(4):
                    ACTF(V4(X1[0])[:, h, :], ups[:, h * 256:(h + 1) * 256], AF.Exp, upr + [cst2], X1[1], scale=-1.0, bias=nprm64[:, l * 4 + h:l * 4 + h + 1])
                TS("dve", X1[0], X1[0], 1.0, None, ALU.add, None, X1[1], X1[1])
                RCP(X1[0], X1[0], X1[1], X1[1])
                TS("dve", X1[0], X1[0], -0.6065306597126334, None, ALU.mult, None, X1[1], X1[1])
                aps, apr = pbk(2)
                for h in range(4):
                    MM2(aps[:, h * 256:(h + 1) * 256], aup[:, 64 * h:64 * h + 64], adc[0], True, True, adc[1] + [wua_r], apr)
                for h in range(4):
                    ACTF(V4(X2[0])[:, h, :], aps[:, h * 256:(h + 1) * 256], AF.Exp, apr + [cst2], X2[1], scale=-1.0, bias=nprm64[:, 16 + l * 4 + h:16 + l * 4 + h + 1])
                TS("dve", X2[0], X2[0], 1.0, None, ALU.add, None, X2[1], X2[1])
                RCP(X2[0], X2[0], X2[1], X2[1])
                TT("dve", V4(KR0[0]), V4(kc[0]), bc(88 + l * 4), ALU.mult, kc[1] + [cst], KR0[1])
                ACTF(X4[0], KR0[0], AF.Square, KR0[1], X4[1])
                sps, spr = pbk(2)
                for hc in range(2):
                    MM2(sps[:, hc * 512:(hc + 1) * 512], on_r, X4[0][:, hc * 512:(hc + 1) * 512], True, True, X4[1] + [cst2], [spr[hc]])
                TS("dve", X4[0], sps, 1e-24, None, ALU.max, None, spr, X4[1])
                ACTF(X4[0], X4[0], AF.Ln, X4[1], X4[1])
                ACTF(X4[0], X4[0], AF.Exp, X4[1], X4[1], scale=-0.5)
                TT("dve", KR0[0], KR0[0], X4[0], ALU.mult, KR0[1] + X4[1], KR0[1])
                STT("dve", V4(X4[0]), V4(X2[0]), -1.0, bc(104 + l * 4), ALU.add, ALU.mult, X2[1] + [cst], X4[1])
                STT("dve", kc[0], X4[0], 1.0, kc[0], ALU.add, ALU.mult, X4[1] + kc[1], kc[1])
                TT("dve", RK[0], KR1[0], kc[0], ALU.mult, KR1[1] + kc[1], RK[1])
                TT("dve", V4(RK[0]), V4(RK[0]), bc(120 + l * 4), ALU.mult, RK[1] + [cst], RK[1])
                TT("dve", X2[0], KR0[0], X2[0], ALU.mult, KR0[1] + X2[1], X2[1])
                P.add("dve", lambda e: e.tensor_tensor_scan(out=C1[0], data0=scanmask[0], data1=X1[0], initial=0.0, op0=ALU.mult, op1=ALU.add), scanmask[1] + X1[1], C1[1])
                ACTF(T[0], C1[0], AF.Exp, C1[1], T[1])
                CP("dve", gC[0], V16(T[0])[:, :, 63], T[1], gC[1])
                TT("dve", KR1[0], KR1[0], T[0], ALU.mult, KR1[1] + T[1], KR1[1])
                ACTF(T[0], C1[0], AF.Exp, C1[1], T[1], scale=-1.0)
                TT("dve", kt[0], kc[0], T[0], ALU.mult, kc[1] + T[1], kt[1])
                TT("dve", bt[0], X2[0], T[0], ALU.mult, X2[1] + T[1], bt[1])
                TT("dve", T[0], C1[0], X1[0], ALU.subtract, C1[1] + X1[1], T[1])
                ACTF(T[0], T[0], AF.Exp, T[1], T[1])
                TT("dve", KR0[0], KR0[0], T[0], ALU.mult, KR0[1] + T[1], KR0[1])
                TT("dve", V16(T[0]), V16(C1[0])[:, :, 63:64].broadcast_to([128, 16, 64]), V16(C1[0]), ALU.subtract, C1[1], T[1])
                ACTF(T[0], T[0], AF.Exp, T[1], T[1])
                TT("dve", kc[0], kc[0], T[0], ALU.mult, kc[1] + T[1], kc[1])
                STT("dve", X2[0], X2[0], -1.0, T[0], ALU.mult, ALU.mult, X2[1] + T[1], X2[1])

                for (srcb, dstv, dstr) in ((KR0, V(UWrhs, 128)[:, :, 64:128], UWrhs[1]), (vc, V16(Vtok[0]), Vtok[1]), (kc, V16(Khtok[0]), Khtok[1]), (X2, V16(Bntok[0]), Bntok[1])):
                    tp, tpr = pbk(2)
                    for p in range(16):
                        MM2(tp[:, p * 64:(p + 1) * 64], V16(srcb[0])[:, p, :], idb, True, True, srcb[1] + [cst], [tpr[p // 8]])
                    CP("act", dstv, V16(tp), tpr, dstr)
                for (lh, dst, msk) in ((bt, NAbT, maskT2n), (kt, AkT, maskT2p)):
                    ap_, apr_ = pbk(4)
                    for p in range(16):
                        MM2(ap_[:, p * 128:(p + 1) * 128], V16(lh[0])[:, p, :], KR4[:, :, p, :], True, True, lh[1] + KR[1], [apr_[p // 4]])
                    TT("dve", V(dst, 128), ap_.rearrange("c (p t) -> c p t", p=16), msk[:].unsqueeze(1).broadcast_to([128, 16, 128]), ALU.mult, apr_ + [cst, cst2], dst[1])
                ap_, apr_ = pbk(2)
                for p in range(16):
                    MM2(ap_[:, p * 64:(p + 1) * 64], KR4[:, 0, p, :], V16(bt[0])[:, p, :], True, True, bt[1] + KR0[1], [apr_[p // 8]])
                TT("dve", V16(NAkb[0]), V16(ap_), maskLn[:].unsqueeze(1).broadcast_to([128, 16, 64]), ALU.mult, apr_ + [cst2], NAkb[1])
                NAb3 = V(NAbT, 128)
                R = T
                TT("dve", V16(R[0]), NAb3[:, :, 0:64], id_bc, ALU.add, NAbT[1] + id2[1], R[1])
                Pprev = (NAb3[:, :, 0:64], NAbT[1])
                PTprev = (V16(NAkb[0]), NAkb[1])
                Pbufs = [X1, X4]
                PTbufs = [C1, NAkb]
                for k in range(1, 6):
                    Pn = Pbufs[(k - 1) % 2]
                    PTn = PTbufs[(k - 1) % 2]
                    if k < 5:
                        pp, ppr = pbk(2)
                        for p in range(16):
                            MM2(pp[:, p * 64:(p + 1) * 64], PTprev[0][:, p, :], Pprev[0][:, p, :], True, True, PTprev[1] + Pprev[1], [ppr[p // 8]])
                    pt2, pt2r = pbk(2)
                    for p in range(16):
                        MM2(pt2[:, p * 64:(p + 1) * 64], Pprev[0][:, p, :], PTprev[0][:, p, :], True, True, PTprev[1] + Pprev[1], [pt2r[p // 8]])
                    if k < 5:
                        CP("act", Pn[0], pp, ppr, Pn[1])
                    CP("dve", PTn[0], pt2, pt2r, PTn[1])
                    rr, rrr = pbk(2)
                    for p in range(16):
                        MM2(rr[:, p * 64:(p + 1) * 64], V16(PTn[0])[:, p, :], V16(R[0])[:, p, :], True, True, PTn[1] + R[1], [rrr[p // 8]])
                    TT("dve", R[0], rr, R[0], ALU.add, rrr + R[1], R[1])
                    Pprev = (V16(Pn[0]), Pn[1])
                    PTprev = (V16(PTn[0]), PTn[1])
                xp, xpr = pbk(2)
                for p in range(16):
                    MM2(xp[:, p * 64:(p + 1) * 64], V(AkT, 128)[:, p, 0:64], V16(Vtok[0])[:, p, :], True, True, AkT[1] + Vtok[1], [xpr[p // 8]])
                CP("act", V(UWrhs, 128)[:, :, 0:64], V16(xp), xpr, UWrhs[1])
                up4, up4r = pbk(4)
                for p in range(16):
                    MM2(up4[:, p * 128:(p + 1) * 128], V16(R[0])[:, p, :], V(UWrhs, 128)[:, p, :], True, True, R[1] + UWrhs[1], [up4r[p // 4]])
                CP("act", UW[0][:, 0:1024], up4[:, 0:1024], up4r[0:2], UW[1][0:4])
                CP("dve", UW[0][:, 1024:2048], up4[:, 1024:2048], up4r[2:4], UW[1][4:8])
                UW3 = V(UW, 128)
                Qs, PTs, Dg, GT, Ysb, zr, Gg = X1, X4, pv[0], pv[1], kt, bt, C1
                qp, qpr = pbk(2)
                for p in range(16):
                    MM2(qp[:, p * 64:(p + 1) * 64], V16(Khtok[0])[:, p, :], V16(Vtok[0])[:, p, :], True, False, Khtok[1] + Vtok[1], [qpr[p // 8]])
                    MM2(qp[:, p * 64:(p + 1) * 64], V16(Bntok[0])[:, p, :], UW3[:, p, 0:64], False, True, Bntok[1] + UW[1], [qpr[p // 8]])
                CP("act", Qs[0], qp, qpr, Qs[1])
                pp2, pp2r = pbk(2)
                for p in range(16):
                    MM2(pp2[:, p * 64:(p + 1) * 64], UW3[:, p, 64:128], V16(Bntok[0])[:, p, :], True, True, Bntok[1] + UW[1], [pp2r[p // 8]])
                TT("dve", V16(Dg[0]), id_bc, gC[0].unsqueeze(2).broadcast_to([128, 16, 64]), ALU.mult, gC[1] + id2[1], Dg[1])
                TT("dve", PTs[0], pp2, Dg[0], ALU.add, pp2r + Dg[1], PTs[1])
                gp_, gpr = pbk(2)
                for p in range(16):
                    MM2(gp_[:, p * 64:(p + 1) * 64], UW3[:, p, 64:128], NAb3[:, p, 64:128], True, True, UW[1] + NAbT[1], [gpr[p // 8]])
                TT("dve", GT[0], gp_, KR1[0], ALU.add, gpr + KR1[1], GT[1])
                Q4 = Qs[0].rearrange("c (h n v) -> c h n v", h=4, n=4)
                for hf in range(2):
                    for n in range(4):
                        sp_, spr_ = pbk(1)
                        for h in range(4):
                            MM(HF(sp_, hf)[:, h * 64:(h + 1) * 64], HF(V16(PTs[0]), hf)[:, h * 4 + n, :], HF(Sst5, hf)[:, par, n, h, :], True, True, PTs[1] + [sst_res[par][n]], spr_)
                        src_ps = HF(sp_, hf)[:, 0:256].rearrange("c (h v) -> c h v", h=4)
                        if n < 3:
                            TT("dve", HF(Sst5, hf)[:, par, n + 1, :, :], src_ps, HF(Q4, hf)[:, :, n, :], ALU.add, spr_ + Qs[1], [sst_res[par][n + 1]])
                        else:
                            TT("dve", HF(Sfin[0], hf).rearrange("c (h v) -> c h v", h=4), src_ps, HF(Q4, hf)[:, :, n, :], ALU.add, spr_ + Qs[1], Sfin[1])
                    sh, shr = pbk(1)
                    if hf == 0:
                        MM(sh[64:128, 0:256], identf[0:64, 0:64], Sfin[0][0:64, :], True, True, Sfin[1] + [cst], shr)
                        CP("act", Sst5[64:128, par, 0, :, :], sh[64:128, 0:256].rearrange("c (h v) -> c h v", h=4), shr, [sst_res[par][0]])
                    else:
                        MM(sh[0:64, 0:256], identf[64:128, 64:128], Sfin[0][64:128, :], True, True, Sfin[1] + [cst], shr)
                        CP("act", Sst5[0:64, 1 - par, 0, :, :], sh[0:64, 0:256].rearrange("c (h v) -> c h v", h=4), shr, [sst_res[1 - par][0]])
                yp, ypr = pbk(2)
                for p in range(16):
                    h, n = p // 4, p % 4
                    MM2(yp[:, p * 64:(p + 1) * 64], Sst5[:, par, n, h, :], V16(GT[0])[:, p, :], True, False, [sst_res[par][n]] + GT[1], [ypr[p // 8]])
                    MM2(yp[:, p * 64:(p + 1) * 64], V16(Vtok[0])[:, p, :], V(AkT, 128)[:, p, 64:128], False, False, Vtok[1] + AkT[1], [ypr[p // 8]])
                    MM2(yp[:, p * 64:(p + 1) * 64], UW3[:, p, 0:64], NAb3[:, p, 64:128], False, True, UW[1] + NAbT[1], [ypr[p // 8]])
                CP("act", Ysb[0], yp, ypr, Ysb[1])
                mp, mpr = pbk(2)
                for hc in range(2):
                    MM2(mp[:, hc * 512:(hc + 1) * 512], on_m, Ysb[0][:, hc * 512:(hc + 1) * 512], True, True, Ysb[1] + [cst2], [mpr[hc]])
                TT("dve", Ysb[0], Ysb[0], mp, ALU.subtract, Ysb[1] + mpr, Ysb[1])
                ACTF(T[0], Ysb[0], AF.Square, Ysb[1], T[1])
                vp_, vpr_ = pbk(2)
                for hc in range(2):
                    MM2(vp_[:, hc * 512:(hc + 1) * 512], on_m, T[0][:, hc * 512:(hc + 1) * 512], True, True, T[1] + [cst2], [vpr_[hc]])
                ACTF(X4[0], vp_, AF.Ln, vpr_, X4[1], bias=GN_EPS)
                ACTF(X4[0], X4[0], AF.Exp, X4[1], X4[1], scale=-0.5)
                TT("dve", Ysb[0], Ysb[0], X4[0], ALU.mult, Ysb[1] + X4[1], Ysb[1])
                for h in range(4):
                    TS("dve", V4(Ysb[0])[:, h, :], V4(Ysb[0])[:, h, :], prm64[:, 136 + l * 4 + h:137 + l * 4 + h], prm64[:, 152 + l * 4 + h:153 + l * 4 + h], ALU.mult, ALU.add, Ysb[1] + [cst], Ysb[1])
                bp, bpr = pbk(2)
                for hc in range(2):
                    MM2(bp[:, hc * 512:(hc + 1) * 512], on_r, RK[0][:, hc * 512:(hc + 1) * 512], True, True, RK[1] + [cst2], [bpr[hc]])
                TT("dve", T[0], bp, vc[0], ALU.mult, bpr + vc[1], T[1])
                TT("dve", Ysb[0], Ysb[0], T[0], ALU.add, Ysb[1] + T[1], Ysb[1])
                for hf in range(2):
                    t0 = tb + 256 * hf
                    dma("sp", V4(HF(zr[0], hf)), src(3968, t0, t0 + 256), [pt_res[31], pt_res[32]], zr[1], "rw_z")
                ACTF(Gg[0], zr[0], AF.Exp, zr[1], Gg[1], scale=-1.0)
                TS("dve", Gg[0], Gg[0], 1.0, None, ALU.add, None, Gg[1], Gg[1])
                RCP(Gg[0], Gg[0], Gg[1], Gg[1])
                TT("dve", Gg[0], Gg[0], zr[0], ALU.mult, Gg[1] + zr[1], Gg[1])
                outb = (Dg[0].bitcast(BF16)[:, 0:1024], Dg[1])
                TT("dve", outb[0], Ysb[0], Gg[0], ALU.mult, Ysb[1] + Gg[1], outb[1])
                for hf in range(2):
                    t0 = tb + 256 * hf
                    for h in range(4):
                        dma("sp", big[64 * (h % 2):64 * (h % 2) + 64, 6 + h // 2, t0:t0 + 256], V4(HF(outb[0], hf))[:, h, :], outb[1], [big_res[6 + h // 2][gp]], "rw_o")


        def emit_layer(l):
            src_d = x_d if l == 0 else xs_d
            if stop_after == 'setup':
                return
            A.reset()
            xt = [A.alloc(D) for _ in range(3)]
            hb = [A.alloc(D, BF16) for _ in range(2)]
            junk = A.alloc(D, BF16)
            for i in range(NT):
                xa, xr = xt[i % 3]
                ha, hr = hb[i % 2]
                dma("sp", xa, src_d[i * 128:(i + 1) * 128, :], [xs_res[i]] if l > 0 else [], xr, f"xt{i % 3}")
                P.add("act", (lambda xa=xa, i=i: lambda e: e.activation(out=junk[0], in_=xa, func=AF.Square, accum_out=ss[:, 3 * i:3 * i + 1]))(), xr, junk[1] + [ss_res[i]])
                P.add("act", (lambda i=i: lambda e: e.activation(out=ss[:, 3 * i + 1:3 * i + 2], in_=ss[:, 3 * i:3 * i + 1], func=AF.Ln, scale=1.0 / D, bias=NORM_EPS))(), [ss_res[i]], [ss_res[i]])
                P.add("act", (lambda i=i: lambda e: e.activation(out=ss[:, 3 * i + 2:3 * i + 3], in_=ss[:, 3 * i + 1:3 * i + 2], func=AF.Exp, scale=-0.5))(), [ss_res[i]], [ss_res[i]])
                P.add("dve", (lambda xa=xa, ha=ha, i=i: lambda e: e.tensor_scalar(out=ha, in0=xa, scalar1=ss[:, 3 * i + 2:3 * i + 3], scalar2=None, op0=ALU.mult))(), xr + [ss_res[i]], hr)
                bank = i % 2
                pa, pr_ = psb(bank)
                pab = pa.bitcast(BF16)
                for k in range(8):
                    P.add("pe", (lambda pab=pab, ha=ha, k=k: lambda e: e.transpose(pab[:, k * 128:(k + 1) * 128], ha[:, k * 128:(k + 1) * 128], identb[:]))(), hr + [cst2], pr_)
                eng = "act" if i % 2 == 0 else "dve"
                dst = big[:, :, i * 128:(i + 1) * 128]
                srcv = pab.rearrange("p (k t) -> p k t", k=8)
                wr = [big_res[k][i // 4] for k in range(8)]
                if eng == "act":
                    P.add("act", (lambda dst=dst, srcv=srcv: lambda e: e.activation(out=dst, in_=srcv, func=AF.Copy))(), pr_, wr)
                else:
                    P.add("dve", (lambda dst=dst, srcv=srcv: lambda e: e.tensor_copy(out=dst, in_=srcv))(), pr_, wr)

            if stop_after == 'A':
                return
            A.reset()
            wst = [A.alloc(8 * 512) for _ in range(2)]
            wbf = [A.alloc(8 * 512, BF16) for _ in range(2)]
            stage = [A.alloc(S) for _ in range(2)]
            vst = [A.alloc(512, BF16) for _ in range(2)]
            allbig = [big_res[k][g] for k in range(8) for g in range(NG)]

            def load_w(r):
                width = 512 if r < 8 else 128
                wa, wr_ = wst[r % 2]
                wb, wbr = wbf[r % 2]
                wa3 = wa.rearrange("p (k c) -> p k c", k=8)
                wb3 = wb.rearrange("p (k c) -> p k c", k=8)
                if "nowload" not in DBG:
                    dma("sp", wa3[:, :, 0:width], win_d[l, :, r * 512:r * 512 + width].rearrange("(k p) c -> p k c", p=128), [], wr_, f"wst{r % 2}")
                else:
                    P.add("dve", lambda e: e.memset(wa3[:, :, 0:width], 0.5), [], wr_)
                for k in range(8):
                    P.add("pool", (lambda wa3=wa3, wb3=wb3, k=k, width=width: lambda e: e.tensor_scalar(out=wb3[:, k, 0:width], in0=wa3[:, k, 0:width], scalar1=prm128[:, l * 8 + k:l * 8 + k + 1], scalar2=None, op0=ALU.mult))(), wr_ + [cst], wbr)

            load_w(0)
            pbank = 0
            ev = 0
            for r in range(9):
                if r + 1 < 9:
                    load_w(r + 1)
                width = 512 if r < 8 else 128
                wb, wbr = wbf[r % 2]
                wb3 = wb.rearrange("p (k c) -> p k c", k=8)
                if r == 2:
                    for i in range(NT):
                        pa, pr_ = psb(4 + pbank % 4)
                        pbank += 1
                        for k in range(8):
                            P.add("pe", (lambda pa=pa, k=k, i=i, wb3=wb3: lambda e: e.matmul(pa, big[:, k, i * 128:(i + 1) * 128], wb3[:, k, :], start=(k == 0), stop=(k == 7)))(), [big_res[k][i // 4], ] + wbr, pr_)
                        va, vr = vst[i % 2]
                        eng = "act" if ev % 2 == 0 else "dve"
                        ev += 1
                        if eng == "act":
                            P.add("act", (lambda va=va, pa=pa: lambda e: e.activation(out=va, in_=pa, func=AF.Copy))(), pr_, vr)
                        else:
                            P.add("dve", (lambda va=va, pa=pa: lambda e: e.tensor_copy(out=va, in_=pa))(), pr_, vr)
                        if "nostore" not in DBG:
                            dma("sp", vtok_d[i * 128:(i + 1) * 128, :], va, vr, [vtok_res[i]], f"vst{i % 2}")
                    continue
                for jj in range(width // 128):
                    j = 4 * r + jj
                    sa, sr = stage[j % 2]
                    for tg in range(NG):
                        pa, pr_ = psb(4 + pbank % 4)
                        pbank += 1
                        for k in range(8):
                            P.add("pe", (lambda pa=pa, k=k, tg=tg, jj=jj, wb3=wb3: lambda e: e.matmul(pa, wb3[:, k, jj * 128:(jj + 1) * 128], big[:, k, tg * 512:(tg + 1) * 512], start=(k == 0), stop=(k == 7)))(), [big_res[k][tg]] + wbr, pr_)
                        eng = "act" if ev % 2 == 0 else "dve"
                        ev += 1
                        sres = sr[tg * 2:(tg + 1) * 2]
                        if eng == "act":
                            P.add("act", (lambda sa=sa, pa=pa, tg=tg: lambda e: e.activation(out=sa[:, tg * 512:(tg + 1) * 512], in_=pa, func=AF.Copy))(), pr_, sres)
                        else:
                            P.add("dve", (lambda sa=sa, pa=pa, tg=tg: lambda e: e.tensor_copy(out=sa[:, tg * 512:(tg + 1) * 512], in_=pa))(), pr_, sres)
                        if "nostore" not in DBG and "nopt" not in DBG:
                            dma("sp", pt_d[j * 128:(j + 1) * 128, tg * 512:(tg + 1) * 512], sa[:, tg * 512:(tg + 1) * 512], sres, [pt_res[j]], f"stg{j % 2}_{tg}")

            if l == tap_layer and "pt" in taps:
                dma("sp", tap_t["pt"].ap()[0:1024, :], pt_d[0:1024, :], pt_res, [out_res], "tap")
                dma("sp", tap_t["pt"].ap()[1536:INC, :], pt_d[1536:INC, :], pt_res, [out_res], "tap")

            if stop_after == 'B':
                return
            A.reset()
            qkf = A.alloc(S)
            qb = [A.alloc(S, BF16) for _ in range(2)]
            kb = [A.alloc(S, BF16) for _ in range(2)]
            vh = [A.alloc(NT * 128, BF16) for _ in range(2)]
            zf = [A.alloc(512) for _ in range(2)]
            ob = [A.alloc(4 * 512) for _ in range(2)]
            ptb = [A.alloc(2 * 512, BF16) for _ in range(3)]
            r01 = A.alloc(2 * 512)
            o01 = A.alloc(2 * 512)
            osb = A.alloc(512)
            sqb = A.alloc(512, BF16)
            rsd = A.alloc(512)
            gat = A.alloc(512)
            linit = 0.8 - 0.6 * math.exp(-0.3 * l)
            pt_cnt = [0]
            st_cnt = [0]
            pend = [None]
            zi = 0
            for h in range(4):
                hs = h % 2
                for c4 in range(4):
                    dma("sp", qkf[0][:, c4 * 1024:(c4 + 1) * 1024], pt_d[h * 128:(h + 1) * 128, c4 * 1024:(c4 + 1) * 1024], [pt_res[h]], qkf[1], "qkf")
                P.add("dve", (lambda hs=hs: lambda e: e.tensor_copy(out=qb[hs][0], in_=qkf[0]))(), qkf[1], qb[hs][1])
                for c4 in range(4):
                    dma("sp", qkf[0][:, c4 * 1024:(c4 + 1) * 1024], pt_d[512 + h * 128:512 + (h + 1) * 128, c4 * 1024:(c4 + 1) * 1024], [pt_res[4 + h]], qkf[1], "qkf")
                P.add("pool", (lambda hs=hs: lambda e: e.tensor_copy(out=kb[hs][0], in_=qkf[0]))(), qkf[1], kb[hs][1])
                vh3 = vh[hs][0].rearrange("p (n d) -> p n d", n=NT)
                for c4 in range(4):
                    dma("sp", vh3[:, c4 * 8:(c4 + 1) * 8, :], vtok_d[c4 * 1024:(c4 + 1) * 1024, h * 128:(h + 1) * 128].rearrange("(n p) d -> p n d", p=128), vtok_res, vh[hs][1], f"vh{hs}")
                qbh, kbh = qb[hs][0], kb[hs][0]
                for g in range(NG):
                    nk = 4 * g + 4
                    zsl, zsr = zf[zi % 2]
                    dma("sp", zsl, pt_d[1536 + h * 128:1536 + (h + 1) * 128, g * 512:(g + 1) * 512], [pt_res[12 + h]], zsr, f"zf{zi % 2}")
                    zi += 1
                    accs = [psb(4 + a_) for a_ in range(4)]

                    def emit_qk(kt, g=g, h=h, kbh=kbh, qbh=qbh, hs=hs):
                        sbank = (st_cnt[0] % 2) * 2
                        st_cnt[0] += 1
                        s2, s2r = psb(sbank, 2)
                        r = kt - 4 * g
                        diag = r >= -1
                        c0 = 128 * max(r, 0)
                        for m in range(2):
                            P.add("pe", (lambda s2=s2, m=m, kt=kt, diag=diag, c0=c0: lambda e: e.matmul(s2[:, m * 512 + c0:(m + 1) * 512], kbh[m * 64:(m + 1) * 64, kt * 128:(kt + 1) * 128], qbh[m * 64:(m + 1) * 64, g * 512 + c0:(g + 1) * 512], start=True, stop=not diag))(), qb[hs][1] + kb[hs][1], [s2r[m]])
                            if diag:
                                for s_ in range(max(r, 0), 4):
                                    dlt = s_ - r
                                    bi = 0 if dlt == 0 else (1 if dlt == 1 else 2)
                                    P.add("pe", (lambda s2=s2, m=m, s_=s_, bi=bi: lambda e: e.matmul(s2[:, m * 512 + s_ * 128:m * 512 + (s_ + 1) * 128], identb[:], bblk[:, h, bi, :], start=False, stop=(s_ == 3)))(), [cst2, bias_r], [s2r[m]])
                        return (s2, s2r, diag, c0)

                    def emit_rest(kt, qk, nk=nk, h=h, accs=accs, vh3=vh3, hs=hs):
                        s2, s2r, diag, c0 = qk
                        pa_, par = ptb[pt_cnt[0] % 3]
                        pt_cnt[0] += 1
                        s23 = s2.rearrange("p (m q) -> p m q", m=2)[:, :, c0:512]
                        pa3 = pa_.rearrange("p (m q) -> p m q", m=2)[:, :, c0:512]
                        if diag:
                            P.add("act", (lambda: lambda e: e.activation(out=pa3, in_=s23, func=AF.Exp, scale=SCALE))(), s2r, par)
                        else:
                            P.add("act", (lambda: lambda e: e.activation(out=pa3, in_=s23, func=AF.Exp, scale=SCALE, bias=cfar[:, 4 + h:5 + h]))(), s2r + [cst2], par)
                        for m in range(2):
                            P.add("pe", (lambda m=m, acc=accs[m][0]: lambda e: e.matmul(acc[:, c0:512], vh3[:, kt, :], pa_[:, m * 512 + c0:(m + 1) * 512], start=(kt == 0), stop=(kt == nk - 1)))(), par + vh[hs][1], accs[m][1])
                            P.add("pe", (lambda m=m, acc=accs[2 + m][0]: lambda e: e.matmul(acc[:, c0:512], onesb[:], pa_[:, m * 512 + c0:(m + 1) * 512], start=(kt == 0), stop=(kt == nk - 1)))(), par + [cst2], accs[2 + m][1])

                    nxt = emit_qk(0)
                    for kt in range(nk):
                        cur = nxt
                        if kt + 1 < nk:
                            nxt = emit_qk(kt + 1)
                        emit_rest(kt, cur)
                        if kt == 2 and pend[0] is not None:
                            pend[0]()
                            pend[0] = None
                    oba, obr = ob[g % 2]
                    acc4, acc4r = psb(4, 4)
                    P.add("act", (lambda oba=oba, acc4=acc4: lambda e: e.activation(out=oba[:, 0:1024], in_=acc4[:, 0:1024], func=AF.Copy))(), acc4r[0:2], obr)
                    P.add("dve", (lambda oba=oba, acc4=acc4: lambda e: e.tensor_copy(out=oba[:, 1024:2048], in_=acc4[:, 1024:2048]))(), acc4r[2:4], obr)
                    P.add("dve", (lambda oba=oba: lambda e: e.reciprocal(out=r01[0], in_=oba[:, 1024:2048]))(), obr, r01[1])
                    P.add("dve", (lambda oba=oba: lambda e: e.tensor_tensor(out=o01[0], in0=oba[:, 0:1024], in1=r01[0], op=ALU.mult))(), obr + r01[1], o01[1])
                    P.add("dve", (lambda: lambda e: e.scalar_tensor_tensor(out=osb[0], in0=o01[0][:, 512:1024], scalar=lamv[:, l * 8 + 5:l * 8 + 6], in1=o01[0][:, 0:512], op0=ALU.mult, op1=ALU.add))(), o01[1] + [lam_r], osb[1])

                    def tail(h=h, g=g, zsl=zsl, zsr=zsr):
                        P.add("act", (lambda: lambda e: e.activation(out=sqb[0], in_=osb[0], func=AF.Square))(), osb[1], sqb[1])
                        P.add("act", (lambda: lambda e: e.activation(out=gat[0], in_=zsl, func=AF.Exp, scale=-1.0))(), zsr, gat[1])
                        sq_ps, sq_r = psb(0)
                        P.add("pe", (lambda: lambda e: e.matmul(sq_ps, onesb[:], sqb[0], start=True, stop=True))(), sqb[1] + [cst2], sq_r)
                        P.add("act", (lambda: lambda e: e.activation(out=rsd[0], in_=sq_ps, func=AF.Ln, scale=1.0 / 128, bias=SUBLN_EPS))(), sq_r, rsd[1])
                        P.add("act", (lambda: lambda e: e.activation(out=rsd[0], in_=rsd[0], func=AF.Exp, scale=-0.5))(), rsd[1], rsd[1])
                        P.add("pool", (lambda: lambda e: e.tensor_scalar(out=gat[0], in0=gat[0], scalar1=1.0, scalar2=None, op0=ALU.add))(), gat[1], gat[1])
                        P.add("dve", (lambda: lambda e: e.reciprocal(out=gat[0], in_=gat[0]))(), gat[1], gat[1])
                        P.add("pool", (lambda: lambda e: e.tensor_tensor(out=gat[0], in0=gat[0], in1=zsl, op=ALU.mult))(), gat[1] + zsr, gat[1])
                        P.add("dve", (lambda: lambda e: e.tensor_tensor(out=osb[0], in0=osb[0], in1=rsd[0], op=ALU.mult))(), osb[1] + rsd[1], osb[1])
                        P.add("dve", (lambda: lambda e: e.tensor_scalar(out=osb[0], in0=osb[0], scalar1=prm128[:, 32 + l:33 + l], scalar2=(1.0 - linit), op0=ALU.mult, op1=ALU.mult))(), osb[1] + [cst], osb[1])
                        P.add("dve", (lambda: lambda e: e.tensor_tensor(out=big[:, h, g * 512:(g + 1) * 512], in0=osb[0], in1=gat[0], op=ALU.mult))(), osb[1] + gat[1], [big_res[h][g]])

                    pend[0] = tail
            if pend[0] is not None:
                pend[0]()
                pend[0] = None

            if stop_after == 'C':
                return
            A.reset()
            cbuf = [[A.alloc(516) for _ in range(4)] for _ in range(2)]
            cw = lambda k, j: prm128[:, 36 + l * 6 + k * 2 + j:36 + l * 6 + k * 2 + j + 1]
            it = 0
            for j in range(2):
                for g in range(NG):
                    bs = cbuf[it % 2]
                    it += 1
                    (cba, cbr), (cca, ccr), (cha, chr_), (cza, czr) = bs
                    t0 = g * 512
                    lo = 2 if g > 0 else 0
                    dma("sp", cca[:, 2 - lo:514], pt_d[(18 + j) * 128:(19 + j) * 128, t0 - lo:t0 + 512], [pt_res[18 + j]], ccr, f"cv{it % 2}")
                    dma("sp", cha[:, 2 - lo:514], pt_d[(20 + j) * 128:(21 + j) * 128, t0 - lo:t0 + 512], [pt_res[20 + j]], chr_, f"cv{it % 2}")
                    dma("sp", cba[:, 0:512], pt_d[(16 + j) * 128:(17 + j) * 128, t0:t0 + 512], [pt_res[16 + j]], cbr, f"cv{it % 2}")
                    dma("sp", cza[:, 0:512], pt_d[(22 + j) * 128:(23 + j) * 128, t0:t0 + 512], [pt_res[22 + j]], czr, f"cv{it % 2}")
                    if g == 0:
                        P.add("pool", (lambda cca=cca: lambda e: e.memset(cca[:, 0:2], 0.0))(), [], ccr)
                        P.add("pool", (lambda cha=cha: lambda e: e.memset(cha[:, 0:2], 0.0))(), [], chr_)
                    P.add("pool", (lambda cca=cca, cha=cha: lambda e: e.tensor_tensor(out=cca[:, 0:514], in0=cca[:, 0:514], in1=cha[:, 0:514], op=ALU.mult))(), ccr + chr_, ccr)
                    P.add("dve", (lambda cca=cca, cha=cha, j=j: lambda e: e.tensor_scalar(out=cha[:, 0:512], in0=cca[:, 0:512], scalar1=cw(0, j), scalar2=None, op0=ALU.mult))(), ccr + [cst], chr_)
                    P.add("dve", (lambda cca=cca, cha=cha, j=j: lambda e: e.scalar_tensor_tensor(out=cha[:, 0:512], in0=cca[:, 1:513], scalar=cw(1, j), in1=cha[:, 0:512], op0=ALU.mult, op1=ALU.add))(), ccr + chr_ + [cst], chr_)
                    P.add("dve", (lambda cca=cca, cha=cha, j=j: lambda e: e.scalar_tensor_tensor(out=cha[:, 0:512], in0=cca[:, 2:514], scalar=cw(2, j), in1=cha[:, 0:512], op0=ALU.mult, op1=ALU.add))(), ccr + chr_ + [cst], chr_)
                    P.add("act", (lambda cza=cza, cca=cca: lambda e: e.activation(out=cca[:, 0:512], in_=cza[:, 0:512], func=AF.Exp, scale=-1.0))(), czr + ccr, ccr)
                    P.add("pool", (lambda cca=cca: lambda e: e.tensor_scalar(out=cca[:, 0:512], in0=cca[:, 0:512], scalar1=1.0, scalar2=None, op0=ALU.add))(), ccr, ccr)
                    P.add("dve", (lambda cca=cca: lambda e: e.reciprocal(out=cca[:, 0:512], in_=cca[:, 0:512]))(), ccr, ccr)
                    P.add("pool", (lambda cca=cca, cza=cza: lambda e: e.tensor_tensor(out=cca[:, 0:512], in0=cca[:, 0:512], in1=cza[:, 0:512], op=ALU.mult))(), ccr + czr, ccr)
                    P.add("pool", (lambda cha=cha, cba=cba: lambda e: e.tensor_tensor(out=cha[:, 0:512], in0=cha[:, 0:512], in1=cba[:, 0:512], op=ALU.mult))(), chr_ + cbr, chr_)
                    P.add("dve", (lambda cha=cha, cca=cca, j=j, g=g: lambda e: e.tensor_tensor(out=big[:, 4 + j, g * 512:(g + 1) * 512], in0=cha[:, 0:512], in1=cca[:, 0:512], op=ALU.mult))(), chr_ + ccr, [big_res[4 + j][g]])

            if stop_after == 'D':
                return
            if do_rwkv:
                rwkv_phase(l)
                P.pool_as = POOL_AS
            else:
                for k in (6, 7):
                    for g in range(NG):
                        P.add("pool", (lambda k=k, g=g: lambda e: e.memset(big[:, k, g * 512:(g + 1) * 512], 0.0))(), [], [big_res[k][g]])

            if l == tap_layer and "mixed" in taps:
                dma("sp", tap_t["mixed"].ap().rearrange("(k p) t -> p k t", p=128), big[:], allbig, [out_res], "tap")

            if stop_after == 'E':
                return
            A.reset()
            wo_st = [A.alloc(D) for _ in range(2)]
            wo = A.alloc(8 * D, BF16)
            wo3 = wo[0].rearrange("p (k d) -> p k d", k=8)
            xin = [A.alloc(D) for _ in range(2)]
            xo = [A.alloc(D) for _ in range(2)]
            for k in range(8):
                wa, wr_ = wo_st[k % 2]
                dma("sp", wa, wout_d[l, k * 128:(k + 1) * 128, :], [], wr_, f"wo{k % 2}")
                P.add("pool" if k % 2 else "dve", (lambda wa=wa, k=k: lambda e: e.tensor_copy(out=wo3[:, k, :], in_=wa))(), wr_, wo[1])
            last = (l == depth - 1)
            if last:
                fga, fgr = A.alloc(D)
                dma("sp", fga, fg_t.ap(), [], fgr, "fgl")
            for i in range(NT):
                xa, xr = xin[i % 2]
                ya, yr = xo[i % 2]
                dma("sp", xa, src_d[i * 128:(i + 1) * 128, :], [xs_res[i]] if l > 0 else [], xr, f"xin{i % 2}")
                p2, p2r = psb((i % 2) * 2, 2)
                for half in range(2):
                    for k in range(8):
                        P.add("pe", (lambda p2=p2, half=half, k=k, i=i: lambda e: e.matmul(p2[:, half * 512:(half + 1) * 512], big[:, k, i * 128:(i + 1) * 128], wo3[:, k, half * 512:(half + 1) * 512], start=(k == 0), stop=(k == 7)))(), [big_res[k][i // 4]] + wo[1], [p2r[half]])
                P.add("dve", (lambda ya=ya, p2=p2, xa=xa: lambda e: e.tensor_tensor(out=ya, in0=p2, in1=xa, op=ALU.add))(), p2r + xr, yr)
                if not last:
                    dma("sp", xs_d[i * 128:(i + 1) * 128, :], ya, yr, [xs_res[i]], f"xo{i % 2}")
                else:
                    if "xs" in taps:
                        dma("sp", tap_t["xs"].ap()[i * 128:(i + 1) * 128, :], ya, yr, [out_res], f"xo{i % 2}")
                    fr = Res(f"fin{i}")
                    P.add("act", (lambda ya=ya, xa=xa, i=i: lambda e: e.activation(out=xa, in_=ya, func=AF.Square, accum_out=ss[:, 3 * i:3 * i + 1]))(), yr, xr + [ss_res[i]])
                    P.add("act", (lambda i=i: lambda e: e.activation(out=ss[:, 3 * i + 1:3 * i + 2], in_=ss[:, 3 * i:3 * i + 1], func=AF.Ln, scale=1.0 / D, bias=NORM_EPS))(), [ss_res[i]], [ss_res[i]])
                    P.add("act", (lambda i=i: lambda e: e.activation(out=ss[:, 3 * i + 2:3 * i + 3], in_=ss[:, 3 * i + 1:3 * i + 2], func=AF.Exp, scale=-0.5))(), [ss_res[i]], [ss_res[i]])
                    P.add("dve", (lambda ya=ya, xa=xa, i=i: lambda e: e.scalar_tensor_tensor(out=xa, in0=ya, scalar=ss[:, 3 * i + 2:3 * i + 3], in1=fga, op0=ALU.mult, op1=ALU.mult))(), yr + [ss_res[i]] + fgr, xr)
                    dma("sp", out_d[i * 128:(i + 1) * 128, :], xa, xr, [out_res], f"xo{i % 2}")

        for l_ in range(depth):
            emit_layer(l_)
        P.add("sp", lambda e: e.nop(), [out_res], [])
        nsem = P.emit(nc, st)
    return nc, len(P.ops), nsem


_CACHE = {}


def kernel(**inputs):
    x = np.asarray(inputs["x"], np.float32)
    prm128, fg, lamrep, prm64 = host_params(inputs)
    consts = make_consts()
    if "nc" not in _CACHE:
        _CACHE["nc"] = build(L)[0]
    nc = _CACHE["nc"]
    shared = {
        "w_in": np.ascontiguousarray(np.asarray(inputs["w_in"], np.float32)),
        "w_out": np.ascontiguousarray(np.asarray(inputs["w_out"], np.float32)),
        "rel_bias": np.ascontiguousarray(np.asarray(inputs["rel_bias"], np.float32)),
        "w_up": np.ascontiguousarray(np.asarray(inputs["w_up"], np.float32)),
        "a_up": np.ascontiguousarray(np.asarray(inputs["a_up"], np.float32)),
        "prm128": prm128, "fg": fg, "lamrep": lamrep, "prm64": prm64,
    }
    shared.update(consts)
    in_maps = []
    for b in range(8):
        m = dict(shared)
        m["x"] = np.ascontiguousarray(x[b])
        in_maps.append(m)
    res = run_bass_kernel_spmd(nc, in_maps, core_ids=list(range(8)))
    return np.stack([np.asarray(r["out"], np.float32) for r in res.results], axis=0)
```

```python
import math
from contextlib import ExitStack

import numpy as np
import ml_dtypes

import concourse.bass as bass
import concourse.mybir as mybir
from concourse.bass_utils import run_bass_kernel_spmd

F32 = mybir.dt.float32
BF16 = mybir.dt.bfloat16
AF = mybir.ActivationFunctionType
ALU = mybir.AluOpType
AX = mybir.AxisListType

S = 4096
D = 1024
NT = 32
NG = 8
L = 4
INC = 4224
NEG8 = -240000.0
NORM_EPS = 1e-6
SUBLN_EPS = 1e-5
GN_EPS = 64e-5
SCALE = 0.125
DBG = set()
POOL_AS = "dve"
RWKV_POOL_AS = "dve"


class Res:
    __slots__ = ("name", "writer", "readers")

    def __init__(self, name):
        self.name = name
        self.writer = None
        self.readers = []


class Op:
    __slots__ = ("eng", "fn", "deps", "dma", "idx", "sig", "waits", "has_dep")


class Prog:
    def __init__(self):
        self.ops = []
        self.pool_as = POOL_AS

    def add(self, eng, fn, reads=(), writes=(), dma=None):
        if dma is None and eng == "pool" and self.pool_as:
            eng = self.pool_as
        op = Op()
        op.eng = eng
        op.fn = fn
        op.dma = dma
        op.idx = len(self.ops)
        op.deps = {}
        op.has_dep = False
        op.sig = None

        def dep(d, kind):
            if d is None or d is op:
                return
            if op.deps.get(d) != "raw":
                op.deps[d] = kind

        for r in reads:
            dep(r.writer, "raw")
        for w in writes:
            dep(w.writer, "waw")
            for rd in w.readers:
                dep(rd, "war")
        k = (op.eng, op.dma)
        for r in reads:
            r.readers = [x for x in r.readers if (x.eng, x.dma) != k]
            r.readers.append(op)
        for w in writes:
            w.writer = op
            w.readers = []
        self.ops.append(op)
        return op

    def finalize(self):
        for op in self.ops:
            keep = {}
            for d, kind in op.deps.items():
                if d.dma is None and op.dma is None and d.eng == op.eng:
                    if op.eng == "pe":
                        continue
                keep[d] = kind
            op.deps = keep
            for d in keep:
                d.has_dep = True
        cnt = {}
        waited = {}
        for op in self.ops:
            w = {}
            for d in op.deps:
                key = d.dma if d.dma else d.eng
                val = cnt[key] if d.dma else d.sig
                if w.get(key, 0) < val:
                    w[key] = val
            q = waited.setdefault(op.eng, {})
            op.waits = []
            for kk, v in w.items():
                if q.get(kk, 0) < v:
                    q[kk] = v
                    op.waits.append((kk, v))
            if op.dma:
                cnt[op.dma] = cnt.get(op.dma, 0) + 16
                op.sig = cnt[op.dma]
            elif op.has_dep:
                cnt[op.eng] = cnt.get(op.eng, 0) + 1
                op.sig = cnt[op.eng]
        self.cnt = cnt

    def emit(self, nc, st):
        self.finalize()
        keys = set()
        for op in self.ops:
            for kk, _ in op.waits:
                keys.add(kk)
            if op.dma:
                keys.add(op.dma)
            elif op.sig is not None:
                keys.add(op.eng)
        sems = {kk: st.enter_context(nc.semaphore("s_" + kk)) for kk in sorted(keys)}
        block = st.enter_context(nc.Block())
        ops = self.ops

        def run(name):
            def body(e):
                for op in ops:
                    if op.eng != name:
                        continue
                    for kk, v in op.waits:
                        e.wait_ge(sems[kk], v)
                    ins = op.fn(e)
                    if op.sig is not None:
                        ins.then_inc(sems[op.dma or op.eng], 16 if op.dma else 1)

            return body

        block.tensor(run("pe"))
        block.scalar(run("act"))
        block.vector(run("dve"))
        block.gpsimd(run("pool"))
        block.sync(run("sp"))
        return len(sems)


def _bucket(dist):
    n = np.maximum(dist, 0)
    max_exact = 16
    nf = np.maximum(n, 1).astype(np.float32)
    large = max_exact + (np.log(nf / max_exact) / math.log(128 / max_exact) * (32 - max_exact)).astype(np.int32)
    large = np.minimum(large, 31)
    return np.where(n < max_exact, n, large)


def _bucket_jax_exact():
    import jax
    import jax.numpy as jnp

    with jax.default_device(jax.devices("cpu")[0]):
        dist = jnp.arange(0, 256)
        n = jnp.maximum(dist, 0)
        nf = jnp.maximum(n, 1).astype(jnp.float32)
        large = 16 + (jnp.log(nf / 16) / math.log(128 / 16) * 16).astype(jnp.int32)
        large = jnp.minimum(large, 31)
        return np.asarray(jnp.where(n < 16, n, large))


def make_consts():
    c = {}
    c["c_identf"] = np.eye(128, dtype=np.float32)
    try:
        bk = _bucket_jax_exact()
    except Exception:
        bk = _bucket(np.arange(256))
    oh = np.zeros((33, 384), np.float32)
    for m in range(384):
        dist = m - 128
        if dist < 0:
            oh[32, m] = 8.0
        else:
            oh[bk[dist], m] = 8.0
    c["c_onehot8"] = oh
    j = np.arange(64)[:, None]
    t = np.arange(64)[None, :]
    strict = (j < t).astype(np.float32)
    incl = (j <= t).astype(np.float32)
    c["c_maskT2"] = np.tile(np.concatenate([strict, incl], axis=1), (2, 1))
    c["c_maskL"] = np.tile((t < j).astype(np.float32), (2, 1))
    sm = np.ones((128, 1024), np.float32)
    sm[:, ::64] = 0.0
    c["c_scanmask"] = sm
    return c


def host_params(inp):
    g = np.asarray(inp["norm_g"], np.float32)
    gT = g.reshape(L, 8, 128).transpose(2, 0, 1).reshape(128, L * 8)
    sublnT = np.asarray(inp["subln_g"], np.float32).T
    convT = np.asarray(inp["conv_w"], np.float32).reshape(L, 3, 2, 128).transpose(3, 0, 1, 2).reshape(128, L * 6)
    prm128 = np.ascontiguousarray(np.concatenate([gT, sublnT, convT], axis=1))
    fg = np.ascontiguousarray(np.broadcast_to(np.asarray(inp["final_norm_g"], np.float32)[None, :], (128, D)))
    lamrep = np.ascontiguousarray(np.broadcast_to(np.asarray(inp["lam_qk"], np.float32).reshape(1, L * 256), (128, L * 256)))
    mu = np.asarray(inp["rwkv_mu"], np.float32)
    mu_rkv = mu[:, :768].reshape(L, 3, 4, 64).transpose(3, 0, 1, 2).reshape(64, L * 12)
    mu_wa = mu[:, 768:896].reshape(L, 2, 64).transpose(2, 0, 1).reshape(64, L * 2)

    def ch(a):
        return np.asarray(a, np.float32).reshape(L, 4, 64).transpose(2, 0, 1).reshape(64, L * 4)

    prm64 = np.ascontiguousarray(np.concatenate(
        [mu_rkv, mu_wa, ch(inp["w0"]), ch(inp["a0"]), ch(inp["k_k"]), ch(inp["k_a"]),
         ch(np.asarray(inp["r_k"]).reshape(L, 256)), ch(inp["lnx_g"]), ch(inp["lnx_b"])], axis=1))
    prm64 = np.ascontiguousarray(np.tile(prm64, (2, 1)))
    return prm128, fg, lamrep, prm64


def build(depth=L, taps=(), do_rwkv=True, tap_layer=0, stop_after=None):
    nc = bass.Bass("TRN2", target_bir_lowering=False)
    P = Prog()
    dram_in = lambda n, s, d=F32: nc.dram_tensor(n, list(s), d, kind="ExternalInput")
    x_t = dram_in("x", [S, D])
    win_t = dram_in("w_in", [L, D, INC])
    wout_t = dram_in("w_out", [L, D, D])
    relb_t = dram_in("rel_bias", [32, 4])
    wup_t = dram_in("w_up", [L, 64, 256])
    aup_t = dram_in("a_up", [L, 64, 256])
    prm128_t = dram_in("prm128", [128, 60])
    fg_t = dram_in("fg", [128, D])
    lamrep_t = dram_in("lamrep", [128, L * 256])
    prm64_t = dram_in("prm64", [128, 168])
    cidf_t = dram_in("c_identf", [128, 128])
    coh_t = dram_in("c_onehot8", [33, 384])
    cm2_t = dram_in("c_maskT2", [128, 128])
    cml_t = dram_in("c_maskL", [128, 64])
    csm_t = dram_in("c_scanmask", [128, 1024])
    out_t = nc.dram_tensor("out", [S, D], F32, kind="ExternalOutput")
    xs_t = nc.dram_tensor("xs", [S, D], F32, kind="Internal")
    pt_t = nc.dram_tensor("ptf", [INC, S], F32, kind="Internal")
    vtok_t = nc.dram_tensor("vtok", [S, 512], BF16, kind="Internal")
    gsc_t = nc.dram_tensor("gsc", [4, 130 * 384], F32, kind="Internal")
    tap_t = {}
    if "pt" in taps:
        tap_t["pt"] = nc.dram_tensor("tap_pt", [INC, S], F32, kind="ExternalOutput")
    if "mixed" in taps:
        tap_t["mixed"] = nc.dram_tensor("tap_mixed", [D, S], BF16, kind="ExternalOutput")
    if "xs" in taps:
        tap_t["xs"] = nc.dram_tensor("tap_xs", [S, D], F32, kind="ExternalOutput")

    x_d, win_d, wout_d = x_t.ap(), win_t.ap(), wout_t.ap()
    out_d, xs_d, pt_d, vtok_d = out_t.ap(), xs_t.ap(), pt_t.ap(), vtok_t.ap()

    xs_res = [Res(f"xs{i}") for i in range(NT)]
    pt_res = [Res(f"pt{j}") for j in range(33)]
    vtok_res = [Res(f"vt{i}") for i in range(NT)]
    out_res = Res("out")
    gsc_res = Res("gsc")

    with ExitStack() as st:
        sb = lambda n, s, d=F32: st.enter_context(nc.sbuf_tensor(n, list(s), d))
        big = sb("big", [128, 8, S], BF16)
        big_res = [[Res(f"big{k}_{g}") for g in range(NG)] for k in range(8)]
        ar = sb("arena", [128, 32768], F32)
        ARES = [Res(f"ar{i}") for i in range(128)]
        ps_all = st.enter_context(nc.psum_tensor("psall", [128, 8 * 512], F32))
        PSR = [Res(f"ps{i}") for i in range(8)]

        def psb(bank, nb=1, parts=128):
            return ps_all[0:parts, bank * 512:(bank + nb) * 512], PSR[bank:bank + nb]

        class Arena:
            def __init__(self):
                self.p = 0

            def reset(self, p=0):
                self.p = p

            def alloc(self, n, dtype=F32, parts=128):
                nf = n if dtype is F32 else (n + 1) // 2
                nf = (nf + 1) // 2 * 2
                off = self.p
                self.p += nf
                assert self.p <= 32768, self.p
                ap = ar[0:parts, off:off + nf]
                if dtype is BF16:
                    ap = ap.bitcast(BF16)[:, 0:n]
                else:
                    ap = ap[:, 0:n]
                return ap, ARES[off // 256:(off + nf - 1) // 256 + 1]

        A = Arena()

        identf = sb("identf", [128, 128])
        identb = sb("identb", [128, 128], BF16)
        onesb = sb("onesb", [128, 128], BF16)
        ones64 = sb("ones64", [128, 64])
        ones64r = sb("ones64r", [128, 64])
        prm128 = sb("prm128s", [128, 60])
        prm64 = sb("prm64s", [128, 168])
        nprm64 = sb("nprm64", [128, 32])
        lamv = sb("lamv", [128, L * 8])
        wup = sb("wups", [128, 256])
        aup = sb("aups", [128, 256])
        wua_r = Res("wua")
        maskT2p = sb("maskT2p", [128, 128])
        maskT2n = sb("maskT2n", [128, 128])
        maskLn = sb("maskLn", [128, 64])
        bblk = sb("bblk", [128, 4, 3, 128], BF16)
        cfar = sb("cfar", [128, 8])
        ss = sb("ss", [128, 3 * NT])
        cst = Res("consts")
        ss_res = [Res(f"ss{i}") for i in range(NT)]
        sst_res = [[Res(f"sst{a}_{n}") for n in range(4)] for a in range(2)]

        def dma(eng, out, in_, reads, writes, key):
            P.add(eng, lambda e: e.dma_start(out=out, in_=in_), reads, writes, dma=key)

        dma("sp", identf[:], cidf_t.ap(), [], [cst], "c0")
        dma("sp", prm128[:], prm128_t.ap(), [], [cst], "c0")
        dma("sp", prm64[:], prm64_t.ap(), [], [cst], "c0")
        lamrep, lamrep_r = A.alloc(L * 256)
        dma("sp", lamrep, lamrep_t.ap(), [], lamrep_r, "c0")
        dma("sp", maskT2p[:], cm2_t.ap(), [], [cst], "c0")
        dma("sp", maskLn[:], cml_t.ap(), [], [cst], "c0")
        onehot, onehot_r = A.alloc(384, parts=64)
        relb, relb_rr = A.alloc(128, parts=64)
        gsb, gsb_rr = A.alloc(384, parts=4)
        relb_r = Res("relb")
        P.add("pool", lambda e: e.memset(onehot[:], 0.0), [], onehot_r)
        dma("sp", onehot[0:33, :], coh_t.ap(), [], onehot_r, "c0")
        P.add("pool", lambda e: e.memset(relb[:], 0.0), [], [relb_r] + relb_rr)
        dma("sp", relb[0:32, 0:4], relb_t.ap(), [], [relb_r], "c1")
        cst2 = Res("consts2")
        P.add("dve", lambda e: e.tensor_copy(out=identb[:], in_=identf[:]), [cst], [cst2])
        P.add("pool", lambda e: e.memset(onesb[:], 1.0), [], [cst2])
        P.add("pool", lambda e: e.memset(ones64[:], 1.0 / 64), [], [cst2])
        P.add("pool", lambda e: e.memset(ones64r[:], 1.0), [], [cst2])
        P.add("dve", lambda e: e.tensor_scalar(out=maskT2n[:], in0=maskT2p[:], scalar1=-1.0, scalar2=None, op0=ALU.mult), [cst], [cst2])
        P.add("dve", lambda e: e.tensor_scalar(out=maskLn[:], in0=maskLn[:], scalar1=-1.0, scalar2=None, op0=ALU.mult), [cst], [cst2])
        P.add("dve", lambda e: e.tensor_scalar(out=nprm64[:], in0=prm64[:, 56:88], scalar1=-1.0, scalar2=None, op0=ALU.mult), [cst], [cst2])
        P.add("pool", lambda e: e.memset(relb[32:33, 0:4], NEG8 / 8.0), [relb_r], [relb_r])

        lam_r = Res("lam")
        for l in range(depth):
            lq = lamrep[:, l * 256:(l + 1) * 256]
            tmpa, tmpr = A.alloc(128)
            tmpr = tmpr + lamrep_r
            for pr in range(2):
                P.add("dve", (lambda pr=pr, lq=lq, tmpa=tmpa: lambda e: e.tensor_tensor(out=tmpa[:, pr * 64:(pr + 1) * 64], in0=lq[:, (2 * pr) * 64:(2 * pr + 1) * 64], in1=lq[:, (2 * pr + 1) * 64:(2 * pr + 2) * 64], op=ALU.mult))(), [cst], tmpr)
                P.add("dve", (lambda pr=pr, l=l, tmpa=tmpa: lambda e: e.reduce_sum(out=lamv[:, l * 8 + pr:l * 8 + pr + 1], in_=tmpa[:, pr * 64:(pr + 1) * 64], axis=AX.X))(), tmpr, [lam_r])
            P.add("act", (lambda l=l: lambda e: e.activation(out=lamv[:, l * 8 + 2:l * 8 + 4], in_=lamv[:, l * 8:l * 8 + 2], func=AF.Exp))(), [lam_r], [lam_r])
            linit = 0.8 - 0.6 * math.exp(-0.3 * l)
            P.add("dve", (lambda l=l: lambda e: e.tensor_tensor(out=lamv[:, l * 8 + 4:l * 8 + 5], in0=lamv[:, l * 8 + 2:l * 8 + 3], in1=lamv[:, l * 8 + 3:l * 8 + 4], op=ALU.subtract))(), [lam_r], [lam_r])
            P.add("dve", (lambda l=l, linit=linit: lambda e: e.tensor_scalar(out=lamv[:, l * 8 + 5:l * 8 + 6], in0=lamv[:, l * 8 + 4:l * 8 + 5], scalar1=linit, scalar2=-1.0, op0=ALU.add, op1=ALU.mult))(), [lam_r], [lam_r])

        (pg, pgr) = psb(0)
        P.add("pe", lambda e: e.matmul(pg[:, 0:384], relb[:, :], onehot[:, :], start=True, stop=True), [relb_r] + onehot_r, pgr)
        gsb_r = Res("gsb")
        P.add("dve", lambda e: e.tensor_copy(out=gsb[:], in_=pg[0:4, 0:384]), pgr, [gsb_r] + gsb_rr)
        dma("sp", gsc_t.ap().rearrange("h (r n) -> h r n", n=384), gsb.unsqueeze(1).broadcast_to([4, 130, 384]), [gsb_r], [gsc_res], "c2")
        tdo, tdo_r = A.alloc(4 * 2 * 128)
        tdo4 = tdo.rearrange("p (h a q) -> p h a q", h=4, a=2)
        for h in range(4):
            for a_, base in ((0, 128), (1, 256)):
                dma("sp", tdo4[:, h, a_, :], bass.AP(gsc_t, h * 130 * 384 + base, [[383, 128], [1, 128]]), [gsc_res], tdo_r, "c3")
            dma("sp", cfar[:, h:h + 1], bass.AP(gsc_t, h * 130 * 384 + 383, [[0, 128], [1, 1]]), [gsc_res], [cst2], "c3")
        P.add("dve", lambda e: e.tensor_scalar(out=cfar[:, 4:8], in0=cfar[:, 0:4], scalar1=SCALE, scalar2=None, op0=ALU.mult), [cst2], [cst2])
        zt, zt_r = A.alloc(128)
        P.add("pool", lambda e: e.memset(zt[:], 0.0), [], zt_r)
        bias_r = Res("biasT")
        for h in range(4):
            P.add("dve", (lambda h=h: lambda e: e.tensor_copy(out=bblk[:, h, 0, :], in_=tdo4[:, h, 0, :]))(), tdo_r, [bias_r])
            P.add("pool", (lambda h=h: lambda e: e.tensor_copy(out=bblk[:, h, 1, :], in_=tdo4[:, h, 1, :]))(), tdo_r, [bias_r])
            P.add("dve", (lambda h=h: lambda e: e.tensor_scalar(out=bblk[:, h, 2, :], in0=zt[:], scalar1=cfar[:, h:h + 1], scalar2=None, op0=ALU.add))(), zt_r + [cst2], [bias_r])

        def TT(eng, out, a, b, op, rd, wr):
            P.add(eng, lambda e: e.tensor_tensor(out=out, in0=a, in1=b, op=op), rd, wr)

        def TS(eng, out, a, s1, s2, op0, op1, rd, wr):
            if s2 is None:
                P.add(eng, lambda e: e.tensor_scalar(out=out, in0=a, scalar1=s1, scalar2=None, op0=op0), rd, wr)
            else:
                P.add(eng, lambda e: e.tensor_scalar(out=out, in0=a, scalar1=s1, scalar2=s2, op0=op0, op1=op1), rd, wr)

        def STT(eng, out, a, sc, b, op0, op1, rd, wr):
            P.add(eng, lambda e: e.scalar_tensor_tensor(out=out, in0=a, scalar=sc, in1=b, op0=op0, op1=op1), rd, wr)

        def ACTF(out, in_, func, rd, wr, scale=1.0, bias=None):
            if bias is None:
                P.add("act", lambda e: e.activation(out=out, in_=in_, func=func, scale=scale), rd, wr)
            else:
                P.add("act", lambda e: e.activation(out=out, in_=in_, func=func, scale=scale, bias=bias), rd, wr)

        def MM(out, lhsT, rhs, st_, sp_, rd, wr):
            P.add("pe", lambda e: e.matmul(out, lhsT, rhs, start=st_, stop=sp_), rd, wr)

        def CP(eng, out, in_, rd, wr):
            if eng == "act":
                P.add("act", lambda e: e.activation(out=out, in_=in_, func=AF.Copy), rd, wr)
            else:
                P.add(eng, lambda e: e.tensor_copy(out=out, in_=in_), rd, wr)

        def RCP(out, in_, rd, wr):
            P.add("dve", lambda e: e.reciprocal(out=out, in_=in_), rd, wr)

        V = lambda buf, w: buf[0].rearrange("c (p t) -> c p t", p=16)
        V16 = lambda ap: ap.rearrange("c (p t) -> c p t", p=16)
        V4 = lambda ap: ap.rearrange("c (h t) -> c h t", h=4)
        pbk_state = [0]

        def pbk(n):
            if pbk_state[0] + n > 8:
                pbk_state[0] = 0
            b = pbk_state[0]
            pbk_state[0] += n
            return psb(b, n, parts=128)

        def rwkv_phase(l):
            P.pool_as = RWKV_POOL_AS
            A.reset()
            al = lambda n, dt=F32: A.alloc(n, dt, parts=128)
            HF = lambda ap, hf: ap[64 * hf:64 * hf + 64]

            def sel(x, hf):
                return x[hf] if isinstance(x, tuple) else HF(x, hf)

            def MM2(out, lhsT, rhs, st_, sp_, rd, wr):
                for hf in range(2):
                    MM(sel(out, hf), sel(lhsT, hf), sel(rhs, hf), st_, sp_, rd, wr)

            scanmask = al(1024)
            Sst, Sst_r0 = al(2048)
            Sst5 = Sst.rearrange("c (a n h v) -> c a n h v", a=2, n=4, h=4)
            Sfin = al(256)
            id2 = al(64)
            gC = al(16)
            KR = al(2048)
            kc, vc = al(1024), al(1024)
            pv = [al(1024), al(1024)]
            X1, X2, X4, C1, T, kt, bt, RK = [al(1024) for _ in range(8)]
            wdc, adc, wdp, adp, th = [al(256) for _ in range(5)]
            UWrhs, NAbT, AkT, UW = [al(2048) for _ in range(4)]
            Vtok, Khtok, Bntok, NAkb = [al(1024) for _ in range(4)]
            KR0 = (KR[0][:, 0:1024], KR[1][0:4])
            KR1 = (KR[0][:, 1024:2048], KR[1][4:8])
            KR4 = KR[0].rearrange("c (q p t) -> c q p t", q=2, p=16)
            idb = (identf[0:64, 0:64], identf[64:128, 64:128])
            id_bc = id2[0].unsqueeze(1).broadcast_to([128, 16, 64])
            on_r = (ones64r[0:64, :], ones64r[64:128, :])
            on_m = (ones64[0:64, :], ones64[64:128, :])

            def bc(col):
                return prm64[:, col:col + 4].unsqueeze(2).broadcast_to([128, 4, 256])

            dma("sp", scanmask[0], csm_t.ap(), [], scanmask[1], "rw_c")
            for hf in range(2):
                dma("sp", HF(wup[:], hf), wup_t.ap()[l], [], [wua_r], "rw_c")
                dma("sp", HF(aup[:], hf), aup_t.ap()[l], [], [wua_r], "rw_c")
            P.add("dve", lambda e: e.tensor_copy(out=id2[0][0:64, :], in_=identf[0:64, 0:64]), [cst], id2[1])
            P.add("dve", lambda e: e.tensor_copy(out=id2[0][64:128, :], in_=identf[64:128, 64:128]), [cst], id2[1])
            P.add("dve", lambda e: e.memset(Sst5[0:64, 0, 0, :, :], 0.0), [], [sst_res[0][0]] + Sst_r0)

            def src(row0, c0, c1):
                return pt_d[row0:row0 + 256, c0:c1].rearrange("(h c) t -> c h t", c=64)

            for gp in range(8):
                par = gp % 2
                tb = gp * 512
                curs = [KR1, kc, vc]
                for q in range(3):
                    row0 = 3072 + 256 * q
                    prs = [pt_res[row0 // 128], pt_res[row0 // 128 + 1]]
                    cur = curs[q]
                    pvb = pv[q % 2]
                    for hf in range(2):
                        t0 = tb + 256 * hf
                        dma("sp", V4(HF(cur[0], hf)), src(row0, t0, t0 + 256), prs, cur[1], f"rw_c{q}")
                        if t0 == 0:
                            P.add("dve", (lambda pvb=pvb: lambda e: e.memset(V4(HF(pvb[0], 0))[:, :, 0:1], 0.0))(), [], pvb[1])
                            dma("sp", V4(HF(pvb[0], hf))[:, :, 1:256], src(row0, 0, 255), prs, pvb[1], f"rw_p{q % 2}")
                        else:
                            dma("sp", V4(HF(pvb[0], hf)), src(row0, t0 - 1, t0 + 255), prs, pvb[1], f"rw_p{q % 2}")
                    TT("dve", pvb[0], pvb[0], cur[0], ALU.subtract, pvb[1] + cur[1], pvb[1])
                    TT("dve", V4(pvb[0]), V4(pvb[0]), bc(l * 12 + q * 4), ALU.mult, pvb[1] + [cst], pvb[1])
                    TT("dve", cur[0], cur[0], pvb[0], ALU.add, pvb[1] + cur[1], cur[1])
                for (cur, prv, row0, mcol, wk) in ((wdc, wdp, 3840, 48 + l * 2, 0), (adc, adp, 3904, 48 + l * 2 + 1, 1)):
                    for hf in range(2):
                        t0 = tb + 256 * hf
                        dma("sp", HF(cur[0], hf), pt_d[row0:row0 + 64, t0:t0 + 256], [pt_res[30]], cur[1], f"rw_wc{wk}")
                        if t0 == 0:
                            P.add("dve", (lambda prv=prv: lambda e: e.memset(HF(prv[0], 0)[:, 0:1], 0.0))(), [], prv[1])
                            dma("sp", HF(prv[0], hf)[:, 1:256], pt_d[row0:row0 + 64, 0:255], [pt_res[30]], prv[1], f"rw_wp{wk}")
                        else:
                            dma("sp", HF(prv[0], hf), pt_d[row0:row0 + 64, t0 - 1:t0 + 255], [pt_res[30]], prv[1], f"rw_wp{wk}")
                    TT("dve", prv[0], prv[0], cur[0], ALU.subtract, prv[1] + cur[1], prv[1])
                    STT("dve", cur[0], prv[0], prm64[:, mcol:mcol + 1], cur[0], ALU.mult, ALU.add, prv[1] + cur[1] + [cst], cur[1])
                ACTF(th[0], wdc[0], AF.Exp, wdc[1], th[1], scale=2.0)
                TS("dve", th[0], th[0], 1.0, None, ALU.add, None, th[1], th[1])
                RCP(th[0], th[0], th[1], th[1])
                TS("dve", th[0], th[0], -2.0, 1.0, ALU.mult, ALU.add, th[1], th[1])
                ups, upr = pbk(2)
                for h in range(4):
                    MM2(ups[:, h * 256:(h + 1) * 256], wup[:, 64 * h:64 * h + 64], th[0], True, True, th[1] + [wua_r], upr)
                for h in range(4):
                    ACTF(V4(X1[0])[:, h, :], ups[:, h * 256:(h + 1) * 256], AF.Exp, upr + [cst2], X1[1], scale=-1.0, bias=nprm64[:, l * 4 + h:l * 4 + h + 1])
                TS("dve", X1[0], X1[0], 1.0, None, ALU.add, None, X1[1], X1[1])
                RCP(X1[0], X1[0], X1[1], X1[1])
                TS("dve", X1[0], X1[0], -0.6065306597126334, None, ALU.mult, None, X1[1], X1[1])
                aps, apr = pbk(2)
                for h in range(4):
                    MM2(aps[:, h * 256:(h + 1) * 256], aup[:, 64 * h:64 * h + 64], adc[0], True, True, adc[1] + [wua_r], apr)
                for h in range(4):
                    ACTF(V4(X2[0])[:, h, :], aps[:, h * 256:(h + 1) * 256], AF.Exp, apr + [cst2], X2[1], scale=-1.0, bias=nprm64[:, 16 + l * 4 + h:16 + l * 4 + h + 1])
                TS("dve", X2[0], X2[0], 1.0, None, ALU.add, None, X2[1], X2[1])
                RCP(X2[0], X2[0], X2[1], X2[1])
                TT("dve", V4(KR0[0]), V4(kc[0]), bc(88 + l * 4), ALU.mult, kc[1] + [cst], KR0[1])
                ACTF(X4[0], KR0[0], AF.Square, KR0[1], X4[1])
                sps, spr = pbk(2)
                for hc in range(2):
                    MM2(sps[:, hc * 512:(hc + 1) * 512], on_r, X4[0][:, hc * 512:(hc + 1) * 512], True, True, X4[1] + [cst2], [spr[hc]])
                TS("dve", X4[0], sps, 1e-24, None, ALU.max, None, spr, X4[1])
                ACTF(X4[0], X4[0], AF.Ln, X4[1], X4[1])
                ACTF(X4[0], X4[0], AF.Exp, X4[1], X4[1], scale=-0.5)
                TT("dve", KR0[0], KR0[0], X4[0], ALU.mult, KR0[1] + X4[1], KR0[1])
                STT("dve", V4(X4[0]), V4(X2[0]), -1.0, bc(104 + l * 4), ALU.add, ALU.mult, X2[1] + [cst], X4[1])
                STT("dve", kc[0], X4[0], 1.0, kc[0], ALU.add, ALU.mult, X4[1] + kc[1], kc[1])
                TT("dve", RK[0], KR1[0], kc[0], ALU.mult, KR1[1] + kc[1], RK[1])
                TT("dve", V4(RK[0]), V4(RK[0]), bc(120 + l * 4), ALU.mult, RK[1] + [cst], RK[1])
                TT("dve", X2[0], KR0[0], X2[0], ALU.mult, KR0[1] + X2[1], X2[1])
                P.add("dve", lambda e: e.tensor_tensor_scan(out=C1[0], data0=scanmask[0], data1=X1[0], initial=0.0, op0=ALU.mult, op1=ALU.add), scanmask[1] + X1[1], C1[1])
                ACTF(T[0], C1[0], AF.Exp, C1[1], T[1])
                CP("dve", gC[0], V16(T[0])[:, :, 63], T[1], gC[1])
                TT("dve", KR1[0], KR1[0], T[0], ALU.mult, KR1[1] + T[1], KR1[1])
                ACTF(T[0], C1[0], AF.Exp, C1[1], T[1], scale=-1.0)
                TT("dve", kt[0], kc[0], T[0], ALU.mult, kc[1] + T[1], kt[1])
                TT("dve", bt[0], X2[0], T[0], ALU.mult, X2[1] + T[1], bt[1])
                TT("dve", T[0], C1[0], X1[0], ALU.subtract, C1[1] + X1[1], T[1])
                ACTF(T[0], T[0], AF.Exp, T[1], T[1])
                TT("dve", KR0[0], KR0[0], T[0], ALU.mult, KR0[1] + T[1], KR0[1])
                TT("dve", V16(T[0]), V16(C1[0])[:, :, 63:64].broadcast_to([128, 16, 64]), V16(C1[0]), ALU.subtract, C1[1], T[1])
                ACTF(T[0], T[0], AF.Exp, T[1], T[1])
                TT("dve", kc[0], kc[0], T[0], ALU.mult, kc[1] + T[1], kc[1])
                STT("dve", X2[0], X2[0], -1.0, T[0], ALU.mult, ALU.mult, X2[1] + T[1], X2[1])

                for (srcb, dstv, dstr) in ((KR0, V(UWrhs, 128)[:, :, 64:128], UWrhs[1]), (vc, V16(Vtok[0]), Vtok[1]), (kc, V16(Khtok[0]), Khtok[1]), (X2, V16(Bntok[0]), Bntok[1])):
                    tp, tpr = pbk(2)
                    for p in range(16):
                        MM2(tp[:, p * 64:(p + 1) * 64], V16(srcb[0])[:, p, :], idb, True, True, srcb[1] + [cst], [tpr[p // 8]])
                    CP("act", dstv, V16(tp), tpr, dstr)
                for (lh, dst, msk) in ((bt, NAbT, maskT2n), (kt, AkT, maskT2p)):
                    ap_, apr_ = pbk(4)
                    for p in range(16):
                        MM2(ap_[:, p * 128:(p + 1) * 128], V16(lh[0])[:, p, :], KR4[:, :, p, :], True, True, lh[1] + KR[1], [apr_[p // 4]])
                    TT("dve", V(dst, 128), ap_.rearrange("c (p t) -> c p t", p=16), msk[:].unsqueeze(1).broadcast_to([128, 16, 128]), ALU.mult, apr_ + [cst, cst2], dst[1])
                ap_, apr_ = pbk(2)
                for p in range(16):
                    MM2(ap_[:, p * 64:(p + 1) * 64], KR4[:, 0, p, :], V16(bt[0])[:, p, :], True, True, bt[1] + KR0[1], [apr_[p // 8]])
                TT("dve", V16(NAkb[0]), V16(ap_), maskLn[:].unsqueeze(1).broadcast_to([128, 16, 64]), ALU.mult, apr_ + [cst2], NAkb[1])
                NAb3 = V(NAbT, 128)
                R = T
                TT("dve", V16(R[0]), NAb3[:, :, 0:64], id_bc, ALU.add, NAbT[1] + id2[1], R[1])
                Pprev = (NAb3[:, :, 0:64], NAbT[1])
                PTprev = (V16(NAkb[0]), NAkb[1])
                Pbufs = [X1, X4]
                PTbufs = [C1, NAkb]
                for k in range(1, 6):
                    Pn = Pbufs[(k - 1) % 2]
                    PTn = PTbufs[(k - 1) % 2]
                    if k < 5:
                        pp, ppr = pbk(2)
                        for p in range(16):
                            MM2(pp[:, p * 64:(p + 1) * 64], PTprev[0][:, p, :], Pprev[0][:, p, :], True, True, PTprev[1] + Pprev[1], [ppr[p // 8]])
                    pt2, pt2r = pbk(2)
                    for p in range(16):
                        MM2(pt2[:, p * 64:(p + 1) * 64], Pprev[0][:, p, :], PTprev[0][:, p, :], True, True, PTprev[1] + Pprev[1], [pt2r[p // 8]])
                    if k < 5:
                        CP("act", Pn[0], pp, ppr, Pn[1])
                    CP("dve", PTn[0], pt2, pt2r, PTn[1])
                    rr, rrr = pbk(2)
                    for p in range(16):
                        MM2(rr[:, p * 64:(p + 1) * 64], V16(PTn[0])[:, p, :], V16(R[0])[:, p, :], True, True, PTn[1] + R[1], [rrr[p // 8]])
                    TT("dve", R[0], rr, R[0], ALU.add, rrr + R[1], R[1])
                    Pprev = (V16(Pn[0]), Pn[1])
                    PTprev = (V16(PTn[0]), PTn[1])
                xp, xpr = pbk(2)
                for p in range(16):
                    MM2(xp[:, p * 64:(p + 1) * 64], V(AkT, 128)[:, p, 0:64], V16(Vtok[0])[:, p, :], True, True, AkT[1] + Vtok[1], [xpr[p // 8]])
                CP("act", V(UWrhs, 128)[:, :, 0:64], V16(xp), xpr, UWrhs[1])
                up4, up4r = pbk(4)
                for p in range(16):
                    MM2(up4[:, p * 128:(p + 1) * 128], V16(R[0])[:, p, :], V(UWrhs, 128)[:, p, :], True, True, R[1] + UWrhs[1], [up4r[p // 4]])
                CP("act", UW[0][:, 0:1024], up4[:, 0:1024], up4r[0:2], UW[1][0:4])
                CP("dve", UW[0][:, 1024:2048], up4[:, 1024:2048], up4r[2:4], UW[1][4:8])
                UW3 = V(UW, 128)
                Qs, PTs, Dg, GT, Ysb, zr, Gg = X1, X4, pv[0], pv[1], kt, bt, C1
                qp, qpr = pbk(2)
                for p in range(16):
                    MM2(qp[:, p * 64:(p + 1) * 64], V16(Khtok[0])[:, p, :], V16(Vtok[0])[:, p, :], True, False, Khtok[1] + Vtok[1], [qpr[p // 8]])
                    MM2(qp[:, p * 64:(p + 1) * 64], V16(Bntok[0])[:, p, :], UW3[:, p, 0:64], False, True, Bntok[1] + UW[1], [qpr[p // 8]])
                CP("act", Qs[0], qp, qpr, Qs[1])
                pp2, pp2r = pbk(2)
                for p in range(16):
                    MM2(pp2[:, p * 64:(p + 1) * 64], UW3[:, p, 64:128], V16(Bntok[0])[:, p, :], True, True, Bntok[1] + UW[1], [pp2r[p // 8]])
                TT("dve", V16(Dg[0]), id_bc, gC[0].unsqueeze(2).broadcast_to([128, 16, 64]), ALU.mult, gC[1] + id2[1], Dg[1])
                TT("dve", PTs[0], pp2, Dg[0], ALU.add, pp2r + Dg[1], PTs[1])
                gp_, gpr = pbk(2)
                for p in range(16):
                    MM2(gp_[:, p * 64:(p + 1) * 64], UW3[:, p, 64:128], NAb3[:, p, 64:128], True, True, UW[1] + NAbT[1], [gpr[p // 8]])
                TT("dve", GT[0], gp_, KR1[0], ALU.add, gpr + KR1[1], GT[1])
                Q4 = Qs[0].rearrange("c (h n v) -> c h n v", h=4, n=4)
                for hf in range(2):
                    for n in range(4):
                        sp_, spr_ = pbk(1)
                        for h in range(4):
                            MM(HF(sp_, hf)[:, h * 64:(h + 1) * 64], HF(V16(PTs[0]), hf)[:, h * 4 + n, :], HF(Sst5, hf)[:, par, n, h, :], True, True, PTs[1] + [sst_res[par][n]], spr_)
                        src_ps = HF(sp_, hf)[:, 0:256].rearrange("c (h v) -> c h v", h=4)
                        if n < 3:
                            TT("dve", HF(Sst5, hf)[:, par, n + 1, :, :], src_ps, HF(Q4, hf)[:, :, n, :], ALU.add, spr_ + Qs[1], [sst_res[par][n + 1]])
                        else:
                            TT("dve", HF(Sfin[0], hf).rearrange("c (h v) -> c h v", h=4), src_ps, HF(Q4, hf)[:, :, n, :], ALU.add, spr_ + Qs[1], Sfin[1])
                    sh, shr = pbk(1)
                    if hf == 0:
                        MM(sh[64:128, 0:256], identf[0:64, 0:64], Sfin[0][0:64, :], True, True, Sfin[1] + [cst], shr)
                        CP("act", Sst5[64:128, par, 0, :, :], sh[64:128, 0:256].rearrange("c (h v) -> c h v", h=4), shr, [sst_res[par][0]])
                    else:
                        MM(sh[0:64, 0:256], identf[64:128, 64:128], Sfin[0][64:128, :], True, True, Sfin[1] + [cst], shr)
                        CP("act", Sst5[0:64, 1 - par, 0, :, :], sh[0:64, 0:256].rearrange("c (h v) -> c h v", h=4), shr, [sst_res[1 - par][0]])
                yp, ypr = pbk(2)
                for p in range(16):
                    h, n = p // 4, p % 4
                    MM2(yp[:, p * 64:(p + 1) * 64], Sst5[:, par, n, h, :], V16(GT[0])[:, p, :], True, False, [sst_res[par][n]] + GT[1], [ypr[p // 8]])
                    MM2(yp[:, p * 64:(p + 1) * 64], V16(Vtok[0])[:, p, :], V(AkT, 128)[:, p, 64:128], False, False, Vtok[1] + AkT[1], [ypr[p // 8]])
                    MM2(yp[:, p * 64:(p + 1) * 64], UW3[:, p, 0:64], NAb3[:, p, 64:128], False, True, UW[1] + NAbT[1], [ypr[p // 8]])
                CP("act", Ysb[0], yp, ypr, Ysb[1])
                mp, mpr = pbk(2)
                for hc in range(2):
                    MM2(mp[:, hc * 512:(hc + 1) * 512], on_m, Ysb[0][:, hc * 512:(hc + 1) * 512], True, True, Ysb[1] + [cst2], [mpr[hc]])
                TT("dve", Ysb[0], Ysb[0], mp, ALU.subtract, Ysb[1] + mpr, Ysb[1])
                ACTF(T[0], Ysb[0], AF.Square, Ysb[1], T[1])
                vp_, vpr_ = pbk(2)
                for hc in range(2):
                    MM2(vp_[:, hc * 512:(hc + 1) * 512], on_m, T[0][:, hc * 512:(hc + 1) * 512], True, True, T[1] + [cst2], [vpr_[hc]])
                ACTF(X4[0], vp_, AF.Ln, vpr_, X4[1], bias=GN_EPS)
                ACTF(X4[0], X4[0], AF.Exp, X4[1], X4[1], scale=-0.5)
                TT("dve", Ysb[0], Ysb[0], X4[0], ALU.mult, Ysb[1] + X4[1], Ysb[1])
                for h in range(4):
                    TS("dve", V4(Ysb[0])[:, h, :], V4(Ysb[0])[:, h, :], prm64[:, 136 + l * 4 + h:137 + l * 4 + h], prm64[:, 152 + l * 4 + h:153 + l * 4 + h], ALU.mult, ALU.add, Ysb[1] + [cst], Ysb[1])
                bp, bpr = pbk(2)
                for hc in range(2):
                    MM2(bp[:, hc * 512:(hc + 1) * 512], on_r, RK[0][:, hc * 512:(hc + 1) * 512], True, True, RK[1] + [cst2], [bpr[hc]])
                TT("dve", T[0], bp, vc[0], ALU.mult, bpr + vc[1], T[1])
                TT("dve", Ysb[0], Ysb[0], T[0], ALU.add, Ysb[1] + T[1], Ysb[1])
                for hf in range(2):
                    t0 = tb + 256 * hf
                    dma("sp", V4(HF(zr[0], hf)), src(3968, t0, t0 + 256), [pt_res[31], pt_res[32]], zr[1], "rw_z")
                ACTF(Gg[0], zr[0], AF.Exp, zr[1], Gg[1], scale=-1.0)
                TS("dve", Gg[0], Gg[0], 1.0, None, ALU.add, None, Gg[1], Gg[1])
                RCP(Gg[0], Gg[0], Gg[1], Gg[1])
                TT("dve", Gg[0], Gg[0], zr[0], ALU.mult, Gg[1] + zr[1], Gg[1])
                outb = (Dg[0].bitcast(BF16)[:, 0:1024], Dg[1])
                TT("dve", outb[0], Ysb[0], Gg[0], ALU.mult, Ysb[1] + Gg[1], outb[1])
                for hf in range(2):
                    t0 = tb + 256 * hf
                    for h in range(4):
                        dma("sp", big[64 * (h % 2):64 * (h % 2) + 64, 6 + h // 2, t0:t0 + 256], V4(HF(outb[0], hf))[:, h, :], outb[1], [big_res[6 + h // 2][gp]], "rw_o")


        def emit_layer(l):
            src_d = x_d if l == 0 else xs_d
            if stop_after == 'setup':
                return
            A.reset()
            xt = [A.alloc(D) for _ in range(3)]
            hb = [A.alloc(D, BF16) for _ in range(2)]
            junk = A.alloc(D, BF16)
            for i in range(NT):
                xa, xr = xt[i % 3]
                ha, hr = hb[i % 2]
                dma("sp", xa, src_d[i * 128:(i + 1) * 128, :], [xs_res[i]] if l > 0 else [], xr, f"xt{i % 3}")
                P.add("act", (lambda xa=xa, i=i: lambda e: e.activation(out=junk[0], in_=xa, func=AF.Square, accum_out=ss[:, 3 * i:3 * i + 1]))(), xr, junk[1] + [ss_res[i]])
                P.add("act", (lambda i=i: lambda e: e.activation(out=ss[:, 3 * i + 1:3 * i + 2], in_=ss[:, 3 * i:3 * i + 1], func=AF.Ln, scale=1.0 / D, bias=NORM_EPS))(), [ss_res[i]], [ss_res[i]])
                P.add("act", (lambda i=i: lambda e: e.activation(out=ss[:, 3 * i + 2:3 * i + 3], in_=ss[:, 3 * i + 1:3 * i + 2], func=AF.Exp, scale=-0.5))(), [ss_res[i]], [ss_res[i]])
                P.add("dve", (lambda xa=xa, ha=ha, i=i: lambda e: e.tensor_scalar(out=ha, in0=xa, scalar1=ss[:, 3 * i + 2:3 * i + 3], scalar2=None, op0=ALU.mult))(), xr + [ss_res[i]], hr)
                bank = i % 2
                pa, pr_ = psb(bank)
                pab = pa.bitcast(BF16)
                for k in range(8):
                    P.add("pe", (lambda pab=pab, ha=ha, k=k: lambda e: e.transpose(pab[:, k * 128:(k + 1) * 128], ha[:, k * 128:(k + 1) * 128], identb[:]))(), hr + [cst2], pr_)
                eng = "act" if i % 2 == 0 else "dve"
                dst = big[:, :, i * 128:(i + 1) * 128]
                srcv = pab.rearrange("p (k t) -> p k t", k=8)
                wr = [big_res[k][i // 4] for k in range(8)]
                if eng == "act":
                    P.add("act", (lambda dst=dst, srcv=srcv: lambda e: e.activation(out=dst, in_=srcv, func=AF.Copy))(), pr_, wr)
                else:
                    P.add("dve", (lambda dst=dst, srcv=srcv: lambda e: e.tensor_copy(out=dst, in_=srcv))(), pr_, wr)

            if stop_after == 'A':
                return
            A.reset()
            wst = [A.alloc(8 * 512) for _ in range(2)]
            wbf = [A.alloc(8 * 512, BF16) for _ in range(2)]
            stage = [A.alloc(S) for _ in range(2)]
            vst = [A.alloc(512, BF16) for _ in range(2)]
            allbig = [big_res[k][g] for k in range(8) for g in range(NG)]

            def load_w(r):
                width = 512 if r < 8 else 128
                wa, wr_ = wst[r % 2]
                wb, wbr = wbf[r % 2]
                wa3 = wa.rearrange("p (k c) -> p k c", k=8)
                wb3 = wb.rearrange("p (k c) -> p k c", k=8)
                if "nowload" not in DBG:
                    dma("sp", wa3[:, :, 0:width], win_d[l, :, r * 512:r * 512 + width].rearrange("(k p) c -> p k c", p=128), [], wr_, f"wst{r % 2}")
                else:
                    P.add("dve", lambda e: e.memset(wa3[:, :, 0:width], 0.5), [], wr_)
                for k in range(8):
                    P.add("pool", (lambda wa3=wa3, wb3=wb3, k=k, width=width: lambda e: e.tensor_scalar(out=wb3[:, k, 0:width], in0=wa3[:, k, 0:width], scalar1=prm128[:, l * 8 + k:l * 8 + k + 1], scalar2=None, op0=ALU.mult))(), wr_ + [cst], wbr)

            load_w(0)
            pbank = 0
            ev = 0
            for r in range(9):
                if r + 1 < 9:
                    load_w(r + 1)
                width = 512 if r < 8 else 128
                wb, wbr = wbf[r % 2]
                wb3 = wb.rearrange("p (k c) -> p k c", k=8)
                if r == 2:
                    for i in range(NT):
                        pa, pr_ = psb(4 + pbank % 4)
                        pbank += 1
                        for k in range(8):
                            P.add("pe", (lambda pa=pa, k=k, i=i, wb3=wb3: lambda e: e.matmul(pa, big[:, k, i * 128:(i + 1) * 128], wb3[:, k, :], start=(k == 0), stop=(k == 7)))(), [big_res[k][i // 4], ] + wbr, pr_)
                        va, vr = vst[i % 2]
                        eng = "act" if ev % 2 == 0 else "dve"
                        ev += 1
                        if eng == "act":
                            P.add("act", (lambda va=va, pa=pa: lambda e: e.activation(out=va, in_=pa, func=AF.Copy))(), pr_, vr)
                        else:
                            P.add("dve", (lambda va=va, pa=pa: lambda e: e.tensor_copy(out=va, in_=pa))(), pr_, vr)
                        if "nostore" not in DBG:
                            dma("sp", vtok_d[i * 128:(i + 1) * 128, :], va, vr, [vtok_res[i]], f"vst{i % 2}")
                    continue
                for jj in range(width // 128):
                    j = 4 * r + jj
                    sa, sr = stage[j % 2]
                    for tg in range(NG):
                        pa, pr_ = psb(4 + pbank % 4)
                        pbank += 1
                        for k in range(8):
                            P.add("pe", (lambda pa=pa, k=k, tg=tg, jj=jj, wb3=wb3: lambda e: e.matmul(pa, wb3[:, k, jj * 128:(jj + 1) * 128], big[:, k, tg * 512:(tg + 1) * 512], start=(k == 0), stop=(k == 7)))(), [big_res[k][tg]] + wbr, pr_)
                        eng = "act" if ev % 2 == 0 else "dve"
                        ev += 1
                        sres = sr[tg * 2:(tg + 1) * 2]
                        if eng == "act":
                            P.add("act", (lambda sa=sa, pa=pa, tg=tg: lambda e: e.activation(out=sa[:, tg * 512:(tg + 1) * 512], in_=pa, func=AF.Copy))(), pr_, sres)
                        else:
                            P.add("dve", (lambda sa=sa, pa=pa, tg=tg: lambda e: e.tensor_copy(out=sa[:, tg * 512:(tg + 1) * 512], in_=pa))(), pr_, sres)
                        if "nostore" not in DBG and "nopt" not in DBG:
                            dma("sp", pt_d[j * 128:(j + 1) * 128, tg * 512:(tg + 1) * 512], sa[:, tg * 512:(tg + 1) * 512], sres, [pt_res[j]], f"stg{j % 2}_{tg}")

            if l == tap_layer and "pt" in taps:
                dma("sp", tap_t["pt"].ap()[0:1024, :], pt_d[0:1024, :], pt_res, [out_res], "tap")
                dma("sp", tap_t["pt"].ap()[1536:INC, :], pt_d[1536:INC, :], pt_res, [out_res], "tap")

            if stop_after == 'B':
                return
            A.reset()
            qkf = A.alloc(S)
            qb = [A.alloc(S, BF16) for _ in range(2)]
            kb = [A.alloc(S, BF16) for _ in range(2)]
            vh = [A.alloc(NT * 128, BF16) for _ in range(2)]
            zf = [A.alloc(512) for _ in range(2)]
            ob = [A.alloc(4 * 512) for _ in range(2)]
            ptb = [A.alloc(2 * 512, BF16) for _ in range(3)]
            r01 = A.alloc(2 * 512)
            o01 = A.alloc(2 * 512)
            osb = A.alloc(512)
            sqb = A.alloc(512, BF16)
            rsd = A.alloc(512)
            gat = A.alloc(512)
            linit = 0.8 - 0.6 * math.exp(-0.3 * l)
            pt_cnt = [0]
            st_cnt = [0]
            pend = [None]
            zi = 0
            cbuf = [[A.alloc(516) for _ in range(4)] for _ in range(2)]
            cw = lambda k, j: prm128[:, 36 + l * 6 + k * 2 + j:36 + l * 6 + k * 2 + j + 1]

            def conv_load(j, g, it):
                    bs = cbuf[it % 2]
                    (cba, cbr), (cca, ccr), (cha, chr_), (cza, czr) = bs
                    t0 = g * 512
                    lo = 2 if g > 0 else 0
                    dma("sp", cca[:, 2 - lo:514], pt_d[(18 + j) * 128:(19 + j) * 128, t0 - lo:t0 + 512], [pt_res[18 + j]], ccr, f"cv{it % 2}")
                    dma("sp", cha[:, 2 - lo:514], pt_d[(20 + j) * 128:(21 + j) * 128, t0 - lo:t0 + 512], [pt_res[20 + j]], chr_, f"cv{it % 2}")
                    dma("sp", cba[:, 0:512], pt_d[(16 + j) * 128:(17 + j) * 128, t0:t0 + 512], [pt_res[16 + j]], cbr, f"cv{it % 2}")
                    dma("sp", cza[:, 0:512], pt_d[(22 + j) * 128:(23 + j) * 128, t0:t0 + 512], [pt_res[22 + j]], czr, f"cv{it % 2}")
                    if g == 0:
                        P.add("pool", (lambda cca=cca: lambda e: e.memset(cca[:, 0:2], 0.0))(), [], ccr)
                        P.add("pool", (lambda cha=cha: lambda e: e.memset(cha[:, 0:2], 0.0))(), [], chr_)

            def conv_block(j, g, it):
                    bs = cbuf[it % 2]
                    (cba, cbr), (cca, ccr), (cha, chr_), (cza, czr) = bs
                    P.add("pool", (lambda cca=cca, cha=cha: lambda e: e.tensor_tensor(out=cca[:, 0:514], in0=cca[:, 0:514], in1=cha[:, 0:514], op=ALU.mult))(), ccr + chr_, ccr)
                    P.add("dve", (lambda cca=cca, cha=cha, j=j: lambda e: e.tensor_scalar(out=cha[:, 0:512], in0=cca[:, 0:512], scalar1=cw(0, j), scalar2=None, op0=ALU.mult))(), ccr + [cst], chr_)
                    P.add("dve", (lambda cca=cca, cha=cha, j=j: lambda e: e.scalar_tensor_tensor(out=cha[:, 0:512], in0=cca[:, 1:513], scalar=cw(1, j), in1=cha[:, 0:512], op0=ALU.mult, op1=ALU.add))(), ccr + chr_ + [cst], chr_)
                    P.add("dve", (lambda cca=cca, cha=cha, j=j: lambda e: e.scalar_tensor_tensor(out=cha[:, 0:512], in0=cca[:, 2:514], scalar=cw(2, j), in1=cha[:, 0:512], op0=ALU.mult, op1=ALU.add))(), ccr + chr_ + [cst], chr_)
                    P.add("act", (lambda cza=cza, cca=cca: lambda e: e.activation(out=cca[:, 0:512], in_=cza[:, 0:512], func=AF.Exp, scale=-1.0))(), czr + ccr, ccr)
                    P.add("pool", (lambda cca=cca: lambda e: e.tensor_scalar(out=cca[:, 0:512], in0=cca[:, 0:512], scalar1=1.0, scalar2=None, op0=ALU.add))(), ccr, ccr)
                    P.add("dve", (lambda cca=cca: lambda e: e.reciprocal(out=cca[:, 0:512], in_=cca[:, 0:512]))(), ccr, ccr)
                    P.add("pool", (lambda cca=cca, cza=cza: lambda e: e.tensor_tensor(out=cca[:, 0:512], in0=cca[:, 0:512], in1=cza[:, 0:512], op=ALU.mult))(), ccr + czr, ccr)
                    P.add("pool", (lambda cha=cha, cba=cba: lambda e: e.tensor_tensor(out=cha[:, 0:512], in0=cha[:, 0:512], in1=cba[:, 0:512], op=ALU.mult))(), chr_ + cbr, chr_)
                    P.add("dve", (lambda cha=cha, cca=cca, j=j, g=g: lambda e: e.tensor_tensor(out=big[:, 4 + j, g * 512:(g + 1) * 512], in0=cha[:, 0:512], in1=cca[:, 0:512], op=ALU.mult))(), chr_ + ccr, [big_res[4 + j][g]])

            conv_todo = [(j, g, j * NG + g) for j in range(2) for g in range(NG)]
            conv_load(*conv_todo[0])

            def conv_step():
                cur = conv_todo.pop(0)
                conv_block(*cur)
                if conv_todo:
                    conv_load(*conv_todo[0])
            def load_head(h):
                hs = h % 2
                for c4 in range(4):
                    dma("sp", qkf[0][:, c4 * 1024:(c4 + 1) * 1024], pt_d[h * 128:(h + 1) * 128, c4 * 1024:(c4 + 1) * 1024], [pt_res[h]], qkf[1], "qkf")
                P.add("dve", (lambda hs=hs: lambda e: e.tensor_copy(out=qb[hs][0], in_=qkf[0]))(), qkf[1], qb[hs][1])
                for c4 in range(4):
                    dma("sp", qkf[0][:, c4 * 1024:(c4 + 1) * 1024], pt_d[512 + h * 128:512 + (h + 1) * 128, c4 * 1024:(c4 + 1) * 1024], [pt_res[4 + h]], qkf[1], "qkf")
                P.add("pool", (lambda hs=hs: lambda e: e.tensor_copy(out=kb[hs][0], in_=qkf[0]))(), qkf[1], kb[hs][1])
                vh3_ = vh[hs][0].rearrange("p (n d) -> p n d", n=NT)
                for c4 in range(4):
                    dma("sp", vh3_[:, c4 * 8:(c4 + 1) * 8, :], vtok_d[c4 * 1024:(c4 + 1) * 1024, h * 128:(h + 1) * 128].rearrange("(n p) d -> p n d", p=128), vtok_res, vh[hs][1], f"vh{hs}")

            load_head(0)
            for h in range(4):
                hs = h % 2
                vh3 = vh[hs][0].rearrange("p (n d) -> p n d", n=NT)
                qbh, kbh = qb[hs][0], kb[hs][0]
                for g in range(NG):
                    nk = 4 * g + 4
                    zsl, zsr = zf[zi % 2]
                    dma("sp", zsl, pt_d[1536 + h * 128:1536 + (h + 1) * 128, g * 512:(g + 1) * 512], [pt_res[12 + h]], zsr, f"zf{zi % 2}")
                    zi += 1
                    accs = [psb(4 + a_) for a_ in range(4)]

                    def emit_qk(kt, g=g, h=h, kbh=kbh, qbh=qbh, hs=hs):
                        sbank = (st_cnt[0] % 2) * 2
                        st_cnt[0] += 1
                        s2, s2r = psb(sbank, 2)
                        r = kt - 4 * g
                        diag = r >= -1
                        c0 = 128 * max(r, 0)
                        for m in range(2):
                            P.add("pe", (lambda s2=s2, m=m, kt=kt, diag=diag, c0=c0: lambda e: e.matmul(s2[:, m * 512 + c0:(m + 1) * 512], kbh[m * 64:(m + 1) * 64, kt * 128:(kt + 1) * 128], qbh[m * 64:(m + 1) * 64, g * 512 + c0:(g + 1) * 512], start=True, stop=not diag))(), qb[hs][1] + kb[hs][1], [s2r[m]])
                            if diag:
                                for s_ in range(max(r, 0), 4):
                                    dlt = s_ - r
                                    bi = 0 if dlt == 0 else (1 if dlt == 1 else 2)
                                    P.add("pe", (lambda s2=s2, m=m, s_=s_, bi=bi: lambda e: e.matmul(s2[:, m * 512 + s_ * 128:m * 512 + (s_ + 1) * 128], identb[:], bblk[:, h, bi, :], start=False, stop=(s_ == 3)))(), [cst2, bias_r], [s2r[m]])
                        return (s2, s2r, diag, c0)

                    def emit_rest(kt, qk, nk=nk, h=h, accs=accs, vh3=vh3, hs=hs):
                        s2, s2r, diag, c0 = qk
                        pa_, par = ptb[pt_cnt[0] % 3]
                        pt_cnt[0] += 1
                        s23 = s2.rearrange("p (m q) -> p m q", m=2)[:, :, c0:512]
                        pa3 = pa_.rearrange("p (m q) -> p m q", m=2)[:, :, c0:512]
                        if diag:
                            P.add("act", (lambda: lambda e: e.activation(out=pa3, in_=s23, func=AF.Exp, scale=SCALE))(), s2r, par)
                        else:
                            P.add("act", (lambda: lambda e: e.activation(out=pa3, in_=s23, func=AF.Exp, scale=SCALE, bias=cfar[:, 4 + h:5 + h]))(), s2r + [cst2], par)
                        for m in range(2):
                            P.add("pe", (lambda m=m, acc=accs[m][0]: lambda e: e.matmul(acc[:, c0:512], vh3[:, kt, :], pa_[:, m * 512 + c0:(m + 1) * 512], start=(kt == 0), stop=(kt == nk - 1)))(), par + vh[hs][1], accs[m][1])
                            P.add("pe", (lambda m=m, acc=accs[2 + m][0]: lambda e: e.matmul(acc[:, c0:512], onesb[:], pa_[:, m * 512 + c0:(m + 1) * 512], start=(kt == 0), stop=(kt == nk - 1)))(), par + [cst2], accs[2 + m][1])

                    nxt = emit_qk(0)
                    if conv_todo and (g % 2 == 1):
                        conv_step()
                    if g == 5 and h + 1 < 4:
                        load_head(h + 1)
                    for kt in range(nk):
                        cur = nxt
                        if kt + 1 < nk:
                            nxt = emit_qk(kt + 1)
                        emit_rest(kt, cur)
                        if kt == 2 and pend[0] is not None:
                            pend[0]()
                            pend[0] = None
                    oba, obr = ob[g % 2]
                    acc4, acc4r = psb(4, 4)
                    P.add("act", (lambda oba=oba, acc4=acc4: lambda e: e.activation(out=oba[:, 0:1024], in_=acc4[:, 0:1024], func=AF.Copy))(), acc4r[0:2], obr)
                    P.add("dve", (lambda oba=oba, acc4=acc4: lambda e: e.tensor_copy(out=oba[:, 1024:2048], in_=acc4[:, 1024:2048]))(), acc4r[2:4], obr)
                    P.add("dve", (lambda oba=oba: lambda e: e.reciprocal(out=r01[0], in_=oba[:, 1024:2048]))(), obr, r01[1])
                    P.add("dve", (lambda oba=oba: lambda e: e.tensor_tensor(out=o01[0], in0=oba[:, 0:1024], in1=r01[0], op=ALU.mult))(), obr + r01[1], o01[1])
                    P.add("dve", (lambda: lambda e: e.scalar_tensor_tensor(out=osb[0], in0=o01[0][:, 512:1024], scalar=lamv[:, l * 8 + 5:l * 8 + 6], in1=o01[0][:, 0:512], op0=ALU.mult, op1=ALU.add))(), o01[1] + [lam_r], osb[1])

                    def tail(h=h, g=g, zsl=zsl, zsr=zsr):
                        P.add("act", (lambda: lambda e: e.activation(out=sqb[0], in_=osb[0], func=AF.Square))(), osb[1], sqb[1])
                        P.add("act", (lambda: lambda e: e.activation(out=gat[0], in_=zsl, func=AF.Exp, scale=-1.0))(), zsr, gat[1])
                        sq_ps, sq_r = psb(0)
                        P.add("pe", (lambda: lambda e: e.matmul(sq_ps, onesb[:], sqb[0], start=True, stop=True))(), sqb[1] + [cst2], sq_r)
                        P.add("act", (lambda: lambda e: e.activation(out=rsd[0], in_=sq_ps, func=AF.Ln, scale=1.0 / 128, bias=SUBLN_EPS))(), sq_r, rsd[1])
                        P.add("act", (lambda: lambda e: e.activation(out=rsd[0], in_=rsd[0], func=AF.Exp, scale=-0.5))(), rsd[1], rsd[1])
                        P.add("pool", (lambda: lambda e: e.tensor_scalar(out=gat[0], in0=gat[0], scalar1=1.0, scalar2=None, op0=ALU.add))(), gat[1], gat[1])
                        P.add("dve", (lambda: lambda e: e.reciprocal(out=gat[0], in_=gat[0]))(), gat[1], gat[1])
                        P.add("pool", (lambda: lambda e: e.tensor_tensor(out=gat[0], in0=gat[0], in1=zsl, op=ALU.mult))(), gat[1] + zsr, gat[1])
                        P.add("dve", (lambda: lambda e: e.tensor_tensor(out=osb[0], in0=osb[0], in1=rsd[0], op=ALU.mult))(), osb[1] + rsd[1], osb[1])
                        P.add("dve", (lambda: lambda e: e.tensor_scalar(out=osb[0], in0=osb[0], scalar1=prm128[:, 32 + l:33 + l], scalar2=(1.0 - linit), op0=ALU.mult, op1=ALU.mult))(), osb[1] + [cst], osb[1])
                        P.add("dve", (lambda: lambda e: e.tensor_tensor(out=big[:, h, g * 512:(g + 1) * 512], in0=osb[0], in1=gat[0], op=ALU.mult))(), osb[1] + gat[1], [big_res[h][g]])

                    pend[0] = tail
            if pend[0] is not None:
                pend[0]()
                pend[0] = None

            if stop_after == 'C':
                return
            while conv_todo:
                conv_step()
            if stop_after == 'D':
                return
            if do_rwkv:
                rwkv_phase(l)
                P.pool_as = POOL_AS
            else:
                for k in (6, 7):
                    for g in range(NG):
                        P.add("pool", (lambda k=k, g=g: lambda e: e.memset(big[:, k, g * 512:(g + 1) * 512], 0.0))(), [], [big_res[k][g]])

            if l == tap_layer and "mixed" in taps:
                dma("sp", tap_t["mixed"].ap().rearrange("(k p) t -> p k t", p=128), big[:], allbig, [out_res], "tap")

            if stop_after == 'E':
                return
            A.reset()
            wo_st = [A.alloc(D) for _ in range(2)]
            wo = A.alloc(8 * D, BF16)
            wo3 = wo[0].rearrange("p (k d) -> p k d", k=8)
            xin = [A.alloc(D) for _ in range(2)]
            xo = [A.alloc(D) for _ in range(2)]
            for k in range(8):
                wa, wr_ = wo_st[k % 2]
                dma("sp", wa, wout_d[l, k * 128:(k + 1) * 128, :], [], wr_, f"wo{k % 2}")
                P.add("pool" if k % 2 else "dve", (lambda wa=wa, k=k: lambda e: e.tensor_copy(out=wo3[:, k, :], in_=wa))(), wr_, wo[1])
            last = (l == depth - 1)
            if last:
                fga, fgr = A.alloc(D)
                dma("sp", fga, fg_t.ap(), [], fgr, "fgl")
            for i in range(NT):
                xa, xr = xin[i % 2]
                ya, yr = xo[i % 2]
                dma("sp", xa, src_d[i * 128:(i + 1) * 128, :], [xs_res[i]] if l > 0 else [], xr, f"xin{i % 2}")
                p2, p2r = psb((i % 2) * 2, 2)
                for half in range(2):
                    for k in range(8):
                        P.add("pe", (lambda p2=p2, half=half, k=k, i=i: lambda e: e.matmul(p2[:, half * 512:(half + 1) * 512], big[:, k, i * 128:(i + 1) * 128], wo3[:, k, half * 512:(half + 1) * 512], start=(k == 0), stop=(k == 7)))(), [big_res[k][i // 4]] + wo[1], [p2r[half]])
                P.add("dve", (lambda ya=ya, p2=p2, xa=xa: lambda e: e.tensor_tensor(out=ya, in0=p2, in1=xa, op=ALU.add))(), p2r + xr, yr)
                if not last:
                    dma("sp", xs_d[i * 128:(i + 1) * 128, :], ya, yr, [xs_res[i]], f"xo{i % 2}")
                else:
                    if "xs" in taps:
                        dma("sp", tap_t["xs"].ap()[i * 128:(i + 1) * 128, :], ya, yr, [out_res], f"xo{i % 2}")
                    fr = Res(f"fin{i}")
                    P.add("act", (lambda ya=ya, xa=xa, i=i: lambda e: e.activation(out=xa, in_=ya, func=AF.Square, accum_out=ss[:, 3 * i:3 * i + 1]))(), yr, xr + [ss_res[i]])
                    P.add("act", (lambda i=i: lambda e: e.activation(out=ss[:, 3 * i + 1:3 * i + 2], in_=ss[:, 3 * i:3 * i + 1], func=AF.Ln, scale=1.0 / D, bias=NORM_EPS))(), [ss_res[i]], [ss_res[i]])
                    P.add("act", (lambda i=i: lambda e: e.activation(out=ss[:, 3 * i + 2:3 * i + 3], in_=ss[:, 3 * i + 1:3 * i + 2], func=AF.Exp, scale=-0.5))(), [ss_res[i]], [ss_res[i]])
                    P.add("dve", (lambda ya=ya, xa=xa, i=i: lambda e: e.scalar_tensor_tensor(out=xa, in0=ya, scalar=ss[:, 3 * i + 2:3 * i + 3], in1=fga, op0=ALU.mult, op1=ALU.mult))(), yr + [ss_res[i]] + fgr, xr)
                    dma("sp", out_d[i * 128:(i + 1) * 128, :], xa, xr, [out_res], f"xo{i % 2}")

        for l_ in range(depth):
            emit_layer(l_)
        P.add("sp", lambda e: e.nop(), [out_res], [])
        nsem = P.emit(nc, st)
    return nc, len(P.ops), nsem


_CACHE = {}


def kernel(**inputs):
    x = np.asarray(inputs["x"], np.float32)
    prm128, fg, lamrep, prm64 = host_params(inputs)
    consts = make_consts()
    if "nc" not in _CACHE:
        _CACHE["nc"] = build(L)[0]
    nc = _CACHE["nc"]
    shared = {
        "w_in": np.ascontiguousarray(np.asarray(inputs["w_in"], np.float32)),
        "w_out": np.ascontiguousarray(np.asarray(inputs["w_out"], np.float32)),
        "rel_bias": np.ascontiguousarray(np.asarray(inputs["rel_bias"], np.float32)),
        "w_up": np.ascontiguousarray(np.asarray(inputs["w_up"], np.float32)),
        "a_up": np.ascontiguousarray(np.asarray(inputs["a_up"], np.float32)),
        "prm128": prm128, "fg": fg, "lamrep": lamrep, "prm64": prm64,
    }
    shared.update(consts)
    in_maps = []
    for b in range(8):
        m = dict(shared)
        m["x"] = np.ascontiguousarray(x[b])
        in_maps.append(m)
    res = run_bass_kernel_spmd(nc, in_maps, core_ids=list(range(8)))
    return np.stack([np.asarray(r["out"], np.float32) for r in res.results], axis=0)
```

```python
import math
from contextlib import ExitStack

import numpy as np
import ml_dtypes

import concourse.bass as bass
import concourse.mybir as mybir
from concourse.bass_utils import run_bass_kernel_spmd

F32 = mybir.dt.float32
BF16 = mybir.dt.bfloat16
AF = mybir.ActivationFunctionType
ALU = mybir.AluOpType
AX = mybir.AxisListType

S = 4096
D = 1024
NT = 32
NG = 8
L = 4
INC = 4224
NEG8 = -240000.0
NORM_EPS = 1e-6
SUBLN_EPS = 1e-5
GN_EPS = 64e-5
SCALE = 0.125
DBG = set()
POOL_AS = "dve"
RWKV_POOL_AS = "dve"


class Res:
    __slots__ = ("name", "writer", "readers")

    def __init__(self, name):
        self.name = name
        self.writer = None
        self.readers = []


class Op:
    __slots__ = ("eng", "fn", "deps", "dma", "idx", "sig", "waits", "has_dep")


class Prog:
    def __init__(self):
        self.ops = []
        self.pool_as = POOL_AS

    def add(self, eng, fn, reads=(), writes=(), dma=None):
        if dma is None and eng == "pool" and self.pool_as:
            eng = self.pool_as
        op = Op()
        op.eng = eng
        op.fn = fn
        op.dma = dma
        op.idx = len(self.ops)
        op.deps = {}
        op.has_dep = False
        op.sig = None

        def dep(d, kind):
            if d is None or d is op:
                return
            if op.deps.get(d) != "raw":
                op.deps[d] = kind

        for r in reads:
            dep(r.writer, "raw")
        for w in writes:
            dep(w.writer, "waw")
            for rd in w.readers:
                dep(rd, "war")
        k = (op.eng, op.dma)
        for r in reads:
            r.readers = [x for x in r.readers if (x.eng, x.dma) != k]
            r.readers.append(op)
        for w in writes:
            w.writer = op
            w.readers = []
        self.ops.append(op)
        return op

    def finalize(self):
        for op in self.ops:
            keep = {}
            for d, kind in op.deps.items():
                if d.dma is None and op.dma is None and d.eng == op.eng:
                    if op.eng == "pe":
                        continue
                keep[d] = kind
            op.deps = keep
            for d in keep:
                d.has_dep = True
        cnt = {}
        waited = {}
        for op in self.ops:
            w = {}
            for d in op.deps:
                key = d.dma if d.dma else d.eng
                val = cnt[key] if d.dma else d.sig
                if w.get(key, 0) < val:
                    w[key] = val
            q = waited.setdefault(op.eng, {})
            op.waits = []
            for kk, v in w.items():
                if q.get(kk, 0) < v:
                    q[kk] = v
                    op.waits.append((kk, v))
            if op.dma:
                cnt[op.dma] = cnt.get(op.dma, 0) + 16
                op.sig = cnt[op.dma]
            elif op.has_dep:
                cnt[op.eng] = cnt.get(op.eng, 0) + 1
                op.sig = cnt[op.eng]
        self.cnt = cnt

    def emit(self, nc, st):
        self.finalize()
        keys = set()
        for op in self.ops:
            for kk, _ in op.waits:
                keys.add(kk)
            if op.dma:
                keys.add(op.dma)
            elif op.sig is not None:
                keys.add(op.eng)
        sems = {kk: st.enter_context(nc.semaphore("s_" + kk)) for kk in sorted(keys)}
        block = st.enter_context(nc.Block())
        ops = self.ops

        def run(name):
            def body(e):
                for op in ops:
                    if op.eng != name:
                        continue
                    for kk, v in op.waits:
                        e.wait_ge(sems[kk], v)
                    ins = op.fn(e)
                    if op.sig is not None:
                        ins.then_inc(sems[op.dma or op.eng], 16 if op.dma else 1)

            return body

        block.tensor(run("pe"))
        block.scalar(run("act"))
        block.vector(run("dve"))
        block.gpsimd(run("pool"))
        block.sync(run("sp"))
        return len(sems)


def _bucket(dist):
    n = np.maximum(dist, 0)
    max_exact = 16
    nf = np.maximum(n, 1).astype(np.float32)
    large = max_exact + (np.log(nf / max_exact) / math.log(128 / max_exact) * (32 - max_exact)).astype(np.int32)
    large = np.minimum(large, 31)
    return np.where(n < max_exact, n, large)


def _bucket_jax_exact():
    import jax
    import jax.numpy as jnp

    with jax.default_device(jax.devices("cpu")[0]):
        dist = jnp.arange(0, 256)
        n = jnp.maximum(dist, 0)
        nf = jnp.maximum(n, 1).astype(jnp.float32)
        large = 16 + (jnp.log(nf / 16) / math.log(128 / 16) * 16).astype(jnp.int32)
        large = jnp.minimum(large, 31)
        return np.asarray(jnp.where(n < 16, n, large))


def make_consts():
    c = {}
    c["c_identf"] = np.eye(128, dtype=np.float32)
    try:
        bk = _bucket_jax_exact()
    except Exception:
        bk = _bucket(np.arange(256))
    oh = np.zeros((33, 384), np.float32)
    for m in range(384):
        dist = m - 128
        if dist < 0:
            oh[32, m] = 8.0
        else:
            oh[bk[dist], m] = 8.0
    c["c_onehot8"] = oh
    j = np.arange(64)[:, None]
    t = np.arange(64)[None, :]
    strict = (j < t).astype(np.float32)
    incl = (j <= t).astype(np.float32)
    c["c_maskT2"] = np.tile(np.concatenate([strict, incl], axis=1), (2, 1))
    c["c_maskL"] = np.tile((t < j).astype(np.float32), (2, 1))
    sm = np.ones((128, 1024), np.float32)
    sm[:, ::64] = 0.0
    c["c_scanmask"] = sm
    return c


def host_params(inp):
    g = np.asarray(inp["norm_g"], np.float32)
    gT = g.reshape(L, 8, 128).transpose(2, 0, 1).reshape(128, L * 8)
    sublnT = np.asarray(inp["subln_g"], np.float32).T
    convT = np.asarray(inp["conv_w"], np.float32).reshape(L, 3, 2, 128).transpose(3, 0, 1, 2).reshape(128, L * 6)
    prm128 = np.ascontiguousarray(np.concatenate([gT, sublnT, convT], axis=1))
    fg = np.ascontiguousarray(np.broadcast_to(np.asarray(inp["final_norm_g"], np.float32)[None, :], (128, D)))
    lamrep = np.ascontiguousarray(np.broadcast_to(np.asarray(inp["lam_qk"], np.float32).reshape(1, L * 256), (128, L * 256)))
    mu = np.asarray(inp["rwkv_mu"], np.float32)
    mu_rkv = mu[:, :768].reshape(L, 3, 4, 64).transpose(3, 0, 1, 2).reshape(64, L * 12)
    mu_wa = mu[:, 768:896].reshape(L, 2, 64).transpose(2, 0, 1).reshape(64, L * 2)

    def ch(a):
        return np.asarray(a, np.float32).reshape(L, 4, 64).transpose(2, 0, 1).reshape(64, L * 4)

    prm64 = np.ascontiguousarray(np.concatenate(
        [mu_rkv, mu_wa, ch(inp["w0"]), ch(inp["a0"]), ch(inp["k_k"]), ch(inp["k_a"]),
         ch(np.asarray(inp["r_k"]).reshape(L, 256)), ch(inp["lnx_g"]), ch(inp["lnx_b"])], axis=1))
    prm64 = np.ascontiguousarray(np.tile(prm64, (2, 1)))
    return prm128, fg, lamrep, prm64


def build(depth=L, taps=(), do_rwkv=True, tap_layer=0, stop_after=None):
    nc = bass.Bass("TRN2", target_bir_lowering=False)
    P = Prog()
    dram_in = lambda n, s, d=F32: nc.dram_tensor(n, list(s), d, kind="ExternalInput")
    x_t = dram_in("x", [S, D])
    win_t = dram_in("w_in", [L, D, INC])
    wout_t = dram_in("w_out", [L, D, D])
    relb_t = dram_in("rel_bias", [32, 4])
    wup_t = dram_in("w_up", [L, 64, 256])
    aup_t = dram_in("a_up", [L, 64, 256])
    prm128_t = dram_in("prm128", [128, 60])
    fg_t = dram_in("fg", [128, D])
    lamrep_t = dram_in("lamrep", [128, L * 256])
    prm64_t = dram_in("prm64", [128, 168])
    cidf_t = dram_in("c_identf", [128, 128])
    coh_t = dram_in("c_onehot8", [33, 384])
    cm2_t = dram_in("c_maskT2", [128, 128])
    cml_t = dram_in("c_maskL", [128, 64])
    csm_t = dram_in("c_scanmask", [128, 1024])
    out_t = nc.dram_tensor("out", [S, D], F32, kind="ExternalOutput")
    xs_t = nc.dram_tensor("xs", [S, D], F32, kind="Internal")
    pt_t = nc.dram_tensor("ptf", [INC, S], F32, kind="Internal")
    vtok_t = nc.dram_tensor("vtok", [S, 512], BF16, kind="Internal")
    gsc_t = nc.dram_tensor("gsc", [4, 130 * 384], F32, kind="Internal")
    tap_t = {}
    if "pt" in taps:
        tap_t["pt"] = nc.dram_tensor("tap_pt", [INC, S], F32, kind="ExternalOutput")
    if "mixed" in taps:
        tap_t["mixed"] = nc.dram_tensor("tap_mixed", [D, S], BF16, kind="ExternalOutput")
    if "xs" in taps:
        tap_t["xs"] = nc.dram_tensor("tap_xs", [S, D], F32, kind="ExternalOutput")

    x_d, win_d, wout_d = x_t.ap(), win_t.ap(), wout_t.ap()
    out_d, xs_d, pt_d, vtok_d = out_t.ap(), xs_t.ap(), pt_t.ap(), vtok_t.ap()

    xs_res = [Res(f"xs{i}") for i in range(NT)]
    pt_res = [Res(f"pt{j}") for j in range(33)]
    vtok_res = [Res(f"vt{i}") for i in range(NT)]
    out_res = Res("out")
    gsc_res = Res("gsc")

    with ExitStack() as st:
        sb = lambda n, s, d=F32: st.enter_context(nc.sbuf_tensor(n, list(s), d))
        big = sb("big", [128, 8, S], BF16)
        big_res = [[Res(f"big{k}_{g}") for g in range(NG)] for k in range(8)]
        ar = sb("arena", [128, 32768], F32)
        ARES = [Res(f"ar{i}") for i in range(128)]
        ps_all = st.enter_context(nc.psum_tensor("psall", [128, 8 * 512], F32))
        PSR = [Res(f"ps{i}") for i in range(8)]

        def psb(bank, nb=1, parts=128):
            return ps_all[0:parts, bank * 512:(bank + nb) * 512], PSR[bank:bank + nb]

        class Arena:
            def __init__(self):
                self.p = 0

            def reset(self, p=0):
                self.p = p

            def alloc(self, n, dtype=F32, parts=128):
                nf = n if dtype is F32 else (n + 1) // 2
                nf = (nf + 1) // 2 * 2
                off = self.p
                self.p += nf
                assert self.p <= 32768, self.p
                ap = ar[0:parts, off:off + nf]
                if dtype is BF16:
                    ap = ap.bitcast(BF16)[:, 0:n]
                else:
                    ap = ap[:, 0:n]
                return ap, ARES[off // 256:(off + nf - 1) // 256 + 1]

        A = Arena()

        identf = sb("identf", [128, 128])
        identb = sb("identb", [128, 128], BF16)
        onesb = sb("onesb", [128, 128], BF16)
        ones64 = sb("ones64", [128, 64])
        ones64r = sb("ones64r", [128, 64])
        prm128 = sb("prm128s", [128, 60])
        prm64 = sb("prm64s", [128, 168])
        nprm64 = sb("nprm64", [128, 32])
        lamv = sb("lamv", [128, L * 8])
        wup = sb("wups", [128, 256])
        aup = sb("aups", [128, 256])
        wua_r = Res("wua")
        maskT2p = sb("maskT2p", [128, 128])
        maskT2n = sb("maskT2n", [128, 128])
        maskLn = sb("maskLn", [128, 64])
        bblk = sb("bblk", [128, 4, 3, 128], BF16)
        cfar = sb("cfar", [128, 8])
        ss = sb("ss", [128, 3 * NT])
        cst = Res("consts")
        ss_res = [Res(f"ss{i}") for i in range(NT)]
        sst_res = [[Res(f"sst{a}_{n}") for n in range(4)] for a in range(2)]

        def dma(eng, out, in_, reads, writes, key):
            P.add(eng, lambda e: e.dma_start(out=out, in_=in_), reads, writes, dma=key)

        dma("sp", identf[:], cidf_t.ap(), [], [cst], "c0")
        dma("sp", prm128[:], prm128_t.ap(), [], [cst], "c0")
        dma("sp", prm64[:], prm64_t.ap(), [], [cst], "c0")
        lamrep, lamrep_r = A.alloc(L * 256)
        dma("sp", lamrep, lamrep_t.ap(), [], lamrep_r, "c0")
        dma("sp", maskT2p[:], cm2_t.ap(), [], [cst], "c0")
        dma("sp", maskLn[:], cml_t.ap(), [], [cst], "c0")
        onehot, onehot_r = A.alloc(384, parts=64)
        relb, relb_rr = A.alloc(128, parts=64)
        gsb, gsb_rr = A.alloc(384, parts=4)
        relb_r = Res("relb")
        P.add("pool", lambda e: e.memset(onehot[:], 0.0), [], onehot_r)
        dma("sp", onehot[0:33, :], coh_t.ap(), [], onehot_r, "c0")
        P.add("pool", lambda e: e.memset(relb[:], 0.0), [], [relb_r] + relb_rr)
        dma("sp", relb[0:32, 0:4], relb_t.ap(), [], [relb_r], "c1")
        cst2 = Res("consts2")
        P.add("dve", lambda e: e.tensor_copy(out=identb[:], in_=identf[:]), [cst], [cst2])
        P.add("pool", lambda e: e.memset(onesb[:], 1.0), [], [cst2])
        P.add("pool", lambda e: e.memset(ones64[:], 1.0 / 64), [], [cst2])
        P.add("pool", lambda e: e.memset(ones64r[:], 1.0), [], [cst2])
        P.add("dve", lambda e: e.tensor_scalar(out=maskT2n[:], in0=maskT2p[:], scalar1=-1.0, scalar2=None, op0=ALU.mult), [cst], [cst2])
        P.add("dve", lambda e: e.tensor_scalar(out=maskLn[:], in0=maskLn[:], scalar1=-1.0, scalar2=None, op0=ALU.mult), [cst], [cst2])
        P.add("dve", lambda e: e.tensor_scalar(out=nprm64[:], in0=prm64[:, 56:88], scalar1=-1.0, scalar2=None, op0=ALU.mult), [cst], [cst2])
        P.add("pool", lambda e: e.memset(relb[32:33, 0:4], NEG8 / 8.0), [relb_r], [relb_r])

        lam_r = Res("lam")
        for l in range(depth):
            lq = lamrep[:, l * 256:(l + 1) * 256]
            tmpa, tmpr = A.alloc(128)
            tmpr = tmpr + lamrep_r
            for pr in range(2):
                P.add("dve", (lambda pr=pr, lq=lq, tmpa=tmpa: lambda e: e.tensor_tensor(out=tmpa[:, pr * 64:(pr + 1) * 64], in0=lq[:, (2 * pr) * 64:(2 * pr + 1) * 64], in1=lq[:, (2 * pr + 1) * 64:(2 * pr + 2) * 64], op=ALU.mult))(), [cst], tmpr)
                P.add("dve", (lambda pr=pr, l=l, tmpa=tmpa: lambda e: e.reduce_sum(out=lamv[:, l * 8 + pr:l * 8 + pr + 1], in_=tmpa[:, pr * 64:(pr + 1) * 64], axis=AX.X))(), tmpr, [lam_r])
            P.add("act", (lambda l=l: lambda e: e.activation(out=lamv[:, l * 8 + 2:l * 8 + 4], in_=lamv[:, l * 8:l * 8 + 2], func=AF.Exp))(), [lam_r], [lam_r])
            linit = 0.8 - 0.6 * math.exp(-0.3 * l)
            P.add("dve", (lambda l=l: lambda e: e.tensor_tensor(out=lamv[:, l * 8 + 4:l * 8 + 5], in0=lamv[:, l * 8 + 2:l * 8 + 3], in1=lamv[:, l * 8 + 3:l * 8 + 4], op=ALU.subtract))(), [lam_r], [lam_r])
            P.add("dve", (lambda l=l, linit=linit: lambda e: e.tensor_scalar(out=lamv[:, l * 8 + 5:l * 8 + 6], in0=lamv[:, l * 8 + 4:l * 8 + 5], scalar1=linit, scalar2=-1.0, op0=ALU.add, op1=ALU.mult))(), [lam_r], [lam_r])

        (pg, pgr) = psb(0)
        P.add("pe", lambda e: e.matmul(pg[:, 0:384], relb[:, :], onehot[:, :], start=True, stop=True), [relb_r] + onehot_r, pgr)
        gsb_r = Res("gsb")
        P.add("dve", lambda e: e.tensor_copy(out=gsb[:], in_=pg[0:4, 0:384]), pgr, [gsb_r] + gsb_rr)
        dma("sp", gsc_t.ap().rearrange("h (r n) -> h r n", n=384), gsb.unsqueeze(1).broadcast_to([4, 130, 384]), [gsb_r], [gsc_res], "c2")
        tdo, tdo_r = A.alloc(4 * 2 * 128)
        tdo4 = tdo.rearrange("p (h a q) -> p h a q", h=4, a=2)
        for h in range(4):
            for a_, base in ((0, 128), (1, 256)):
                dma("sp", tdo4[:, h, a_, :], bass.AP(gsc_t, h * 130 * 384 + base, [[383, 128], [1, 128]]), [gsc_res], tdo_r, "c3")
            dma("sp", cfar[:, h:h + 1], bass.AP(gsc_t, h * 130 * 384 + 383, [[0, 128], [1, 1]]), [gsc_res], [cst2], "c3")
        P.add("dve", lambda e: e.tensor_scalar(out=cfar[:, 4:8], in0=cfar[:, 0:4], scalar1=SCALE, scalar2=None, op0=ALU.mult), [cst2], [cst2])
        zt, zt_r = A.alloc(128)
        P.add("pool", lambda e: e.memset(zt[:], 0.0), [], zt_r)
        bias_r = Res("biasT")
        for h in range(4):
            P.add("dve", (lambda h=h: lambda e: e.tensor_copy(out=bblk[:, h, 0, :], in_=tdo4[:, h, 0, :]))(), tdo_r, [bias_r])
            P.add("pool", (lambda h=h: lambda e: e.tensor_copy(out=bblk[:, h, 1, :], in_=tdo4[:, h, 1, :]))(), tdo_r, [bias_r])
            P.add("dve", (lambda h=h: lambda e: e.tensor_scalar(out=bblk[:, h, 2, :], in0=zt[:], scalar1=cfar[:, h:h + 1], scalar2=None, op0=ALU.add))(), zt_r + [cst2], [bias_r])

        def TT(eng, out, a, b, op, rd, wr):
            P.add(eng, lambda e: e.tensor_tensor(out=out, in0=a, in1=b, op=op), rd, wr)

        def TS(eng, out, a, s1, s2, op0, op1, rd, wr):
            if s2 is None:
                P.add(eng, lambda e: e.tensor_scalar(out=out, in0=a, scalar1=s1, scalar2=None, op0=op0), rd, wr)
            else:
                P.add(eng, lambda e: e.tensor_scalar(out=out, in0=a, scalar1=s1, scalar2=s2, op0=op0, op1=op1), rd, wr)

        def STT(eng, out, a, sc, b, op0, op1, rd, wr):
            P.add(eng, lambda e: e.scalar_tensor_tensor(out=out, in0=a, scalar=sc, in1=b, op0=op0, op1=op1), rd, wr)

        def ACTF(out, in_, func, rd, wr, scale=1.0, bias=None):
            if bias is None:
                P.add("act", lambda e: e.activation(out=out, in_=in_, func=func, scale=scale), rd, wr)
            else:
                P.add("act", lambda e: e.activation(out=out, in_=in_, func=func, scale=scale, bias=bias), rd, wr)

        def MM(out, lhsT, rhs, st_, sp_, rd, wr):
            P.add("pe", lambda e: e.matmul(out, lhsT, rhs, start=st_, stop=sp_), rd, wr)

        def CP(eng, out, in_, rd, wr):
            if eng == "act":
                P.add("act", lambda e: e.activation(out=out, in_=in_, func=AF.Copy), rd, wr)
            else:
                P.add(eng, lambda e: e.tensor_copy(out=out, in_=in_), rd, wr)

        def RCP(out, in_, rd, wr):
            P.add("dve", lambda e: e.reciprocal(out=out, in_=in_), rd, wr)

        V = lambda buf, w: buf[0].rearrange("c (p t) -> c p t", p=16)
        V16 = lambda ap: ap.rearrange("c (p t) -> c p t", p=16)
        V4 = lambda ap: ap.rearrange("c (h t) -> c h t", h=4)
        pbk_state = [0]

        def pbk(n):
            if pbk_state[0] + n > 8:
                pbk_state[0] = 0
            b = pbk_state[0]
            pbk_state[0] += n
            return psb(b, n, parts=128)

        def rwkv_phase(l):
            P.pool_as = RWKV_POOL_AS
            A.reset()
            al = lambda n, dt=F32: A.alloc(n, dt, parts=128)
            HF = lambda ap, hf: ap[64 * hf:64 * hf + 64]
            VH = lambda ap: ap.rearrange("c (h t) -> c h t", h=2)
            V8 = lambda ap: ap.rearrange("c (p t) -> c p t", p=8)
            W8 = lambda buf: buf[0].rearrange("c (p t) -> c p t", p=8)

            def sel(x, hf):
                return x[hf] if isinstance(x, tuple) else HF(x, hf)

            def MM2(out, lhsT, rhs, st_, sp_, rd, wr):
                for hf in range(2):
                    MM(sel(out, hf), sel(lhsT, hf), sel(rhs, hf), st_, sp_, rd, wr)

            scanmask = al(1024)
            id2 = al(64)
            wdc, adc, wdp, adp, th = [al(256) for _ in range(5)]
            idb = (identf[0:64, 0:64], identf[64:128, 64:128])
            id_bc = id2[0].unsqueeze(1).broadcast_to([128, 8, 64])
            on_r = (ones64r[0:64, :], ones64r[64:128, :])
            on_m = (ones64[0:64, :], ones64[64:128, :])
            SB = []
            for s in range(2):
                b = {}
                b["Sst"] = al(1024)
                b["Sfin"] = al(128)
                b["gC"] = al(8)
                b["KR"] = al(1024)
                b["kc"], b["vc"] = al(512), al(512)
                b["pv"] = [al(512), al(512)]
                for nm in ("X1", "X2", "X4", "C1", "T", "kt", "bt", "RK"):
                    b[nm] = al(512)
                for nm in ("UWrhs", "NAbT", "AkT", "UW"):
                    b[nm] = al(1024)
                for nm in ("Vtok", "Khtok", "Bntok", "NAkb"):
                    b[nm] = al(512)
                b["sst_res"] = [[Res(f"sst{s}_{a_}_{n}") for n in range(4)] for a_ in range(2)]
                SB.append(b)

            dma("sp", scanmask[0], csm_t.ap(), [], scanmask[1], "rw_c")
            for hf in range(2):
                dma("sp", HF(wup[:], hf), wup_t.ap()[l], [], [wua_r], "rw_c")
                dma("sp", HF(aup[:], hf), aup_t.ap()[l], [], [wua_r], "rw_c")
            P.add("dve", lambda e: e.tensor_copy(out=id2[0][0:64, :], in_=identf[0:64, 0:64]), [cst], id2[1])
            P.add("dve", lambda e: e.tensor_copy(out=id2[0][64:128, :], in_=identf[64:128, 64:128]), [cst], id2[1])
            for s in range(2):
                S5 = SB[s]["Sst"][0].rearrange("c (a n h v) -> c a n h v", a=2, n=4, h=2)
                P.add("dve", (lambda S5=S5: lambda e: e.memset(S5[0:64, 0, 0, :, :], 0.0))(), [], [SB[s]["sst_res"][0][0]] + SB[s]["Sst"][1])

            def prologue(gp):
                tb = gp * 512
                for (cur, prv, row0, mcol, wk) in ((wdc, wdp, 3840, 48 + l * 2, 0), (adc, adp, 3904, 48 + l * 2 + 1, 1)):
                    for hf in range(2):
                        t0 = tb + 256 * hf
                        dma("sp", HF(cur[0], hf), pt_d[row0:row0 + 64, t0:t0 + 256], [pt_res[30]], cur[1], f"rw_wc{wk}")
                        if t0 == 0:
                            P.add("dve", (lambda prv=prv: lambda e: e.memset(HF(prv[0], 0)[:, 0:1], 0.0))(), [], prv[1])
                            dma("sp", HF(prv[0], hf)[:, 1:256], pt_d[row0:row0 + 64, 0:255], [pt_res[30]], prv[1], f"rw_wp{wk}")
                        else:
                            dma("sp", HF(prv[0], hf), pt_d[row0:row0 + 64, t0 - 1:t0 + 255], [pt_res[30]], prv[1], f"rw_wp{wk}")
                    TT("dve", prv[0], prv[0], cur[0], ALU.subtract, prv[1] + cur[1], prv[1])
                    STT("dve", cur[0], prv[0], prm64[:, mcol:mcol + 1], cur[0], ALU.mult, ALU.add, prv[1] + cur[1] + [cst], cur[1])
                ACTF(th[0], wdc[0], AF.Exp, wdc[1], th[1], scale=2.0)
                TS("dve", th[0], th[0], 1.0, None, ALU.add, None, th[1], th[1])
                RCP(th[0], th[0], th[1], th[1])
                TS("dve", th[0], th[0], -2.0, 1.0, ALU.mult, ALU.add, th[1], th[1])

            def stream(s, gp):
                b = SB[s]
                par = gp % 2
                tb = gp * 512
                Sst5 = b["Sst"][0].rearrange("c (a n h v) -> c a n h v", a=2, n=4, h=2)
                sst_res_ = b["sst_res"]
                Sfin, gC, KR, kc, vc, pv = b["Sfin"], b["gC"], b["KR"], b["kc"], b["vc"], b["pv"]
                X1, X2, X4, C1, T, kt, bt, RK = [b[nm] for nm in ("X1", "X2", "X4", "C1", "T", "kt", "bt", "RK")]
                UWrhs, NAbT, AkT, UW = [b[nm] for nm in ("UWrhs", "NAbT", "AkT", "UW")]
                Vtok, Khtok, Bntok, NAkb = [b[nm] for nm in ("Vtok", "Khtok", "Bntok", "NAkb")]
                KR0 = (KR[0][:, 0:512], KR[1][0:2])
                KR1 = (KR[0][:, 512:1024], KR[1][2:4])
                KR4 = KR[0].rearrange("c (q p t) -> c q p t", q=2, p=8)

                def bc(col):
                    return prm64[:, col + 2 * s:col + 2 * s + 2].unsqueeze(2).broadcast_to([128, 2, 256])

                def src(row0, c0, c1):
                    return pt_d[row0 + 128 * s:row0 + 128 * s + 128, c0:c1].rearrange("(h c) t -> c h t", c=64)

                curs = [KR1, kc, vc]
                for q in range(3):
                    row0 = 3072 + 256 * q
                    prs = [pt_res[row0 // 128 + s]]
                    cur = curs[q]
                    pvb = pv[q % 2]
                    for hf in range(2):
                        t0 = tb + 256 * hf
                        dma("sp", VH(HF(cur[0], hf)), src(row0, t0, t0 + 256), prs, cur[1], f"rw_c{q}_{s}")
                        if t0 == 0:
                            P.add("dve", (lambda pvb=pvb: lambda e: e.memset(VH(HF(pvb[0], 0))[:, :, 0:1], 0.0))(), [], pvb[1])
                            dma("sp", VH(HF(pvb[0], hf))[:, :, 1:256], src(row0, 0, 255), prs, pvb[1], f"rw_p{q % 2}_{s}")
                        else:
                            dma("sp", VH(HF(pvb[0], hf)), src(row0, t0 - 1, t0 + 255), prs, pvb[1], f"rw_p{q % 2}_{s}")
                    TT("dve", pvb[0], pvb[0], cur[0], ALU.subtract, pvb[1] + cur[1], pvb[1])
                    TT("dve", VH(pvb[0]), VH(pvb[0]), bc(l * 12 + q * 4), ALU.mult, pvb[1] + [cst], pvb[1])
                    TT("dve", cur[0], cur[0], pvb[0], ALU.add, pvb[1] + cur[1], cur[1])
                    yield
                ups, upr = pbk(1)
                for hh in range(2):
                    MM2(ups[:, hh * 256:(hh + 1) * 256], wup[:, 64 * (2 * s + hh):64 * (2 * s + hh) + 64], th[0], True, True, th[1] + [wua_r], upr)
                yield
                for hh in range(2):
                    ACTF(VH(X1[0])[:, hh, :], ups[:, hh * 256:(hh + 1) * 256], AF.Exp, upr + [cst2], X1[1], scale=-1.0, bias=nprm64[:, l * 4 + 2 * s + hh:l * 4 + 2 * s + hh + 1])
                yield
                TS("dve", X1[0], X1[0], 1.0, None, ALU.add, None, X1[1], X1[1])
                RCP(X1[0], X1[0], X1[1], X1[1])
                TS("dve", X1[0], X1[0], -0.6065306597126334, None, ALU.mult, None, X1[1], X1[1])
                aps, apr = pbk(1)
                for hh in range(2):
                    MM2(aps[:, hh * 256:(hh + 1) * 256], aup[:, 64 * (2 * s + hh):64 * (2 * s + hh) + 64], adc[0], True, True, adc[1] + [wua_r], apr)
                yield
                for hh in range(2):
                    ACTF(VH(X2[0])[:, hh, :], aps[:, hh * 256:(hh + 1) * 256], AF.Exp, apr + [cst2], X2[1], scale=-1.0, bias=nprm64[:, 16 + l * 4 + 2 * s + hh:16 + l * 4 + 2 * s + hh + 1])
                yield
                TS("dve", X2[0], X2[0], 1.0, None, ALU.add, None, X2[1], X2[1])
                RCP(X2[0], X2[0], X2[1], X2[1])
                TT("dve", VH(KR0[0]), VH(kc[0]), bc(88 + l * 4), ALU.mult, kc[1] + [cst], KR0[1])
                yield
                ACTF(X4[0], KR0[0], AF.Square, KR0[1], X4[1])
                yield
                sps, spr = pbk(1)
                MM2(sps, on_r, X4[0], True, True, X4[1] + [cst2], spr)
                yield
                TS("dve", X4[0], sps, 1e-24, None, ALU.max, None, spr, X4[1])
                yield
                ACTF(X4[0], X4[0], AF.Ln, X4[1], X4[1])
                ACTF(X4[0], X4[0], AF.Exp, X4[1], X4[1], scale=-0.5)
                yield
                TT("dve", KR0[0], KR0[0], X4[0], ALU.mult, KR0[1] + X4[1], KR0[1])
                STT("dve", VH(X4[0]), VH(X2[0]), -1.0, bc(104 + l * 4), ALU.add, ALU.mult, X2[1] + [cst], X4[1])
                STT("dve", kc[0], X4[0], 1.0, kc[0], ALU.add, ALU.mult, X4[1] + kc[1], kc[1])
                yield
                TT("dve", RK[0], KR1[0], kc[0], ALU.mult, KR1[1] + kc[1], RK[1])
                TT("dve", VH(RK[0]), VH(RK[0]), bc(120 + l * 4), ALU.mult, RK[1] + [cst], RK[1])
                TT("dve", X2[0], KR0[0], X2[0], ALU.mult, KR0[1] + X2[1], X2[1])
                P.add("dve", lambda e: e.tensor_tensor_scan(out=C1[0], data0=scanmask[0][:, 0:512], data1=X1[0], initial=0.0, op0=ALU.mult, op1=ALU.add), scanmask[1] + X1[1], C1[1])
                yield
                ACTF(T[0], C1[0], AF.Exp, C1[1], T[1])
                yield
                CP("dve", gC[0], V8(T[0])[:, :, 63], T[1], gC[1])
                TT("dve", KR1[0], KR1[0], T[0], ALU.mult, KR1[1] + T[1], KR1[1])
                yield
                ACTF(T[0], C1[0], AF.Exp, C1[1], T[1], scale=-1.0)
                yield
                TT("dve", kt[0], kc[0], T[0], ALU.mult, kc[1] + T[1], kt[1])
                TT("dve", bt[0], X2[0], T[0], ALU.mult, X2[1] + T[1], bt[1])
                TT("dve", T[0], C1[0], X1[0], ALU.subtract, C1[1] + X1[1], T[1])
                yield
                ACTF(T[0], T[0], AF.Exp, T[1], T[1])
                yield
                TT("dve", KR0[0], KR0[0], T[0], ALU.mult, KR0[1] + T[1], KR0[1])
                TT("dve", V8(T[0]), V8(C1[0])[:, :, 63:64].broadcast_to([128, 8, 64]), V8(C1[0]), ALU.subtract, C1[1], T[1])
                yield
                ACTF(T[0], T[0], AF.Exp, T[1], T[1])
                yield
                TT("dve", kc[0], kc[0], T[0], ALU.mult, kc[1] + T[1], kc[1])
                STT("dve", X2[0], X2[0], -1.0, T[0], ALU.mult, ALU.mult, X2[1] + T[1], X2[1])
                yield
                for (srcb, dstv, dstr) in ((KR0, W8(UWrhs)[:, :, 64:128], UWrhs[1]), (vc, V8(Vtok[0]), Vtok[1]), (kc, V8(Khtok[0]), Khtok[1]), (X2, V8(Bntok[0]), Bntok[1])):
                    tp, tpr = pbk(1)
                    for p in range(8):
                        MM2(tp[:, p * 64:(p + 1) * 64], V8(srcb[0])[:, p, :], idb, True, True, srcb[1] + [cst], tpr)
                    CP("act", dstv, V8(tp), tpr, dstr)
                    yield
                for (lh, dst, msk) in ((bt, NAbT, maskT2n), (kt, AkT, maskT2p)):
                    ap_, apr_ = pbk(2)
                    for p in range(8):
                        MM2(ap_[:, p * 128:(p + 1) * 128], V8(lh[0])[:, p, :], KR4[:, :, p, :], True, True, lh[1] + KR[1], [apr_[p // 4]])
                    yield
                    TT("dve", W8(dst), ap_.rearrange("c (p t) -> c p t", p=8), msk[:].unsqueeze(1).broadcast_to([128, 8, 128]), ALU.mult, apr_ + [cst, cst2], dst[1])
                    yield
                ap_, apr_ = pbk(1)
                for p in range(8):
                    MM2(ap_[:, p * 64:(p + 1) * 64], KR4[:, 0, p, :], V8(bt[0])[:, p, :], True, True, bt[1] + KR0[1], apr_)
                yield
                TT("dve", V8(NAkb[0]), V8(ap_), maskLn[:].unsqueeze(1).broadcast_to([128, 8, 64]), ALU.mult, apr_ + [cst2], NAkb[1])
                NAb3 = W8(NAbT)
                R = T
                TT("dve", V8(R[0]), NAb3[:, :, 0:64], id_bc, ALU.add, NAbT[1] + id2[1], R[1])
                yield
                Pprev = (NAb3[:, :, 0:64], NAbT[1])
                PTprev = (V8(NAkb[0]), NAkb[1])
                Pbufs = [X1, X4]
                PTbufs = [C1, NAkb]
                for k in range(1, 6):
                    Pn = Pbufs[(k - 1) % 2]
                    PTn = PTbufs[(k - 1) % 2]
                    if k < 5:
                        pp, ppr = pbk(1)
                        for p in range(8):
                            MM2(pp[:, p * 64:(p + 1) * 64], PTprev[0][:, p, :], Pprev[0][:, p, :], True, True, PTprev[1] + Pprev[1], ppr)
                    pt2, pt2r = pbk(1)
                    for p in range(8):
                        MM2(pt2[:, p * 64:(p + 1) * 64], Pprev[0][:, p, :], PTprev[0][:, p, :], True, True, PTprev[1] + Pprev[1], pt2r)
                    yield
                    if k < 5:
                        CP("act", Pn[0], pp, ppr, Pn[1])
                    CP("dve", PTn[0], pt2, pt2r, PTn[1])
                    yield
                    rr, rrr = pbk(1)
                    for p in range(8):
                        MM2(rr[:, p * 64:(p + 1) * 64], V8(PTn[0])[:, p, :], V8(R[0])[:, p, :], True, True, PTn[1] + R[1], rrr)
                    yield
                    TT("dve", R[0], rr, R[0], ALU.add, rrr + R[1], R[1])
                    yield
                    Pprev = (V8(Pn[0]), Pn[1])
                    PTprev = (V8(PTn[0]), PTn[1])
                xp, xpr = pbk(1)
                for p in range(8):
                    MM2(xp[:, p * 64:(p + 1) * 64], W8(AkT)[:, p, 0:64], V8(Vtok[0])[:, p, :], True, True, AkT[1] + Vtok[1], xpr)
                yield
                CP("act", W8(UWrhs)[:, :, 0:64], V8(xp), xpr, UWrhs[1])
                yield
                up4, up4r = pbk(2)
                for p in range(8):
                    MM2(up4[:, p * 128:(p + 1) * 128], V8(R[0])[:, p, :], W8(UWrhs)[:, p, :], True, True, R[1] + UWrhs[1], [up4r[p // 4]])
                yield
                CP("act", UW[0][:, 0:512], up4[:, 0:512], [up4r[0]], UW[1][0:2])
                CP("dve", UW[0][:, 512:1024], up4[:, 512:1024], [up4r[1]], UW[1][2:4])
                yield
                UW3 = W8(UW)
                Qs, PTs, Dg, GT, Ysb, zr, Gg = X1, X4, pv[0], pv[1], kt, bt, C1
                qp, qpr = pbk(1)
                for p in range(8):
                    MM2(qp[:, p * 64:(p + 1) * 64], V8(Khtok[0])[:, p, :], V8(Vtok[0])[:, p, :], True, False, Khtok[1] + Vtok[1], qpr)
                    MM2(qp[:, p * 64:(p + 1) * 64], V8(Bntok[0])[:, p, :], UW3[:, p, 0:64], False, True, Bntok[1] + UW[1], qpr)
                pp2, pp2r = pbk(1)
                for p in range(8):
                    MM2(pp2[:, p * 64:(p + 1) * 64], UW3[:, p, 64:128], V8(Bntok[0])[:, p, :], True, True, Bntok[1] + UW[1], pp2r)
                yield
                CP("act", Qs[0], qp, qpr, Qs[1])
                TT("dve", V8(Dg[0]), id_bc, gC[0].unsqueeze(2).broadcast_to([128, 8, 64]), ALU.mult, gC[1] + id2[1], Dg[1])
                TT("dve", PTs[0], pp2, Dg[0], ALU.add, pp2r + Dg[1], PTs[1])
                yield
                gp_, gpr = pbk(1)
                for p in range(8):
                    MM2(gp_[:, p * 64:(p + 1) * 64], UW3[:, p, 64:128], NAb3[:, p, 64:128], True, True, UW[1] + NAbT[1], gpr)
                yield
                TT("dve", GT[0], gp_, KR1[0], ALU.add, gpr + KR1[1], GT[1])
                yield
                Q4 = Qs[0].rearrange("c (h n v) -> c h n v", h=2, n=4)
                for hf in range(2):
                    for n in range(4):
                        sp_, spr_ = pbk(1)
                        for hh in range(2):
                            MM(HF(sp_, hf)[:, hh * 64:(hh + 1) * 64], HF(V8(PTs[0]), hf)[:, hh * 4 + n, :], HF(Sst5, hf)[:, par, n, hh, :], True, True, PTs[1] + [sst_res_[par][n]], spr_)
                        yield
                        src_ps = HF(sp_, hf)[:, 0:128].rearrange("c (h v) -> c h v", h=2)
                        if n < 3:
                            TT("dve", HF(Sst5, hf)[:, par, n + 1, :, :], src_ps, HF(Q4, hf)[:, :, n, :], ALU.add, spr_ + Qs[1], [sst_res_[par][n + 1]])
                        else:
                            TT("dve", HF(Sfin[0], hf).rearrange("c (h v) -> c h v", h=2), src_ps, HF(Q4, hf)[:, :, n, :], ALU.add, spr_ + Qs[1], Sfin[1])
                        yield
                    sh, shr = pbk(1)
                    if hf == 0:
                        MM(sh[64:128, 0:128], identf[0:64, 0:64], Sfin[0][0:64, :], True, True, Sfin[1] + [cst], shr)
                        yield
                        CP("act", Sst5[64:128, par, 0, :, :], sh[64:128, 0:128].rearrange("c (h v) -> c h v", h=2), shr, [sst_res_[par][0]])
                    else:
                        MM(sh[0:64, 0:128], identf[64:128, 64:128], Sfin[0][64:128, :], True, True, Sfin[1] + [cst], shr)
                        yield
                        CP("act", Sst5[0:64, 1 - par, 0, :, :], sh[0:64, 0:128].rearrange("c (h v) -> c h v", h=2), shr, [sst_res_[1 - par][0]])
                    yield
                yp, ypr = pbk(1)
                for p in range(8):
                    hh, n = p // 4, p % 4
                    MM2(yp[:, p * 64:(p + 1) * 64], Sst5[:, par, n, hh, :], V8(GT[0])[:, p, :], True, False, [sst_res_[par][n]] + GT[1], ypr)
                    MM2(yp[:, p * 64:(p + 1) * 64], V8(Vtok[0])[:, p, :], W8(AkT)[:, p, 64:128], False, False, Vtok[1] + AkT[1], ypr)
                    MM2(yp[:, p * 64:(p + 1) * 64], UW3[:, p, 0:64], NAb3[:, p, 64:128], False, True, UW[1] + NAbT[1], ypr)
                yield
                CP("act", Ysb[0], yp, ypr, Ysb[1])
                yield
                mp, mpr = pbk(1)
                MM2(mp, on_m, Ysb[0], True, True, Ysb[1] + [cst2], mpr)
                yield
                TT("dve", Ysb[0], Ysb[0], mp, ALU.subtract, Ysb[1] + mpr, Ysb[1])
                yield
                ACTF(T[0], Ysb[0], AF.Square, Ysb[1], T[1])
                yield
                vp_, vpr_ = pbk(1)
                MM2(vp_, on_m, T[0], True, True, T[1] + [cst2], vpr_)
                bp, bpr = pbk(1)
                MM2(bp, on_r, RK[0], True, True, RK[1] + [cst2], bpr)
                yield
                ACTF(X4[0], vp_, AF.Ln, vpr_, X4[1], bias=GN_EPS)
                ACTF(X4[0], X4[0], AF.Exp, X4[1], X4[1], scale=-0.5)
                yield
                TT("dve", Ysb[0], Ysb[0], X4[0], ALU.mult, Ysb[1] + X4[1], Ysb[1])
                for hh in range(2):
                    TS("dve", VH(Ysb[0])[:, hh, :], VH(Ysb[0])[:, hh, :], prm64[:, 136 + l * 4 + 2 * s + hh:137 + l * 4 + 2 * s + hh], prm64[:, 152 + l * 4 + 2 * s + hh:153 + l * 4 + 2 * s + hh], ALU.mult, ALU.add, Ysb[1] + [cst], Ysb[1])
                TT("dve", T[0], bp, vc[0], ALU.mult, bpr + vc[1], T[1])
                TT("dve", Ysb[0], Ysb[0], T[0], ALU.add, Ysb[1] + T[1], Ysb[1])
                for hf in range(2):
                    t0 = tb + 256 * hf
                    dma("sp", VH(HF(zr[0], hf)), src(3968, t0, t0 + 256), [pt_res[31 + s]], zr[1], f"rw_z{s}")
                yield
                ACTF(Gg[0], zr[0], AF.Exp, zr[1], Gg[1], scale=-1.0)
                yield
                TS("dve", Gg[0], Gg[0], 1.0, None, ALU.add, None, Gg[1], Gg[1])
                RCP(Gg[0], Gg[0], Gg[1], Gg[1])
                TT("dve", Gg[0], Gg[0], zr[0], ALU.mult, Gg[1] + zr[1], Gg[1])
                outb = (Dg[0].bitcast(BF16)[:, 0:512], Dg[1])
                TT("dve", outb[0], Ysb[0], Gg[0], ALU.mult, Ysb[1] + Gg[1], outb[1])
                for hf in range(2):
                    t0 = tb + 256 * hf
                    for hh in range(2):
                        dma("sp", big[64 * hh:64 * hh + 64, 6 + s, t0:t0 + 256], VH(HF(outb[0], hf))[:, hh, :], outb[1], [big_res[6 + s][gp]], f"rw_o{s}")
                yield

            for gp in range(8):
                prologue(gp)
                gens = [stream(0, gp), stream(1, gp)]
                live = [True, True]
                while any(live):
                    for i in range(2):
                        if live[i]:
                            try:
                                next(gens[i])
                            except StopIteration:
                                live[i] = False


        def emit_layer(l):
            src_d = x_d if l == 0 else xs_d
            if stop_after == 'setup':
                return
            A.reset()
            xt = [A.alloc(D) for _ in range(3)]
            hb = [A.alloc(D, BF16) for _ in range(2)]
            junk = A.alloc(D, BF16)
            for i in range(NT):
                xa, xr = xt[i % 3]
                ha, hr = hb[i % 2]
                dma("sp", xa, src_d[i * 128:(i + 1) * 128, :], [xs_res[i]] if l > 0 else [], xr, f"xt{i % 3}")
                P.add("act", (lambda xa=xa, i=i: lambda e: e.activation(out=junk[0], in_=xa, func=AF.Square, accum_out=ss[:, 3 * i:3 * i + 1]))(), xr, junk[1] + [ss_res[i]])
                P.add("act", (lambda i=i: lambda e: e.activation(out=ss[:, 3 * i + 1:3 * i + 2], in_=ss[:, 3 * i:3 * i + 1], func=AF.Ln, scale=1.0 / D, bias=NORM_EPS))(), [ss_res[i]], [ss_res[i]])
                P.add("act", (lambda i=i: lambda e: e.activation(out=ss[:, 3 * i + 2:3 * i + 3], in_=ss[:, 3 * i + 1:3 * i + 2], func=AF.Exp, scale=-0.5))(), [ss_res[i]], [ss_res[i]])
                P.add("dve", (lambda xa=xa, ha=ha, i=i: lambda e: e.tensor_scalar(out=ha, in0=xa, scalar1=ss[:, 3 * i + 2:3 * i + 3], scalar2=None, op0=ALU.mult))(), xr + [ss_res[i]], hr)
                bank = i % 2
                pa, pr_ = psb(bank)
                pab = pa.bitcast(BF16)
                for k in range(8):
                    P.add("pe", (lambda pab=pab, ha=ha, k=k: lambda e: e.transpose(pab[:, k * 128:(k + 1) * 128], ha[:, k * 128:(k + 1) * 128], identb[:]))(), hr + [cst2], pr_)
                eng = "act" if i % 2 == 0 else "dve"
                dst = big[:, :, i * 128:(i + 1) * 128]
                srcv = pab.rearrange("p (k t) -> p k t", k=8)
                wr = [big_res[k][i // 4] for k in range(8)]
                if eng == "act":
                    P.add("act", (lambda dst=dst, srcv=srcv: lambda e: e.activation(out=dst, in_=srcv, func=AF.Copy))(), pr_, wr)
                else:
                    P.add("dve", (lambda dst=dst, srcv=srcv: lambda e: e.tensor_copy(out=dst, in_=srcv))(), pr_, wr)

            if stop_after == 'A':
                return
            A.reset()
            wst = [A.alloc(8 * 512) for _ in range(2)]
            wbf = [A.alloc(8 * 512, BF16) for _ in range(2)]
            stage = [A.alloc(S) for _ in range(2)]
            vst = [A.alloc(512, BF16) for _ in range(2)]
            allbig = [big_res[k][g] for k in range(8) for g in range(NG)]

            def load_w(r):
                width = 512 if r < 8 else 128
                wa, wr_ = wst[r % 2]
                wb, wbr = wbf[r % 2]
                wa3 = wa.rearrange("p (k c) -> p k c", k=8)
                wb3 = wb.rearrange("p (k c) -> p k c", k=8)
                if "nowload" not in DBG:
                    dma("sp", wa3[:, :, 0:width], win_d[l, :, r * 512:r * 512 + width].rearrange("(k p) c -> p k c", p=128), [], wr_, f"wst{r % 2}")
                else:
                    P.add("dve", lambda e: e.memset(wa3[:, :, 0:width], 0.5), [], wr_)
                for k in range(8):
                    P.add("pool", (lambda wa3=wa3, wb3=wb3, k=k, width=width: lambda e: e.tensor_scalar(out=wb3[:, k, 0:width], in0=wa3[:, k, 0:width], scalar1=prm128[:, l * 8 + k:l * 8 + k + 1], scalar2=None, op0=ALU.mult))(), wr_ + [cst], wbr)

            load_w(0)
            pbank = 0
            ev = 0
            for r in range(9):
                if r + 1 < 9:
                    load_w(r + 1)
                width = 512 if r < 8 else 128
                wb, wbr = wbf[r % 2]
                wb3 = wb.rearrange("p (k c) -> p k c", k=8)
                if r == 2:
                    for i in range(NT):
                        pa, pr_ = psb(4 + pbank % 4)
                        pbank += 1
                        for k in range(8):
                            P.add("pe", (lambda pa=pa, k=k, i=i, wb3=wb3: lambda e: e.matmul(pa, big[:, k, i * 128:(i + 1) * 128], wb3[:, k, :], start=(k == 0), stop=(k == 7)))(), [big_res[k][i // 4], ] + wbr, pr_)
                        va, vr = vst[i % 2]
                        eng = "act" if ev % 2 == 0 else "dve"
                        ev += 1
                        if eng == "act":
                            P.add("act", (lambda va=va, pa=pa: lambda e: e.activation(out=va, in_=pa, func=AF.Copy))(), pr_, vr)
                        else:
                            P.add("dve", (lambda va=va, pa=pa: lambda e: e.tensor_copy(out=va, in_=pa))(), pr_, vr)
                        if "nostore" not in DBG:
                            dma("sp", vtok_d[i * 128:(i + 1) * 128, :], va, vr, [vtok_res[i]], f"vst{i % 2}")
                    continue
                for jj in range(width // 128):
                    j = 4 * r + jj
                    sa, sr = stage[j % 2]
                    for tg in range(NG):
                        pa, pr_ = psb(4 + pbank % 4)
                        pbank += 1
                        for k in range(8):
                            P.add("pe", (lambda pa=pa, k=k, tg=tg, jj=jj, wb3=wb3: lambda e: e.matmul(pa, wb3[:, k, jj * 128:(jj + 1) * 128], big[:, k, tg * 512:(tg + 1) * 512], start=(k == 0), stop=(k == 7)))(), [big_res[k][tg]] + wbr, pr_)
                        eng = "act" if ev % 2 == 0 else "dve"
                        ev += 1
                        sres = sr[tg * 2:(tg + 1) * 2]
                        if eng == "act":
                            P.add("act", (lambda sa=sa, pa=pa, tg=tg: lambda e: e.activation(out=sa[:, tg * 512:(tg + 1) * 512], in_=pa, func=AF.Copy))(), pr_, sres)
                        else:
                            P.add("dve", (lambda sa=sa, pa=pa, tg=tg: lambda e: e.tensor_copy(out=sa[:, tg * 512:(tg + 1) * 512], in_=pa))(), pr_, sres)
                        if "nostore" not in DBG and "nopt" not in DBG:
                            dma("sp", pt_d[j * 128:(j + 1) * 128, tg * 512:(tg + 1) * 512], sa[:, tg * 512:(tg + 1) * 512], sres, [pt_res[j]], f"stg{j % 2}_{tg}")

            if l == tap_layer and "pt" in taps:
                dma("sp", tap_t["pt"].ap()[0:1024, :], pt_d[0:1024, :], pt_res, [out_res], "tap")
                dma("sp", tap_t["pt"].ap()[1536:INC, :], pt_d[1536:INC, :], pt_res, [out_res], "tap")

            if stop_after == 'B':
                return
            A.reset()
            qkf = A.alloc(S)
            qb = [A.alloc(S, BF16) for _ in range(2)]
            kb = [A.alloc(S, BF16) for _ in range(2)]
            vh = [A.alloc(NT * 128, BF16) for _ in range(2)]
            zf = [A.alloc(512) for _ in range(2)]
            ob = [A.alloc(4 * 512) for _ in range(2)]
            ptb = [A.alloc(2 * 512, BF16) for _ in range(3)]
            r01 = A.alloc(2 * 512)
            o01 = A.alloc(2 * 512)
            osb = A.alloc(512)
            sqb = A.alloc(512, BF16)
            rsd = A.alloc(512)
            gat = A.alloc(512)
            linit = 0.8 - 0.6 * math.exp(-0.3 * l)
            pt_cnt = [0]
            st_cnt = [0]
            pend = [None]
            zi = 0
            cbuf = [[A.alloc(516) for _ in range(4)] for _ in range(2)]
            cw = lambda k, j: prm128[:, 36 + l * 6 + k * 2 + j:36 + l * 6 + k * 2 + j + 1]

            def conv_load(j, g, it):
                    bs = cbuf[it % 2]
                    (cba, cbr), (cca, ccr), (cha, chr_), (cza, czr) = bs
                    t0 = g * 512
                    lo = 2 if g > 0 else 0
                    dma("sp", cca[:, 2 - lo:514], pt_d[(18 + j) * 128:(19 + j) * 128, t0 - lo:t0 + 512], [pt_res[18 + j]], ccr, f"cv{it % 2}")
                    dma("sp", cha[:, 2 - lo:514], pt_d[(20 + j) * 128:(21 + j) * 128, t0 - lo:t0 + 512], [pt_res[20 + j]], chr_, f"cv{it % 2}")
                    dma("sp", cba[:, 0:512], pt_d[(16 + j) * 128:(17 + j) * 128, t0:t0 + 512], [pt_res[16 + j]], cbr, f"cv{it % 2}")
                    dma("sp", cza[:, 0:512], pt_d[(22 + j) * 128:(23 + j) * 128, t0:t0 + 512], [pt_res[22 + j]], czr, f"cv{it % 2}")
                    if g == 0:
                        P.add("pool", (lambda cca=cca: lambda e: e.memset(cca[:, 0:2], 0.0))(), [], ccr)
                        P.add("pool", (lambda cha=cha: lambda e: e.memset(cha[:, 0:2], 0.0))(), [], chr_)

            def conv_block(j, g, it):
                    bs = cbuf[it % 2]
                    (cba, cbr), (cca, ccr), (cha, chr_), (cza, czr) = bs
                    P.add("pool", (lambda cca=cca, cha=cha: lambda e: e.tensor_tensor(out=cca[:, 0:514], in0=cca[:, 0:514], in1=cha[:, 0:514], op=ALU.mult))(), ccr + chr_, ccr)
                    P.add("dve", (lambda cca=cca, cha=cha, j=j: lambda e: e.tensor_scalar(out=cha[:, 0:512], in0=cca[:, 0:512], scalar1=cw(0, j), scalar2=None, op0=ALU.mult))(), ccr + [cst], chr_)
                    P.add("dve", (lambda cca=cca, cha=cha, j=j: lambda e: e.scalar_tensor_tensor(out=cha[:, 0:512], in0=cca[:, 1:513], scalar=cw(1, j), in1=cha[:, 0:512], op0=ALU.mult, op1=ALU.add))(), ccr + chr_ + [cst], chr_)
                    P.add("dve", (lambda cca=cca, cha=cha, j=j: lambda e: e.scalar_tensor_tensor(out=cha[:, 0:512], in0=cca[:, 2:514], scalar=cw(2, j), in1=cha[:, 0:512], op0=ALU.mult, op1=ALU.add))(), ccr + chr_ + [cst], chr_)
                    P.add("act", (lambda cza=cza, cca=cca: lambda e: e.activation(out=cca[:, 0:512], in_=cza[:, 0:512], func=AF.Exp, scale=-1.0))(), czr + ccr, ccr)
                    P.add("pool", (lambda cca=cca: lambda e: e.tensor_scalar(out=cca[:, 0:512], in0=cca[:, 0:512], scalar1=1.0, scalar2=None, op0=ALU.add))(), ccr, ccr)
                    P.add("dve", (lambda cca=cca: lambda e: e.reciprocal(out=cca[:, 0:512], in_=cca[:, 0:512]))(), ccr, ccr)
                    P.add("pool", (lambda cca=cca, cza=cza: lambda e: e.tensor_tensor(out=cca[:, 0:512], in0=cca[:, 0:512], in1=cza[:, 0:512], op=ALU.mult))(), ccr + czr, ccr)
                    P.add("pool", (lambda cha=cha, cba=cba: lambda e: e.tensor_tensor(out=cha[:, 0:512], in0=cha[:, 0:512], in1=cba[:, 0:512], op=ALU.mult))(), chr_ + cbr, chr_)
                    P.add("dve", (lambda cha=cha, cca=cca, j=j, g=g: lambda e: e.tensor_tensor(out=big[:, 4 + j, g * 512:(g + 1) * 512], in0=cha[:, 0:512], in1=cca[:, 0:512], op=ALU.mult))(), chr_ + ccr, [big_res[4 + j][g]])

            conv_todo = [(j, g, j * NG + g) for j in range(2) for g in range(NG)]
            conv_load(*conv_todo[0])

            def conv_step():
                cur = conv_todo.pop(0)
                conv_block(*cur)
                if conv_todo:
                    conv_load(*conv_todo[0])
            def load_head(h):
                hs = h % 2
                for c4 in range(4):
                    dma("sp", qkf[0][:, c4 * 1024:(c4 + 1) * 1024], pt_d[h * 128:(h + 1) * 128, c4 * 1024:(c4 + 1) * 1024], [pt_res[h]], qkf[1], "qkf")
                P.add("dve", (lambda hs=hs: lambda e: e.tensor_copy(out=qb[hs][0], in_=qkf[0]))(), qkf[1], qb[hs][1])
                for c4 in range(4):
                    dma("sp", qkf[0][:, c4 * 1024:(c4 + 1) * 1024], pt_d[512 + h * 128:512 + (h + 1) * 128, c4 * 1024:(c4 + 1) * 1024], [pt_res[4 + h]], qkf[1], "qkf")
                P.add("pool", (lambda hs=hs: lambda e: e.tensor_copy(out=kb[hs][0], in_=qkf[0]))(), qkf[1], kb[hs][1])
                vh3_ = vh[hs][0].rearrange("p (n d) -> p n d", n=NT)
                for c4 in range(4):
                    dma("sp", vh3_[:, c4 * 8:(c4 + 1) * 8, :], vtok_d[c4 * 1024:(c4 + 1) * 1024, h * 128:(h + 1) * 128].rearrange("(n p) d -> p n d", p=128), vtok_res, vh[hs][1], f"vh{hs}")

            load_head(0)
            for h in range(4):
                hs = h % 2
                vh3 = vh[hs][0].rearrange("p (n d) -> p n d", n=NT)
                qbh, kbh = qb[hs][0], kb[hs][0]
                for g in range(NG):
                    nk = 4 * g + 4
                    zsl, zsr = zf[zi % 2]
                    dma("sp", zsl, pt_d[1536 + h * 128:1536 + (h + 1) * 128, g * 512:(g + 1) * 512], [pt_res[12 + h]], zsr, f"zf{zi % 2}")
                    zi += 1
                    accs = [psb(4 + a_) for a_ in range(4)]

                    def emit_qk(kt, g=g, h=h, kbh=kbh, qbh=qbh, hs=hs):
                        sbank = (st_cnt[0] % 2) * 2
                        st_cnt[0] += 1
                        s2, s2r = psb(sbank, 2)
                        r = kt - 4 * g
                        diag = r >= -1
                        c0 = 128 * max(r, 0)
                        for m in range(2):
                            P.add("pe", (lambda s2=s2, m=m, kt=kt, diag=diag, c0=c0: lambda e: e.matmul(s2[:, m * 512 + c0:(m + 1) * 512], kbh[m * 64:(m + 1) * 64, kt * 128:(kt + 1) * 128], qbh[m * 64:(m + 1) * 64, g * 512 + c0:(g + 1) * 512], start=True, stop=not diag))(), qb[hs][1] + kb[hs][1], [s2r[m]])
                            if diag:
                                for s_ in range(max(r, 0), 4):
                                    dlt = s_ - r
                                    bi = 0 if dlt == 0 else (1 if dlt == 1 else 2)
                                    P.add("pe", (lambda s2=s2, m=m, s_=s_, bi=bi: lambda e: e.matmul(s2[:, m * 512 + s_ * 128:m * 512 + (s_ + 1) * 128], identb[:], bblk[:, h, bi, :], start=False, stop=(s_ == 3)))(), [cst2, bias_r], [s2r[m]])
                        return (s2, s2r, diag, c0)

                    def emit_rest(kt, qk, nk=nk, h=h, accs=accs, vh3=vh3, hs=hs):
                        s2, s2r, diag, c0 = qk
                        pa_, par = ptb[pt_cnt[0] % 3]
                        pt_cnt[0] += 1
                        s23 = s2.rearrange("p (m q) -> p m q", m=2)[:, :, c0:512]
                        pa3 = pa_.rearrange("p (m q) -> p m q", m=2)[:, :, c0:512]
                        if diag:
                            P.add("act", (lambda: lambda e: e.activation(out=pa3, in_=s23, func=AF.Exp, scale=SCALE))(), s2r, par)
                        else:
                            P.add("act", (lambda: lambda e: e.activation(out=pa3, in_=s23, func=AF.Exp, scale=SCALE, bias=cfar[:, 4 + h:5 + h]))(), s2r + [cst2], par)
                        for m in range(2):
                            P.add("pe", (lambda m=m, acc=accs[m][0]: lambda e: e.matmul(acc[:, c0:512], vh3[:, kt, :], pa_[:, m * 512 + c0:(m + 1) * 512], start=(kt == 0), stop=(kt == nk - 1)))(), par + vh[hs][1], accs[m][1])
                            P.add("pe", (lambda m=m, acc=accs[2 + m][0]: lambda e: e.matmul(acc[:, c0:512], onesb[:], pa_[:, m * 512 + c0:(m + 1) * 512], start=(kt == 0), stop=(kt == nk - 1)))(), par + [cst2], accs[2 + m][1])

                    nxt = emit_qk(0)
                    if conv_todo and (g % 2 == 1):
                        conv_step()
                    if g == 5 and h + 1 < 4:
                        load_head(h + 1)
                    for kt in range(nk):
                        cur = nxt
                        if kt + 1 < nk:
                            nxt = emit_qk(kt + 1)
                        emit_rest(kt, cur)
                        if kt == 2 and pend[0] is not None:
                            pend[0]()
                            pend[0] = None
                    oba, obr = ob[g % 2]
                    acc4, acc4r = psb(4, 4)
                    P.add("act", (lambda oba=oba, acc4=acc4: lambda e: e.activation(out=oba[:, 0:1024], in_=acc4[:, 0:1024], func=AF.Copy))(), acc4r[0:2], obr)
                    P.add("dve", (lambda oba=oba, acc4=acc4: lambda e: e.tensor_copy(out=oba[:, 1024:2048], in_=acc4[:, 1024:2048]))(), acc4r[2:4], obr)
                    P.add("dve", (lambda oba=oba: lambda e: e.reciprocal(out=r01[0], in_=oba[:, 1024:2048]))(), obr, r01[1])
                    P.add("dve", (lambda oba=oba: lambda e: e.tensor_tensor(out=o01[0], in0=oba[:, 0:1024], in1=r01[0], op=ALU.mult))(), obr + r01[1], o01[1])
                    P.add("dve", (lambda: lambda e: e.scalar_tensor_tensor(out=osb[0], in0=o01[0][:, 512:1024], scalar=lamv[:, l * 8 + 5:l * 8 + 6], in1=o01[0][:, 0:512], op0=ALU.mult, op1=ALU.add))(), o01[1] + [lam_r], osb[1])

                    def tail(h=h, g=g, zsl=zsl, zsr=zsr):
                        P.add("act", (lambda: lambda e: e.activation(out=sqb[0], in_=osb[0], func=AF.Square))(), osb[1], sqb[1])
                        P.add("act", (lambda: lambda e: e.activation(out=gat[0], in_=zsl, func=AF.Exp, scale=-1.0))(), zsr, gat[1])
                        sq_ps, sq_r = psb(0)
                        P.add("pe", (lambda: lambda e: e.matmul(sq_ps, onesb[:], sqb[0], start=True, stop=True))(), sqb[1] + [cst2], sq_r)
                        P.add("act", (lambda: lambda e: e.activation(out=rsd[0], in_=sq_ps, func=AF.Ln, scale=1.0 / 128, bias=SUBLN_EPS))(), sq_r, rsd[1])
                        P.add("act", (lambda: lambda e: e.activation(out=rsd[0], in_=rsd[0], func=AF.Exp, scale=-0.5))(), rsd[1], rsd[1])
                        P.add("pool", (lambda: lambda e: e.tensor_scalar(out=gat[0], in0=gat[0], scalar1=1.0, scalar2=None, op0=ALU.add))(), gat[1], gat[1])
                        P.add("dve", (lambda: lambda e: e.reciprocal(out=gat[0], in_=gat[0]))(), gat[1], gat[1])
                        P.add("pool", (lambda: lambda e: e.tensor_tensor(out=gat[0], in0=gat[0], in1=zsl, op=ALU.mult))(), gat[1] + zsr, gat[1])
                        P.add("dve", (lambda: lambda e: e.tensor_tensor(out=osb[0], in0=osb[0], in1=rsd[0], op=ALU.mult))(), osb[1] + rsd[1], osb[1])
                        P.add("dve", (lambda: lambda e: e.tensor_scalar(out=osb[0], in0=osb[0], scalar1=prm128[:, 32 + l:33 + l], scalar2=(1.0 - linit), op0=ALU.mult, op1=ALU.mult))(), osb[1] + [cst], osb[1])
                        P.add("dve", (lambda: lambda e: e.tensor_tensor(out=big[:, h, g * 512:(g + 1) * 512], in0=osb[0], in1=gat[0], op=ALU.mult))(), osb[1] + gat[1], [big_res[h][g]])

                    pend[0] = tail
            if pend[0] is not None:
                pend[0]()
                pend[0] = None

            if stop_after == 'C':
                return
            while conv_todo:
                conv_step()
            if stop_after == 'D':
                return
            if do_rwkv:
                rwkv_phase(l)
                P.pool_as = POOL_AS
            else:
                for k in (6, 7):
                    for g in range(NG):
                        P.add("pool", (lambda k=k, g=g: lambda e: e.memset(big[:, k, g * 512:(g + 1) * 512], 0.0))(), [], [big_res[k][g]])

            if l == tap_layer and "mixed" in taps:
                dma("sp", tap_t["mixed"].ap().rearrange("(k p) t -> p k t", p=128), big[:], allbig, [out_res], "tap")

            if stop_after == 'E':
                return
            A.reset()
            wo_st = [A.alloc(D) for _ in range(2)]
            wo = A.alloc(8 * D, BF16)
            wo3 = wo[0].rearrange("p (k d) -> p k d", k=8)
            xin = [A.alloc(D) for _ in range(2)]
            xo = [A.alloc(D) for _ in range(2)]
            for k in range(8):
                wa, wr_ = wo_st[k % 2]
                dma("sp", wa, wout_d[l, k * 128:(k + 1) * 128, :], [], wr_, f"wo{k % 2}")
                P.add("pool" if k % 2 else "dve", (lambda wa=wa, k=k: lambda e: e.tensor_copy(out=wo3[:, k, :], in_=wa))(), wr_, wo[1])
            last = (l == depth - 1)
            if last:
                fga, fgr = A.alloc(D)
                dma("sp", fga, fg_t.ap(), [], fgr, "fgl")
            for i in range(NT):
                xa, xr = xin[i % 2]
                ya, yr = xo[i % 2]
                dma("sp", xa, src_d[i * 128:(i + 1) * 128, :], [xs_res[i]] if l > 0 else [], xr, f"xin{i % 2}")
                p2, p2r = psb((i % 2) * 2, 2)
                for half in range(2):
                    for k in range(8):
                        P.add("pe", (lambda p2=p2, half=half, k=k, i=i: lambda e: e.matmul(p2[:, half * 512:(half + 1) * 512], big[:, k, i * 128:(i + 1) * 128], wo3[:, k, half * 512:(half + 1) * 512], start=(k == 0), stop=(k == 7)))(), [big_res[k][i // 4]] + wo[1], [p2r[half]])
                P.add("dve", (lambda ya=ya, p2=p2, xa=xa: lambda e: e.tensor_tensor(out=ya, in0=p2, in1=xa, op=ALU.add))(), p2r + xr, yr)
                if not last:
                    dma("sp", xs_d[i * 128:(i + 1) * 128, :], ya, yr, [xs_res[i]], f"xo{i % 2}")
                else:
                    if "xs" in taps:
                        dma("sp", tap_t["xs"].ap()[i * 128:(i + 1) * 128, :], ya, yr, [out_res], f"xo{i % 2}")
                    fr = Res(f"fin{i}")
                    P.add("act", (lambda ya=ya, xa=xa, i=i: lambda e: e.activation(out=xa, in_=ya, func=AF.Square, accum_out=ss[:, 3 * i:3 * i + 1]))(), yr, xr + [ss_res[i]])
                    P.add("act", (lambda i=i: lambda e: e.activation(out=ss[:, 3 * i + 1:3 * i + 2], in_=ss[:, 3 * i:3 * i + 1], func=AF.Ln, scale=1.0 / D, bias=NORM_EPS))(), [ss_res[i]], [ss_res[i]])
                    P.add("act", (lambda i=i: lambda e: e.activation(out=ss[:, 3 * i + 2:3 * i + 3], in_=ss[:, 3 * i + 1:3 * i + 2], func=AF.Exp, scale=-0.5))(), [ss_res[i]], [ss_res[i]])
                    P.add("dve", (lambda ya=ya, xa=xa, i=i: lambda e: e.scalar_tensor_tensor(out=xa, in0=ya, scalar=ss[:, 3 * i + 2:3 * i + 3], in1=fga, op0=ALU.mult, op1=ALU.mult))(), yr + [ss_res[i]] + fgr, xr)
                    dma("sp", out_d[i * 128:(i + 1) * 128, :], xa, xr, [out_res], f"xo{i % 2}")

        for l_ in range(depth):
            emit_layer(l_)
        P.add("sp", lambda e: e.nop(), [out_res], [])
        nsem = P.emit(nc, st)
    return nc, len(P.ops), nsem


_CACHE = {}


def kernel(**inputs):
    x = np.asarray(inputs["x"], np.float32)
    prm128, fg, lamrep, prm64 = host_params(inputs)
    consts = make_consts()
    if "nc" not in _CACHE:
        _CACHE["nc"] = build(L)[0]
    nc = _CACHE["nc"]
    shared = {
        "w_in": np.ascontiguousarray(np.asarray(inputs["w_in"], np.float32)),
        "w_out": np.ascontiguousarray(np.asarray(inputs["w_out"], np.float32)),
        "rel_bias": np.ascontiguousarray(np.asarray(inputs["rel_bias"], np.float32)),
        "w_up": np.ascontiguousarray(np.asarray(inputs["w_up"], np.float32)),
        "a_up": np.ascontiguousarray(np.asarray(inputs["a_up"], np.float32)),
        "prm128": prm128, "fg": fg, "lamrep": lamrep, "prm64": prm64,
    }
    shared.update(consts)
    in_maps = []
    for b in range(8):
        m = dict(shared)
        m["x"] = np.ascontiguousarray(x[b])
        in_maps.append(m)
    res = run_bass_kernel_spmd(nc, in_maps, core_ids=list(range(8)))
    return np.stack([np.asarray(r["out"], np.float32) for r in res.results], axis=0)
```
